# Optimizing a Trainium2 kernel written in Bass

```python
import math
import jax, jax.numpy as jnp
from jax import lax
import numpy as np

D_MODEL = 1024
BATCH = 4
SEQ = 8192
DEPTH = 1
DEC_BATCH = 8
DEC_SEQ = 16
PAST_LEN = 1024

CHUNK = 64
HEAD_DIM = 64
EPS = 1e-6
NEG_INF = -1e30

A_HEADS = 8
A_KV_HEADS = 2
A_GROUP = A_HEADS // A_KV_HEADS
A_WINDOW = 128
A_PREV = (A_WINDOW // CHUNK) * CHUNK
T5_BUCKETS = 32
T5_MAX_DIST = 128

B_HEADS = 8
B_PREV_CHUNKS = 8
B_PREV = B_PREV_CHUNKS * CHUNK
B_REL_CLIP = 128

A_QW = A_HEADS * HEAD_DIM
A_KVW = A_KV_HEADS * HEAD_DIM
B_W = B_HEADS * HEAD_DIM
IN_WIDTH = A_QW + 2 * A_KVW + 3 * B_W + 2 * D_MODEL

D_FF = 3072
CONV_WIDTH = 3

kernel_name = "hybrid_chunk_stream_swa_sink_chunkband_convffn"


def rmsnorm(x, g):
    xf = x.astype(jnp.float32)
    y = xf * lax.rsqrt(jnp.mean(xf * xf, axis=-1, keepdims=True) + EPS)
    return (y * g.astype(jnp.float32)).astype(x.dtype)


def rel_dist(n_q, n_k, n_prev):
    return jnp.arange(n_q)[:, None] - jnp.arange(n_k)[None, :] + n_prev


def t5_bucket(n):
    half = T5_BUCKETS // 2
    max_exact = half // 2
    ret = jnp.where(n < 0, half, 0)
    a = jnp.abs(n)
    af = jnp.maximum(a, 1).astype(jnp.float32)
    large = max_exact + (jnp.log(af / max_exact) / math.log(T5_MAX_DIST / max_exact)
                         * (half - max_exact)).astype(jnp.int32)
    large = jnp.minimum(large, half - 1)
    return ret + jnp.where(a < max_exact, a, large)


def bias_a(t5_table, n_q, n_k, n_prev):
    b = t5_table[:, t5_bucket(rel_dist(n_q, n_k, n_prev))]
    return b.astype(jnp.float32).reshape(A_KV_HEADS, A_GROUP, n_q, n_k)


def bias_b(rel_table, n_q, n_k, n_prev):
    idx = jnp.clip(rel_dist(n_q, n_k, n_prev), -B_REL_CLIP, B_REL_CLIP) + B_REL_CLIP
    return rel_table[:, idx].astype(jnp.float32)[:, None]


def band_attention(q, k, v, bias, valid, sink):
    s = jnp.einsum("bqhgd,bkhd->bhgqk", q, k).astype(jnp.float32) * (HEAD_DIM ** -0.5) + bias
    s = jnp.where(valid, s, NEG_INF)
    m = jnp.max(s, axis=-1, keepdims=True)
    if sink is not None:
        sk = sink.astype(jnp.float32)[None, :, :, None, None]
        m = jnp.maximum(m, sk)
        p = jnp.exp(s - m)
        denom = jnp.sum(p, axis=-1, keepdims=True) + jnp.exp(sk - m)
    else:
        p = jnp.exp(s - m)
        denom = jnp.sum(p, axis=-1, keepdims=True)
    w = (p / denom).astype(v.dtype)
    return jnp.einsum("bhgqk,bkhd->bqhgd", w, v)


def prompt_band(q, k, v, n_prev, bias, sink):
    b, s = q.shape[0], q.shape[1]
    pad = jnp.zeros((b, n_prev) + k.shape[2:], k.dtype)
    kp = jnp.concatenate([pad, k], axis=1)
    vp = jnp.concatenate([pad, v], axis=1)
    band = n_prev + CHUNK

    def one_chunk(c):
        start = c * CHUNK
        qc = lax.dynamic_slice_in_dim(q, start, CHUNK, axis=1)
        kc = lax.dynamic_slice_in_dim(kp, start, band, axis=1)
        vc = lax.dynamic_slice_in_dim(vp, start, band, axis=1)
        valid = (start - n_prev + jnp.arange(band) >= 0)[None, :]
        return band_attention(qc, kc, vc, bias, valid, sink)

    o = lax.map(one_chunk, jnp.arange(s // CHUNK))
    return jnp.moveaxis(o, 0, 1).reshape(b, s, -1)


def sample_band(q, k_new, v_new, k_cache, v_cache, bias, sink):
    b, s = q.shape[0], q.shape[1]
    k = jnp.concatenate([k_cache, k_new], axis=1)
    v = jnp.concatenate([v_cache, v_new], axis=1)
    valid = jnp.ones((1, k.shape[1]), dtype=bool)
    return band_attention(q, k, v, bias, valid, sink).reshape(b, s, -1)


def project(h, w_in):
    b, s = h.shape[0], h.shape[1]
    p = h @ w_in
    cuts = [A_QW, A_QW + A_KVW, A_QW + 2 * A_KVW, A_QW + 2 * A_KVW + B_W,
            A_QW + 2 * A_KVW + 2 * B_W, A_QW + 2 * A_KVW + 3 * B_W,
            A_QW + 2 * A_KVW + 3 * B_W + D_MODEL]
    qa, ka, va, qb, kb, vb, ga, gb = jnp.split(p, cuts, axis=-1)
    qa = qa.reshape(b, s, A_KV_HEADS, A_GROUP, HEAD_DIM)
    ka = ka.reshape(b, s, A_KV_HEADS, HEAD_DIM)
    va = va.reshape(b, s, A_KV_HEADS, HEAD_DIM)
    qb = qb.reshape(b, s, B_HEADS, 1, HEAD_DIM)
    kb = kb.reshape(b, s, B_HEADS, HEAD_DIM)
    vb = vb.reshape(b, s, B_HEADS, HEAD_DIM)
    return qa, ka, va, qb, kb, vb, ga, gb


def conv_ffn(h, conv_buf, w_upg, conv_w, conv_b, w_down):
    u, g = jnp.split(h @ w_upg, 2, axis=-1)
    s = u.shape[1]
    upad = jnp.concatenate([conv_buf, u], axis=1)
    c = conv_b
    for tap in range(CONV_WIDTH):
        c = c + conv_w[tap] * upad[:, tap:tap + s]
    y = (jax.nn.gelu(c, approximate=True) * g) @ w_down
    return y, upad[:, -(CONV_WIDTH - 1):]


def layer(x, a_cache, b_cache, conv_buf, w_in, w_oa, w_ob, w_out, sink, t5_table, rel_table,
          g_pre_mix, g_post_mix, g_pre_ffn, g_post_ffn, w_upg, conv_w, conv_b, w_down):
    s = x.shape[1]
    h = rmsnorm(x, g_pre_mix)
    qa, ka, va, qb, kb, vb, ga, gb = project(h, w_in)
    sink_g = sink.reshape(A_KV_HEADS, A_GROUP)
    if a_cache is None:
        oa = prompt_band(qa, ka, va, A_PREV, bias_a(t5_table, CHUNK, A_PREV + CHUNK, A_PREV), sink_g)
        ob = prompt_band(qb, kb, vb, B_PREV, bias_b(rel_table, CHUNK, B_PREV + CHUNK, B_PREV), None)
        new_rows = (ka[:, -A_PREV:], va[:, -A_PREV:], kb[:, -B_PREV:], vb[:, -B_PREV:])
    else:
        la = a_cache[0].shape[1]
        lb = b_cache[0].shape[1]
        oa = sample_band(qa, ka, va, a_cache[0], a_cache[1], bias_a(t5_table, s, la + s, la), sink_g)
        ob = sample_band(qb, kb, vb, b_cache[0], b_cache[1], bias_b(rel_table, s, lb + s, lb), None)
        new_rows = (ka, va, kb, vb)
    merged = jax.nn.sigmoid(ga) * (oa @ w_oa) + jax.nn.sigmoid(gb) * (ob @ w_ob)
    x = x + rmsnorm(merged @ w_out, g_post_mix)
    f, conv_new = conv_ffn(rmsnorm(x, g_pre_ffn), conv_buf, w_upg, conv_w, conv_b, w_down)
    x = x + rmsnorm(f, g_post_ffn)
    return x, new_rows + (conv_new,)


def setup_inputs(seed: int = 0) -> dict:
    key = jax.random.key(seed)
    ks = jax.random.split(key, 24)
    f32 = jnp.float32

    def nrm(k, shape, scale):
        return jax.random.normal(k, shape, f32) * scale

    a_len = min(A_PREV, PAST_LEN)
    b_len = min(B_PREV, PAST_LEN)
    return {
        "x_prompt": nrm(ks[0], (BATCH, SEQ, D_MODEL), 1.0),
        "x_sample": nrm(ks[1], (DEC_BATCH, DEC_SEQ, D_MODEL), 1.0),
        "cache_a_k": nrm(ks[2], (DEPTH, DEC_BATCH, a_len, A_KV_HEADS, HEAD_DIM), 1.0),
        "cache_a_v": nrm(ks[3], (DEPTH, DEC_BATCH, a_len, A_KV_HEADS, HEAD_DIM), 1.0),
        "cache_b_k": nrm(ks[4], (DEPTH, DEC_BATCH, b_len, B_HEADS, HEAD_DIM), 1.0),
        "cache_b_v": nrm(ks[5], (DEPTH, DEC_BATCH, b_len, B_HEADS, HEAD_DIM), 1.0),
        "state_conv": nrm(ks[6], (DEPTH, DEC_BATCH, CONV_WIDTH - 1, D_FF), 1.0),
        "w_in": nrm(ks[7], (DEPTH, D_MODEL, IN_WIDTH), D_MODEL ** -0.5),
        "w_oa": nrm(ks[8], (DEPTH, A_QW, D_MODEL), A_QW ** -0.5),
        "w_ob": nrm(ks[9], (DEPTH, B_W, D_MODEL), B_W ** -0.5),
        "w_out": nrm(ks[10], (DEPTH, D_MODEL, D_MODEL), D_MODEL ** -0.5),
        "sink_a": nrm(ks[11], (DEPTH, A_HEADS), 1.0),
        "t5_table": nrm(ks[12], (A_HEADS, T5_BUCKETS), 0.5),
        "rel_table_b": nrm(ks[13], (DEPTH, B_HEADS, 2 * B_REL_CLIP + 1), 0.5),
        "g_pre_mix": 1.0 + nrm(ks[14], (DEPTH, D_MODEL), 0.05),
        "g_post_mix": 1.0 + nrm(ks[15], (DEPTH, D_MODEL), 0.05),
        "g_pre_ffn": 1.0 + nrm(ks[16], (DEPTH, D_MODEL), 0.05),
        "g_post_ffn": 1.0 + nrm(ks[17], (DEPTH, D_MODEL), 0.05),
        "w_upg": nrm(ks[18], (DEPTH, D_MODEL, 2 * D_FF), D_MODEL ** -0.5),
        "conv_w": nrm(ks[19], (DEPTH, CONV_WIDTH, D_FF), CONV_WIDTH ** -0.5),
        "conv_b": nrm(ks[20], (DEPTH, D_FF), 0.01),
        "w_down": nrm(ks[21], (DEPTH, D_FF, D_MODEL), D_FF ** -0.5),
    }


def reference(x_prompt, x_sample, cache_a_k, cache_a_v, cache_b_k, cache_b_v, state_conv,
              w_in, w_oa, w_ob, w_out, sink_a, t5_table, rel_table_b,
              g_pre_mix, g_post_mix, g_pre_ffn, g_post_ffn, w_upg, conv_w, conv_b, w_down):
    xp, xs = x_prompt, x_sample
    prompt_states = [[] for _ in range(5)]
    sample_states = [[] for _ in range(5)]
    for l in range(DEPTH):
        weights = (w_in[l], w_oa[l], w_ob[l], w_out[l], sink_a[l], t5_table, rel_table_b[l],
                   g_pre_mix[l], g_post_mix[l], g_pre_ffn[l], g_post_ffn[l],
                   w_upg[l], conv_w[l], conv_b[l], w_down[l])
        zero_buf = jnp.zeros((xp.shape[0], CONV_WIDTH - 1, D_FF), xp.dtype)
        xp, st_p = layer(xp, None, None, zero_buf, *weights)
        xs, st_s = layer(xs, (cache_a_k[l], cache_a_v[l]), (cache_b_k[l], cache_b_v[l]),
                         state_conv[l], *weights)
        for i in range(5):
            prompt_states[i].append(st_p[i])
            sample_states[i].append(st_s[i])
    ps = [jnp.stack(a, axis=0) for a in prompt_states]
    ss = [jnp.stack(a, axis=0) for a in sample_states]
    return (xp, xs, ps[0], ps[1], ps[2], ps[3], ps[4], ss[0], ss[1], ss[2], ss[3], ss[4])
```

```python
import math
import contextlib
import numpy as np
import concourse.bass as bass
import concourse.mybir as mybir
from concourse.bass_utils import run_bass_kernel_spmd

F32 = mybir.dt.float32
BF16 = mybir.dt.bfloat16
AF = mybir.ActivationFunctionType
ALU = mybir.AluOpType

D = 1024
DFF = 3072
INW = 4352
SEQ = 8192
HALF = 4096
NT_MAIN = 32
T_HALO_KV = 4
T_HALO = 4
T0 = 5
NTILES = T0 + NT_MAIN
RT = 8
EPS = 1e-6
NEG = -30000.0
SAMP_SLOT = 4
C_QA, C_KA, C_VA, C_QB, C_KB, C_VB, C_GA, C_GB = 0, 512, 640, 768, 1280, 1792, 2304, 3328


class _Op:
    __slots__ = ("eng", "fn", "reads", "writes", "dsem", "deps", "mark", "rank", "waits", "clock")

    def __init__(self, eng, fn, reads, writes, dsem):
        self.eng = eng
        self.fn = fn
        self.reads = reads
        self.writes = writes
        self.dsem = dsem
        self.deps = None
        self.mark = False
        self.rank = 0
        self.waits = None
        self.clock = None


class Prog:
    ENGS = ("pe", "act", "dve", "pool", "sp")

    def __init__(self):
        self.ops = []
        self.group_final = set()

    def add(self, eng, fn, reads=(), writes=(), dsem=None):
        self.ops.append(_Op(eng, fn, tuple(reads), list(writes), dsem))

    def finalize(self):
        last_w = {}
        readers = {}
        ops = self.ops
        for i, op in enumerate(ops):
            deps = {}
            for k in op.reads:
                w = last_w.get(k)
                if w is not None:
                    deps[w] = "raw"
            for k in op.writes:
                w = last_w.get(k)
                if w is not None and w not in deps:
                    deps[w] = "waw"
                last_by_eng = {}
                for r in readers.get(k, ()):
                    if r == i:
                        continue
                    if ops[r].dsem is not None:
                        if r not in deps:
                            deps[r] = "war"
                    elif last_by_eng.get(ops[r].eng, -1) < r:
                        last_by_eng[ops[r].eng] = r
                for r in last_by_eng.values():
                    if r not in deps:
                        deps[r] = "war"
            for k in op.reads:
                readers.setdefault(k, []).append(i)
            for k in op.writes:
                last_w[k] = i
                readers[k] = []
            need = []
            for d, kind in deps.items():
                dop = ops[d]
                if dop.dsem is not None and dop.dsem == op.dsem and op.dsem in self.group_final:
                    continue
                if dop.dsem is not None or op.dsem is not None:
                    need.append(d)
                elif dop.eng != op.eng:
                    need.append(d)
                elif kind == "raw" and op.eng != "pe":
                    need.append(d)
            op.deps = need
            for d in need:
                ops[d].mark = True
        cnt = {e: 0 for e in self.ENGS}
        dcnt = {}
        for op in ops:
            if op.dsem is not None:
                dcnt[op.dsem] = dcnt.get(op.dsem, 0) + 16
                op.rank = dcnt[op.dsem]
            elif op.mark:
                cnt[op.eng] += 1
                op.rank = cnt[op.eng]
        for op in ops:
            if op.dsem is not None and op.dsem in self.group_final:
                op.rank = dcnt[op.dsem]
        clock = {e: {} for e in self.ENGS}
        nw = 0
        for op in ops:
            ck = clock[op.eng]
            waits = {}
            for d in sorted(op.deps, key=lambda d: -ops[d].rank):
                dop = ops[d]
                sk = ("d", dop.dsem) if dop.dsem is not None else ("e", dop.eng)
                if ck.get(sk, 0) >= dop.rank:
                    continue
                if waits.get(sk, 0) < dop.rank:
                    waits[sk] = dop.rank
                ck[sk] = dop.rank
                if dop.clock is not None:
                    for k2, v2 in dop.clock.items():
                        if ck.get(k2, 0) < v2:
                            ck[k2] = v2
            op.waits = waits
            nw += len(waits)
            if op.dsem is not None:
                op.clock = dict(ck)
            elif op.mark:
                c2 = dict(ck)
                c2[("e", op.eng)] = op.rank
                op.clock = c2
        self.n_waits = nw
        self.counts = cnt
        self.dcounts = dcnt
        import os
        pr = os.environ.get("KPRINT")
        if pr:
            a, b = [int(x) for x in pr.split(":")]
            for i in range(a, min(b, len(ops))):
                op = ops[i]
                print("W", i, op.eng, "rank", op.rank if (op.mark or op.dsem) else "-", "deps", [(d, ops[d].eng, ops[d].rank) for d in op.deps],
                      "waits", op.waits, "r", op.reads, "w", op.writes)

    def emit(self, nc):
        per_eng = {e: [] for e in self.ENGS}
        for op in self.ops:
            per_eng[op.eng].append(op)
        EP = 16384
        with contextlib.ExitStack() as st:
            esem = {e: [st.enter_context(nc.semaphore("es_%s_%d" % (e, j)))
                        for j in range((self.counts[e] + EP - 1) // EP + 1)] for e in self.ENGS}
            dsem = {k: st.enter_context(nc.semaphore("ds_%d" % i)) for i, k in enumerate(self.dcounts)}
            block = st.enter_context(nc.Block())

            def sem_of(sk, v):
                if sk[0] == "d":
                    return dsem[sk[1]], v
                return esem[sk[1]][(v - 1) // EP], (v - 1) % EP + 1

            def run(engname, eng):
                for op in per_eng[engname]:
                    for sk, v in op.waits.items():
                        eng.wait_ge(*sem_of(sk, v))
                    ins = op.fn(eng)
                    if op.dsem is not None:
                        ins.then_inc(dsem[op.dsem], 16)
                    elif op.mark:
                        ins.then_inc(esem[op.eng][(op.rank - 1) // EP], 1)
                if engname == "sp":
                    for k, v in self.dcounts.items():
                        eng.wait_ge(dsem[k], v)

            @block.tensor
            def _(e):
                run("pe", e)

            @block.scalar
            def _(e):
                run("act", e)

            @block.vector
            def _(e):
                run("dve", e)

            @block.gpsimd
            def _(e):
                run("pool", e)

            @block.sync
            def _(e):
                run("sp", e)


def build_nc(stage=9, ntiles_p1=None):
    nc = bass.Bass("TRN2", target_bir_lowering=False)

    def din(name, shape):
        return nc.dram_tensor(name, list(shape), F32, kind="ExternalInput").ap()

    def dout(name, shape):
        return nc.dram_tensor(name, list(shape), F32, kind="ExternalOutput").ap()

    xc_d = din("xc", [NTILES * 128, D])
    xs_d = din("xs", [16, D])
    cak_d = din("cak", [128, 128])
    cav_d = din("cav", [128, 128])
    cbk_d = din("cbk", [512, 512])
    cbv_d = din("cbv", [512, 512])
    sconv_d = din("sconv", [2, DFF])
    win_d = din("w_in", [D, INW])
    woa_d = din("w_oa", [512, D])
    wob_d = din("w_ob", [512, D])
    wout_d = din("w_out", [D, D])
    sink_d = din("sink", [1, 8])
    gpm_d = din("g_pre_mix", [1, D])
    gqm_d = din("g_post_mix", [1, D])
    gpf_d = din("g_pre_ffn", [1, D])
    gqf_d = din("g_post_ffn", [1, D])
    wup_d = din("w_upg", [D, 2 * DFF])
    cw_d = din("conv_w", [3, DFF])
    cb_d = din("conv_b", [1, DFF])
    wdn_d = din("w_down", [DFF, D])
    ba_d = din("biasA", [128, 2 * 8 * 128])
    bb_d = din("biasB", [128, 5 * 8 * 128])
    bas_d = din("biasAs", [128, 2 * 8 * 16])
    bbs_d = din("biasBs", [128, 5 * 8 * 16])
    ident_d = din("ident", [128, 128])
    hv_d = din("hv", [128, 1])

    yp_d = dout("yp", [HALF, D])
    ys_d = dout("ys", [16, D])
    oak_d = dout("oak", [128, 128])
    oav_d = dout("oav", [128, 128])
    obk_d = dout("obk", [512, 512])
    obv_d = dout("obv", [512, 512])
    ocv_d = dout("ocv", [2, DFF])
    sak_d = dout("sak", [16, 128])
    sav_d = dout("sav", [16, 128])
    sbk_d = dout("sbk", [16, 512])
    sbv_d = dout("sbv", [16, 512])
    scv_d = dout("scv", [2, DFF])

    import os as _os
    if _os.environ.get("KDBG"):
        x1s_d = nc.dram_tensor("x1s", [(NTILES + 1) * 128, D], F32, kind="ExternalOutput").ap()
    else:
        x1s_d = nc.dram_tensor("x1s", [(NTILES + 1) * 128, D], F32).ap()

    P = Prog()
    P.group_final.add("setup")
    KDBG = bool(_os.environ.get("KDBG"))

    def dbg_dump(name, ap, shape, dt, reads):
        if not KDBG:
            return
        dd = nc.dram_tensor("dbg_" + name, list(shape), dt, kind="ExternalOutput").ap()
        P.add("sp", lambda e: e.dma_start(out=dd, in_=ap), reads=reads, dsem="dbg")

    with contextlib.ExitStack() as st:
        def sb(name, shape, dt):
            return st.enter_context(nc.sbuf_tensor("s_" + name, list(shape), dt))

        st.enter_context(nc.allow_non_contiguous_dma(reason="small constant / transposing loads"))

        WAR = sb("warena", [128, 73728], BF16)
        PAR = sb("parena", [128, 13440], BF16)
        FAR = sb("farena", [128, 1040], F32)
        xin = [sb("xin%d" % i, [128, D], F32) for i in range(4)]
        xn = [sb("xn%d" % i, [128, D], BF16) for i in range(2)]
        tmp = sb("tmp", [128, 1280], F32)
        gbc = sb("gbc", [128, D], F32)
        identf = sb("identf", [128, 128], F32)
        ident = sb("ident", [128, 128], BF16)
        gcol = sb("gcol", [128, 16], F32)
        cw = sb("cw", [128, 3 * 24], F32)
        cbias = sb("cbias", [128, 24], F32)
        hv = sb("hv", [128, 1], F32)
        hv10 = sb("hv10", [128, 10], BF16)
        epst = sb("epst", [128, 1], F32)
        sinkt = sb("sinkt", [128, 8], F32)
        exps = sb("exps", [128, 8], F32)
        ss = sb("ss", [128, 16], F32)
        sd = sb("sd", [128, 16], F32)
        rstd = sb("rstd", [128, 16], F32)
        junk = sb("junk", [128, 512], BF16)
        den = sb("den", [128, 8], F32)
        rden = sb("rden", [128, 8], F32)
        hist = sb("hist", [128, 48], F32)
        hists = sb("hists", [128, 48], F32)
        bars = sb("bars", [128, 8], F32)

        Win = WAR[:, 0:34816].rearrange("p (k c) -> p k c", k=8)
        Woa = WAR[:, 34816:38912].rearrange("p (k c) -> p k c", k=4)
        Wob = WAR[:, 38912:43008].rearrange("p (k c) -> p k c", k=4)
        Wout = WAR[:, 43008:51200].rearrange("p (k c) -> p k c", k=8)
        o = 51200
        BA = WAR[:, o:o + 4096].bitcast(F32).rearrange("p (t h q) -> p t h q", t=2, h=8)
        o += 4096
        BB = WAR[:, o:o + 10240].bitcast(F32).rearrange("p (t h q) -> p t h q", t=5, h=8)
        o += 10240
        BAs = WAR[:, o:o + 512].bitcast(F32).rearrange("p (t h q) -> p t h q", t=2, h=8)
        o += 512
        BBs = WAR[:, o:o + 1280].bitcast(F32).rearrange("p (t h q) -> p t h q", t=5, h=8)
        o += 1280
        KTa = WAR[:, o:o + RT * 128]
        o += RT * 128
        KTb = WAR[:, o:o + 4 * RT * 128].rearrange("p (c k) -> p c k", c=4)
        o += 4 * RT * 128
        assert o <= 73728
        Wup = WAR[:, 0:49152].rearrange("p (k c) -> p k c", k=8)
        Wdn = WAR[:, 49152:73728].rearrange("p (k c) -> p k c", k=24)

        hT = [PAR[:, i * 1024:(i + 1) * 1024].rearrange("p (k t) -> p k t", k=8) for i in range(2)]
        QT = PAR[:, 2048:3072].rearrange("p (k t) -> p k t", k=8)
        OT = PAR[:, 3072:4096].rearrange("p (k t) -> p k t", k=8)
        mT = PAR[:, 4096:5120].rearrange("p (k t) -> p k t", k=8)
        Vr = PAR[:, 5120:5120 + RT * 650].rearrange("p (s h d) -> p s h d", s=RT, h=10)
        o = 5120 + RT * 650
        pT = [PAR[:, o + i * 512:o + (i + 1) * 512] for i in range(4)]
        o += 2048
        Otok = PAR[:, o:o + 1024]
        o += 1024
        assert o <= 13440
        gtmp = [FAR[:, i * 256:(i + 1) * 256] for i in range(2)]
        h2T = [PAR[:, i * 2048:(i + 1) * 2048].rearrange("p (k t) -> p k t", k=8) for i in range(2)]
        actT = PAR[:, 4096:4096 + 24 * 256].rearrange("p (k t) -> p k t", k=24)
        ubuf = [FAR[:, i * 258:(i + 1) * 258] for i in range(2)]
        cbuf = [FAR[:, 516 + i * 256:516 + (i + 1) * 256] for i in range(2)]

        banks = [st.enter_context(nc.psum_tensor("bank%d" % i, [128, 512], F32)) for i in range(8)]

        def bk(b):
            return banks[b][:, :]

        def bkb(b):
            return banks[b][:, :].bitcast(BF16)

        def BK(b):
            return ("bank", b)

        cnt = {"xin": 0, "cast": 0, "ev": 0}

        def evac_eng():
            cnt["ev"] += 1
            return "act" if cnt["ev"] % 2 else "dve"

        def copy_op(eng, out, in_, scale=None):
            if eng == "act":
                if scale is None:
                    return lambda e: e.activation(out=out, in_=in_, func=AF.Copy)
                return lambda e: e.activation(out=out, in_=in_, func=AF.Copy, scale=scale)
            if scale is None:
                return lambda e: e.tensor_copy(out=out, in_=in_)
            return lambda e: e.tensor_scalar(out=out, in0=in_, scalar1=scale, scalar2=None, op0=ALU.mult)

        def wkeys(name, kc, c0, c1):
            return [(name, kc, c) for c in range(c0 // 1024, (c1 - 1) // 1024 + 1)]

        import os
        SKIP = os.environ.get("KSKIP", "")

        def setup_dma(out, in_, key):
            k0 = key[0] if isinstance(key, tuple) else key
            if k0 in SKIP.split(","):
                return
            P.add("sp", lambda e: e.dma_start(out=out, in_=in_), writes=[key], dsem="setup")

        setup_dma(identf[:, :], ident_d, "identf")
        setup_dma(hv[:, :], hv_d, "hv")
        setup_dma(sinkt[:, :], sink_d.partition_broadcast(128), "sinkt")
        setup_dma(gcol[:, 0:8], gpm_d.rearrange("o (k p) -> p (o k)", p=128), "gcol")
        setup_dma(gcol[:, 8:16], gpf_d.rearrange("o (k p) -> p (o k)", p=128), "gcol")
        for tap in range(3):
            setup_dma(cw[:, tap * 24:(tap + 1) * 24], cw_d[tap:tap + 1, :].rearrange("o (c p) -> p (o c)", p=128), "cw")
        setup_dma(cbias[:, :], cb_d.rearrange("o (c p) -> p (o c)", p=128), "cbias")
        setup_dma(BA.rearrange("p t h q -> p (t h q)"), ba_d, "BA")
        setup_dma(BB.rearrange("p t h q -> p (t h q)"), bb_d, "BB")
        setup_dma(BAs.rearrange("p t h q -> p (t h q)"), bas_d, "BAs")
        setup_dma(BBs.rearrange("p t h q -> p (t h q)"), bbs_d, "BBs")
        setup_dma(gbc[:, :], gqm_d.partition_broadcast(128), "gbc")
        for c in range(24):
            setup_dma(hists[:, 2 * c:2 * c + 2], sconv_d[:, c * 128:(c + 1) * 128].rearrange("t p -> p t"), ("hists", c))

        P.add("dve", lambda e: e.tensor_copy(out=ident[:, :], in_=identf[:, :]), reads=["identf"], writes=["ident"])
        P.add("dve", lambda e: e.memset(epst[:, :], EPS), writes=["epst"])
        P.add("dve", lambda e: e.memset(hist[:, :], 0.0), writes=[("hist", j) for j in range(24)])
        P.add("dve", lambda e: e.memset(den[:, 0:8], 1.0), writes=["den0"])
        P.add("dve", lambda e: e.tensor_scalar(out=rden[:, 0:8], in0=den[:, 0:8], scalar1=hv[:, 0:1], scalar2=None,
                                                op0=ALU.mult), reads=["den0", "hv"], writes=["rden0"])
        P.add("dve", lambda e: e.tensor_copy(out=hv10[:, 0:8], in_=rden[:, 0:8]), reads=["rden0"], writes=["hv10a"])
        P.add("dve", lambda e: e.tensor_copy(out=hv10[:, 8:10], in_=rden[:, 0:2]), reads=["rden0"], writes=["hv10"])
        P.add("act", lambda e: e.activation(out=exps[:, :], in_=sinkt[:, :], func=AF.Exp),
              reads=["sinkt"], writes=["exps"])

        def prep_weight(name, dram, nk, ncols, dest, scale_col=None, scale_const=None, extra_reads=()):
            for kc in range(nk):
                for c0 in range(0, ncols, 1024):
                    w = min(1024, ncols - c0)
                    sl = cnt["xin"] % 4
                    cnt["xin"] += 1
                    stg = xin[sl]
                    P.add("sp", lambda e, stg=stg, kc=kc, c0=c0, w=w: e.dma_start(
                        out=stg[:, 0:w], in_=dram[kc * 128:(kc + 1) * 128, c0:c0 + w]),
                        writes=[("xin", sl)], dsem=("xin", sl))
                    cnt["cast"] += 1
                    eng = "act" if cnt["cast"] % 2 else "dve"
                    if scale_col is not None:
                        sc = gcol[:, scale_col + kc:scale_col + kc + 1]
                        rd = [("xin", sl), "gcol"]
                    else:
                        sc = scale_const
                        rd = [("xin", sl)]
                    rd = rd + list(extra_reads)
                    key = (name, kc, c0 // 1024)
                    if name == "Win" and c0 == 0:
                        dq = dest[:, kc, 0:512].rearrange("p (i two d) -> p i two d", two=2, d=64)
                        for two in range(2):
                            P.add(eng, copy_op(eng, dq[:, :, two, :],
                                               stg[:, two * 256:(two + 1) * 256].rearrange("p (i d) -> p i d", d=64), sc),
                                  reads=rd, writes=[key])
                        P.add(eng, copy_op(eng, dest[:, kc, 512:1024], stg[:, 512:1024], sc), reads=rd, writes=[key])
                    else:
                        P.add(eng, copy_op(eng, dest[:, kc, c0:c0 + w], stg[:, 0:w], sc), reads=rd, writes=[key])

        if stage >= 0:
            prep_weight("Win", win_d, 8, INW, Win, scale_col=0)
            prep_weight("Woa", woa_d, 4, D, Woa)
            prep_weight("Wob", wob_d, 4, D, Wob)
            prep_weight("Wout", wout_d, 8, D, Wout, scale_const=0.5)

        def norm_a(x_ap_tile, sl, ntok, xnb, xnkey, col=0):
            c1 = slice(col, col + 1)
            P.add("act", lambda e: e.activation(out=xnb[0:ntok, :], in_=x_ap_tile[0:ntok, :], func=AF.Square,
                                                accum_out=ss[0:ntok, c1]),
                  reads=[("xin", sl)], writes=[xnkey, ("ss", col)])
            P.add("act", lambda e: e.activation(out=sd[0:ntok, c1], in_=ss[0:ntok, c1], func=AF.Sqrt,
                                                scale=1.0 / D, bias=epst[0:ntok, :]),
                  reads=[("ss", col), "epst"], writes=[("sd", col)])
            P.add("dve", lambda e: e.reciprocal(out=rstd[0:ntok, c1], in_=sd[0:ntok, c1]),
                  reads=[("sd", col)], writes=[("rstd", col)])
            P.add("act", lambda e: e.activation(out=xnb[0:ntok, :], in_=x_ap_tile[0:ntok, :], func=AF.Copy,
                                                scale=rstd[0:ntok, c1]),
                  reads=[("xin", sl), ("rstd", col)], writes=[xnkey])

        def norm_b(ntok, hTd, hkey, xnb, xnkey, b=0, eng=None):
            for kc in range(8):
                P.add("pe", lambda e, kc=kc, b=b: e.transpose(out=bkb(b)[:, kc * 128:kc * 128 + ntok],
                                                         in_=xnb[0:ntok, kc * 128:(kc + 1) * 128],
                                                         identity=ident[0:ntok, 0:ntok]),
                      reads=[xnkey, "ident"], writes=[BK(b)])
            eng = eng or evac_eng()
            P.add(eng, copy_op(eng, hTd[:, :, 0:ntok],
                               bkb(b).rearrange("p (k t) -> p k t", k=8)[:, :, 0:ntok]),
                  reads=[BK(b)], writes=[hkey])

        def post_norm_residual(bankpair, sl, ntok, gkey, out_ap, out_key, cb=4):
            b0, b1 = bankpair
            for i, b in enumerate((b0, b1)):
                P.add("act", lambda e, b=b, i=i: e.activation(out=junk[0:ntok, :],
                                                              in_=bk(b)[0:ntok, :], func=AF.Square,
                                                              accum_out=ss[0:ntok, cb + i:cb + i + 1]),
                      reads=[BK(b)], writes=["junk", ("ss", cb + i)])
            P.add("dve", lambda e: e.tensor_tensor(out=ss[0:ntok, cb + 2:cb + 3], in0=ss[0:ntok, cb:cb + 1],
                                                   in1=ss[0:ntok, cb + 1:cb + 2], op=ALU.add),
                  reads=[("ss", cb), ("ss", cb + 1)], writes=[("ss", cb + 2)])
            P.add("act", lambda e: e.activation(out=sd[0:ntok, cb:cb + 1], in_=ss[0:ntok, cb + 2:cb + 3], func=AF.Sqrt,
                                                scale=1.0 / D, bias=epst[0:ntok, :]),
                  reads=[("ss", cb + 2), "epst"], writes=[("sd", cb)])
            P.add("dve", lambda e: e.reciprocal(out=rstd[0:ntok, cb:cb + 1], in_=sd[0:ntok, cb:cb + 1]),
                  reads=[("sd", cb)], writes=[("rstd", cb)])
            for i, b in enumerate((b0, b1)):
                P.add("dve", lambda e, b=b, i=i: e.scalar_tensor_tensor(
                    out=tmp[0:ntok, i * 512:(i + 1) * 512], in0=bk(b)[0:ntok, :], scalar=rstd[0:ntok, cb:cb + 1],
                    in1=gbc[0:ntok, i * 512:(i + 1) * 512], op0=ALU.mult, op1=ALU.mult),
                    reads=[BK(b), ("rstd", cb), gkey], writes=["tmp"])
            P.add("pool", lambda e: e.tensor_tensor(out=out_ap[0:ntok, 0:D], in0=tmp[0:ntok, 0:D],
                                                    in1=xin[sl][0:ntok, :], op=ALU.add),
                  reads=["tmp", ("xin", sl)], writes=[out_key])

        pj = {"i": 0}

        def next_pj():
            pj["i"] += 1
            return 3 + pj["i"] % 5

        def p1_front(T):
            t, kind, ntok, samp, slot = T["t"], T["kind"], T["ntok"], T["samp"], T["slot"]
            sl = cnt["xin"] % 4
            cnt["xin"] += 1
            T["sl"] = sl
            hi = T["hi"]
            src = xs_d if samp else xc_d[t * 128:(t + 1) * 128, :]
            P.add("sp", lambda e: e.dma_start(out=xin[sl][0:ntok, :], in_=src),
                  writes=[("xin", sl)], dsem=("xin", sl))

        def p1_front_a(T):
            norm_a(xin[T["sl"]], T["sl"], T["ntok"], xn[T["hi"]], ("xn", T["hi"]))

        def p1_front_b(T):
            ntok, hi = T["ntok"], T["hi"]
            norm_b(ntok, hT[hi], ("hT", hi), xn[hi], ("xn", hi), b=2, eng="dve")

        def fm_chunk(T, col0, dest, dkey, scale):
            ntok, hi = T["ntok"], T["hi"]
            b = next_pj()
            for kc in range(8):
                P.add("pe", lambda e, kc=kc: e.matmul(bk(b)[:, 0:ntok], lhsT=Win[:, kc, col0:col0 + 128],
                                                      rhs=hT[hi][:, kc, 0:ntok], start=(kc == 0), stop=(kc == 7)),
                      reads=[("hT", hi)] + wkeys("Win", kc, col0, col0 + 128), writes=[BK(b)])
            eng = evac_eng()
            P.add(eng, copy_op(eng, dest, bk(b)[:, 0:ntok], scale), reads=[BK(b)], writes=[dkey])

        def p1_proj(T):
            t, kind, ntok, samp, slot, hi = T["t"], T["kind"], T["ntok"], T["samp"], T["slot"], T["hi"]
            ks = slice(slot * 128, slot * 128 + ntok)
            if kind == "full":
                for ci in range(4):
                    fm_chunk(T, C_QA + ci * 128, QT[:, ci, 0:ntok], ("QT", ci), 0.125)
                for ci in range(4):
                    fm_chunk(T, C_QB + ci * 128, QT[:, 4 + ci, 0:ntok], ("QT", 4 + ci), 0.125)
            fm_chunk(T, C_KA, KTa[:, ks], ("KTa", slot), None)
            for ci in range(4):
                fm_chunk(T, C_KB + ci * 128, KTb[:, ci, ks], ("KTb", slot, ci), None)
            kvout = T["kvout"]
            if not kvout:
                b = next_pj()
                for kc in range(8):
                    P.add("pe", lambda e, kc=kc, b=b: e.matmul(bk(b)[0:ntok, 0:128], lhsT=hT[hi][:, kc, 0:ntok],
                                                          rhs=Win[:, kc, C_VA:C_VA + 128], start=(kc == 0),
                                                          stop=(kc == 7)),
                          reads=[("hT", hi)] + wkeys("Win", kc, C_VA, C_VA + 128), writes=[BK(b)])
                P.add("act", copy_op("act", Vr[0:ntok, slot, 0:2, 0:64],
                                     bk(b)[0:ntok, 0:128].rearrange("p (h d) -> p h d", h=2)),
                      reads=[BK(b)], writes=[("Va", slot)])
                b = next_pj()
                for kc in range(8):
                    P.add("pe", lambda e, kc=kc, b=b: e.matmul(bk(b)[0:ntok, 0:512], lhsT=hT[hi][:, kc, 0:ntok],
                                                          rhs=Win[:, kc, C_VB:C_VB + 512], start=(kc == 0),
                                                          stop=(kc == 7)),
                          reads=[("hT", hi)] + wkeys("Win", kc, C_VB, C_VB + 512), writes=[BK(b)])
                P.add("dve", copy_op("dve", Vr[0:ntok, slot, 2:10, 0:64],
                                     bk(b)[0:ntok, 0:512].rearrange("p (h d) -> p h d", h=8)),
                      reads=[BK(b)], writes=[("Vb", slot)])
            else:
                segs = [(3, C_KA, 256, 0), (4, C_KB, 512, 256), (3, C_VB, 512, 768)]
                for (b, c0, w, to) in segs:
                    for kc in range(8):
                        P.add("pe", lambda e, kc=kc, b=b, c0=c0, w=w: e.matmul(
                            bk(b)[0:ntok, 0:w], lhsT=hT[hi][:, kc, 0:ntok], rhs=Win[:, kc, c0:c0 + w],
                            start=(kc == 0), stop=(kc == 7)),
                            reads=[("hT", hi)] + wkeys("Win", kc, c0, c0 + w), writes=[BK(b)])
                    eng = evac_eng()
                    P.add(eng, copy_op(eng, tmp[0:ntok, to:to + w], bk(b)[0:ntok, 0:w]),
                          reads=[BK(b)], writes=["tmp"])
                P.add("act", copy_op("act", Vr[0:ntok, slot, 0:2, 0:64],
                                     tmp[0:ntok, 128:256].rearrange("p (h d) -> p h d", h=2)),
                      reads=["tmp"], writes=[("Va", slot)])
                P.add("dve", copy_op("dve", Vr[0:ntok, slot, 2:10, 0:64],
                                     tmp[0:ntok, 768:1280].rearrange("p (h d) -> p h d", h=8)),
                      reads=["tmp"], writes=[("Vb", slot)])
                for (dst, c0, w) in T["kvdst"]:
                    P.add("sp", lambda e, dst=dst, c0=c0, w=w: e.dma_start(out=dst, in_=tmp[0:ntok, c0:c0 + w]),
                          reads=["tmp"], dsem="kvst")
            if (not samp) and t <= T_HALO:
                P.add("pool", lambda e: e.tensor_copy(out=Vr[0:ntok, slot, :, 64], in_=hv10[0:ntok, :]),
                      reads=["hv10"], writes=[("V1", slot)])
            else:
                P.add("pool", lambda e: e.memset(Vr[0:ntok, slot, :, 64], 1.0), writes=[("V1", slot)])

        def attention(T):
            ntok = T["ntok"]
            a_tiles, b_tiles, TA, TB, ka, kb_ = T["a_tiles"], T["b_tiles"], T["TA"], T["TB"], T["TAk"], T["TBk"]
            units = []
            for g in range(2):
                for i, (slot, nk, bi) in enumerate(a_tiles):
                    units.append(dict(kind="A", g=g, slot=slot, nk=nk, bi=bi, first=(i == 0),
                                      last=(i == len(a_tiles) - 1)))
            for g in range(2):
                for i, (slot, nk, bi) in enumerate(b_tiles):
                    units.append(dict(kind="B", g=g, slot=slot, nk=nk, bi=bi, first=(i == 0),
                                      last=(i == len(b_tiles) - 1)))
            sbanks = [2, 3, 4, 5]
            W4 = 4 * ntok

            def emit_qk(ui, u):
                b = sbanks[ui % 4]
                u["b"] = b
                u["p"] = ui % 4
                slot, nk, g = u["slot"], u["nk"], u["g"]
                if u["kind"] == "A":
                    P.add("pe", lambda e: e.matmul(
                        bk(b)[0:nk, 0:W4].rearrange("p (h q) -> p h q", h=4),
                        lhsT=KTa[g * 64:(g + 1) * 64, slot * 128:slot * 128 + nk],
                        rhs=QT[g * 64:(g + 1) * 64, 0:4, 0:ntok], start=True, stop=True),
                        reads=[("KTa", slot)] + [("QT", c) for c in range(4)], writes=[BK(b)])
                else:
                    for hh in range(4):
                        pb = g * 64
                        P.add("pe", lambda e, hh=hh, pb=pb: e.matmul(
                            bk(b)[0:nk, hh * ntok:(hh + 1) * ntok],
                            lhsT=KTb[pb:pb + 64, hh, slot * 128:slot * 128 + nk],
                            rhs=QT[pb:pb + 64, 4 + hh, 0:ntok], start=True, stop=True),
                            reads=[("KTb", slot, hh), ("QT", 4 + hh)], writes=[BK(b)])
                tab, tkey = (TA, ka) if u["kind"] == "A" else (TB, kb_)
                bi = u["bi"]
                P.add("dve", lambda e: e.tensor_tensor(
                    out=bk(b)[0:nk, 0:W4].rearrange("p (h q) -> p h q", h=4),
                    in0=bk(b)[0:nk, 0:W4].rearrange("p (h q) -> p h q", h=4),
                    in1=tab[0:nk, bi, 4 * g:4 * g + 4, :], op=ALU.add),
                    reads=[BK(b), tkey], writes=[BK(b)])
                P.add("act", lambda e: e.activation(out=pT[u["p"]][0:nk, 0:W4], in_=bk(b)[0:nk, 0:W4], func=AF.Exp),
                      reads=[BK(b)], writes=[("pT", u["p"])])

            def emit_pv(u):
                slot, nk, g = u["slot"], u["nk"], u["g"]
                ob = 6 + g
                for hh in range(4):
                    vh = g if u["kind"] == "A" else 2 + 2 * hh + g
                    P.add("pe", lambda e, hh=hh, vh=vh: e.matmul(
                        bk(ob)[0:ntok, hh * 65:(hh + 1) * 65], lhsT=pT[u["p"]][0:nk, hh * ntok:(hh + 1) * ntok],
                        rhs=Vr[0:nk, slot, vh, 0:65], start=(u["first"] and hh == 0), stop=(u["last"] and hh == 3)),
                        reads=[("pT", u["p"]), ("Va", slot) if u["kind"] == "A" else ("Vb", slot), ("V1", slot)],
                        writes=[BK(ob)])
                if u["last"]:
                    o3 = bk(ob)[0:ntok, 0:260].rearrange("p (h d) -> p h d", h=4)
                    dsl = slice(4 * g, 4 * g + 4)
                    if u["kind"] == "A":
                        P.add("dve", lambda e: e.tensor_tensor(out=den[0:ntok, dsl], in0=o3[:, :, 64],
                                                               in1=exps[0:ntok, dsl], op=ALU.add),
                              reads=[BK(ob), "exps"], writes=[("den", g)])
                    else:
                        P.add("dve", lambda e: e.tensor_scalar(out=den[0:ntok, dsl], in0=o3[:, :, 64],
                                                               scalar1=1e-30, scalar2=None, op0=ALU.add),
                              reads=[BK(ob)], writes=[("den", g)])
                    P.add("dve", lambda e: e.reciprocal(out=rden[0:ntok, dsl], in_=den[0:ntok, dsl]),
                          reads=[("den", g)], writes=[("rden", g)])
                    for hh in range(4):
                        c = (4 * g + hh) * 64 if u["kind"] == "A" else 512 + (2 * hh + g) * 64
                        P.add("dve", lambda e, hh=hh, c=c: e.tensor_scalar(
                            out=Otok[0:ntok, c:c + 64], in0=bk(ob)[0:ntok, hh * 65:hh * 65 + 64],
                            scalar1=rden[0:ntok, 4 * g + hh:4 * g + hh + 1], scalar2=None, op0=ALU.mult),
                            reads=[BK(ob), ("rden", g)], writes=[("Otok", c // 128)])

            L = 3
            for i in range(len(units) + L):
                if i < len(units):
                    emit_qk(i, units[i])
                if i - L >= 0:
                    emit_pv(units[i - L])

        def p1_back(T, mid=None, early=None, late=None):
            t, kind, ntok, samp, slot, hi, sl = T["t"], T["kind"], T["ntok"], T["samp"], T["slot"], T["hi"], T["sl"]
            if early is not None:
                early()
            if kind != "full":
                if mid is not None:
                    mid()
                if late is not None:
                    late()
                return None
            attention(T)
            if mid is not None:
                mid()
            if late is not None:
                late()
            if samp:
                dbg_dump("QT", QT[:, :, 0:16], [128, 8, 16], BF16, [("QT", c) for c in range(8)])
                dbg_dump("hT", hT[hi][:, :, 0:16], [128, 8, 16], BF16, [("hT", hi)])
                dbg_dump("Otok", Otok[0:16, :], [16, 1024], BF16, [("Otok", c) for c in range(8)])
                dbg_dump("den", den[0:16, :], [16, 8], F32, [("den", 0), ("den", 1)])
                dbg_dump("KTb", KTb[:, :, 0:640], [128, 4, 640], BF16, [("KTb", s_, c) for s_ in range(5) for c in range(4)])
                dbg_dump("Vr", Vr[:, 0:5, :, :], [128, 5, 10, 65], BF16, [("Vb", s_) for s_ in range(5)] + [("Va", 3), ("Va", 4)] + [("V1", s_) for s_ in range(5)])
            b = 7
            for kc in range(8):
                P.add("pe", lambda e, kc=kc, b=b: e.transpose(out=bkb(b)[:, kc * 128:kc * 128 + ntok],
                                                         in_=Otok[0:ntok, kc * 128:(kc + 1) * 128],
                                                         identity=ident[0:ntok, 0:ntok]),
                      reads=[("Otok", kc), "ident"], writes=[BK(b)])
            eng = "dve"
            P.add(eng, copy_op(eng, OT[:, :, 0:ntok], bkb(b).rearrange("p (k t) -> p k t", k=8)[:, :, 0:ntok]),
                  reads=[BK(b)], writes=["OT"])
            for fc in range(8):
                bs = [0, 1, 2, 3] if fc % 2 == 0 else [4, 5, 6, 7]
                gt = gtmp[fc % 2]
                for (b, c0) in ((bs[0], C_GA + fc * 128), (bs[1], C_GB + fc * 128)):
                    for kc in range(8):
                        P.add("pe", lambda e, kc=kc, b=b, c0=c0: e.matmul(
                            bk(b)[:, 0:ntok], lhsT=Win[:, kc, c0:c0 + 128], rhs=hT[hi][:, kc, 0:ntok],
                            start=(kc == 0), stop=(kc == 7)),
                            reads=[("hT", hi)] + wkeys("Win", kc, c0, c0 + 128), writes=[BK(b)])
                for (b, W, wn, off) in ((bs[2], Woa, "Woa", 0), (bs[3], Wob, "Wob", 4)):
                    for kc in range(4):
                        P.add("pe", lambda e, kc=kc, b=b, W=W, off=off, fc=fc: e.matmul(
                            bk(b)[:, 0:ntok], lhsT=W[:, kc, fc * 128:(fc + 1) * 128], rhs=OT[:, off + kc, 0:ntok],
                            start=(kc == 0), stop=(kc == 3)),
                            reads=["OT", (wn, kc, 0)], writes=[BK(b)])
                for i in range(2):
                    P.add("act", lambda e, i=i, gt=gt, bs=bs: e.activation(out=gt[:, i * 128:i * 128 + ntok],
                                                             in_=bk(bs[i])[:, 0:ntok], func=AF.Tanh, scale=0.5),
                          reads=[BK(bs[i])], writes=[("gt", fc % 2, i)])
                for i in range(2):
                    P.add("dve", lambda e, i=i, gt=gt, bs=bs: e.scalar_tensor_tensor(
                        out=gt[:, i * 128:i * 128 + ntok], in0=gt[:, i * 128:i * 128 + ntok], scalar=1.0,
                        in1=bk(bs[2 + i])[:, 0:ntok], op0=ALU.add, op1=ALU.mult),
                        reads=[("gt", fc % 2, i), BK(bs[2 + i])], writes=[("gt", fc % 2, i)])
                P.add("pool", lambda e, fc=fc, gt=gt: e.tensor_tensor(out=mT[:, fc, 0:ntok], in0=gt[:, 0:ntok],
                                                        in1=gt[:, 128:128 + ntok], op=ALU.add),
                      reads=[("gt", fc % 2, 0), ("gt", fc % 2, 1)], writes=[("mT", fc)])
            for half in range(2):
                b = half
                for kc in range(8):
                    P.add("pe", lambda e, kc=kc, b=b, half=half: e.matmul(
                        bk(b)[0:ntok, :], lhsT=mT[:, kc, 0:ntok], rhs=Wout[:, kc, half * 512:(half + 1) * 512],
                        start=(kc == 0), stop=(kc == 7)),
                        reads=[("mT", kc), ("Wout", kc, 0)], writes=[BK(b)])
            if samp:
                dbg_dump("mT", mT[:, :, 0:16], [128, 8, 16], BF16, [("mT", c) for c in range(8)])
                dbg_dump("OT", OT[:, :, 0:16], [128, 8, 16], BF16, ["OT"])
            def fin():
                post_norm_residual((0, 1), sl, ntok, "gbc", xin[sl], ("xin", sl))
                row = T["x1row"]
                P.add("sp", lambda e: e.dma_start(out=x1s_d[row:row + ntok, :], in_=xin[sl][0:ntok, :]),
                      reads=[("xin", sl)], writes=[("x1s", row)], dsem=("xst", sl))
            return fin

        def load_caches():
            for kt in range(4):
                sl = cnt["xin"] % 4
                cnt["xin"] += 1
                P.add("sp", lambda e, kt=kt, sl=sl: e.dma_start(out=xin[sl][:, 0:512],
                                                                in_=cbk_d[kt * 128:(kt + 1) * 128, :]),
                      writes=[("xin", sl)], dsem=("xin", sl))
                P.add("sp", lambda e, kt=kt, sl=sl: e.dma_start(out=xin[sl][:, 512:1024],
                                                                in_=cbv_d[kt * 128:(kt + 1) * 128, :]),
                      writes=[("xin", sl)], dsem=("xin", sl))
                xb = xn[kt % 2]
                P.add("dve", copy_op("dve", xb[:, 0:512], xin[sl][:, 0:512]), reads=[("xin", sl)],
                      writes=[("xn", kt % 2)])
                P.add("act", copy_op("act", Vr[:, kt, 2:10, 0:64],
                                     xin[sl][:, 512:1024].rearrange("p (h d) -> p h d", h=8)),
                      reads=[("xin", sl)], writes=[("Vb", kt)])
                P.add("pool", lambda e, kt=kt: e.memset(Vr[:, kt, :, 64], 1.0), writes=[("V1", kt)])
                b = kt % 2
                for c in range(4):
                    P.add("pe", lambda e, c=c, b=b, xb=xb: e.transpose(out=bkb(b)[:, c * 128:(c + 1) * 128],
                                                                       in_=xb[:, c * 128:(c + 1) * 128],
                                                                       identity=ident[:, :]),
                          reads=[("xn", kt % 2), "ident"], writes=[BK(b)])
                P.add("dve", copy_op("dve", KTb[:, :, kt * 128:(kt + 1) * 128],
                                     bkb(b)[:, 0:512].rearrange("p (c k) -> p c k", c=4)),
                      reads=[BK(b)], writes=[("KTb", kt, c) for c in range(4)])
            sl = cnt["xin"] % 4
            cnt["xin"] += 1
            P.add("sp", lambda e: e.dma_start(out=xin[sl][:, 0:128], in_=cak_d), writes=[("xin", sl)],
                  dsem=("xin", sl))
            P.add("sp", lambda e: e.dma_start(out=xin[sl][:, 128:256], in_=cav_d), writes=[("xin", sl)],
                  dsem=("xin", sl))
            xb = xn[0]
            P.add("dve", copy_op("dve", xb[:, 0:128], xin[sl][:, 0:128]), reads=[("xin", sl)], writes=[("xn", 0)])
            P.add("act", copy_op("act", Vr[:, 3, 0:2, 0:64], xin[sl][:, 128:256].rearrange("p (h d) -> p h d", h=2)),
                  reads=[("xin", sl)], writes=[("Va", 3)])
            P.add("pe", lambda e: e.transpose(out=bkb(0)[:, 0:128], in_=xb[:, 0:128], identity=ident[:, :]),
                  reads=[("xn", 0), "ident"], writes=[BK(0)])
            P.add("dve", copy_op("dve", KTa[:, 3 * 128:4 * 128], bkb(0)[:, 0:128]), reads=[BK(0)],
                  writes=[("KTa", 3)])

        tiles = []
        samp = dict(t=-1, kind="full", ntok=16, samp=True, slot=SAMP_SLOT, kvout=True,
                    kvdst=[(sak_d, 0, 128), (sav_d, 128, 128), (sbk_d, 256, 512), (sbv_d, 768, 512)],
                    a_tiles=[(3, 128, 0), (SAMP_SLOT, 16, 1)],
                    b_tiles=[(0, 128, 0), (1, 128, 1), (2, 128, 2), (3, 128, 3), (SAMP_SLOT, 16, 4)],
                    TA=BAs, TB=BBs, TAk="BAs", TBk="BBs", x1row=NTILES * 128)
        tiles.append(samp)
        for t in range(NTILES):
            kind = "kv" if t < T_HALO_KV else "full"
            T = dict(t=t, kind=kind, ntok=128, samp=False, slot=t % RT, kvout=(t >= NTILES - 4), x1row=t * 128)
            if T["kvout"]:
                r = (t - (NTILES - 4)) * 128
                dst = [(obk_d[r:r + 128, :], 256, 512), (obv_d[r:r + 128, :], 768, 512)]
                if t == NTILES - 1:
                    dst += [(oak_d, 0, 128), (oav_d, 128, 128)]
                T["kvdst"] = dst
            if kind == "full":
                T["a_tiles"] = [((t - 1) % RT, 128, 0), (t % RT, 128, 1)]
                T["b_tiles"] = [((t - 4 + i) % RT, 128, i) for i in range(5)]
                T.update(TA=BA, TB=BB, TAk="BA", TBk="BB")
            tiles.append(T)
        for i, T in enumerate(tiles):
            T["hi"] = i % 2

        if ntiles_p1 is not None:
            tiles = tiles[:ntiles_p1]
        if stage >= 1:
            load_caches()
        if stage >= 2:
            def front_proj(Tn, between=None):
                p1_front_b(Tn)
                if between is not None:
                    between()
                p1_proj(Tn)

            p1_front(tiles[0])
            p1_front_a(tiles[0])
            front_proj(tiles[0])
            if len(tiles) > 1:
                p1_front(tiles[1])
                p1_front_a(tiles[1])
            pend1 = [None]
            for i, T in enumerate(tiles):
                def early(i=i):
                    if i + 2 < len(tiles):
                        p1_front(tiles[i + 2])

                def mid(i=i):
                    def flush():
                        if pend1[0] is not None:
                            pend1[0]()
                            pend1[0] = None
                    if i + 1 < len(tiles):
                        front_proj(tiles[i + 1], flush)
                    else:
                        flush()

                def late(i=i):
                    if i + 2 < len(tiles):
                        p1_front_a(tiles[i + 2])

                r = p1_back(T, mid, early, late)
                if r is not None:
                    pend1[0] = r
            if pend1[0] is not None:
                pend1[0]()
        if stage < 3:
            mo = int(os.environ.get("KMAXOPS", "0"))
            if mo:
                for i, op in enumerate(P.ops[:mo]):
                    print("OP", i, op.eng, op.reads, op.writes, op.dsem)
                P.ops = P.ops[:mo]
            P.finalize()
            P.emit(nc)
            return nc

        for op in reversed(P.ops):
            if op.eng == "pe":
                op.writes.append(("bar", "pe"))
                break
        for i, eng in enumerate(("act", "dve", "pool")):
            if eng == "act":
                P.add(eng, lambda e, i=i: e.activation(out=bars[:, i:i + 1], in_=epst[:, 0:1], func=AF.Copy),
                      writes=[("bar", eng)])
            else:
                P.add(eng, lambda e, i=i: e.memset(bars[:, i:i + 1], 0.0), writes=[("bar", eng)])
        BARS = [("bar", e) for e in ("pe", "act", "dve", "pool")]
        for i, eng in enumerate(("act", "dve", "pool")):
            if eng == "act":
                P.add(eng, lambda e, i=i: e.activation(out=bars[:, 4 + i:5 + i], in_=epst[:, 0:1], func=AF.Copy),
                      reads=BARS, writes=[("bar2", eng)])
            else:
                P.add(eng, lambda e, i=i: e.memset(bars[:, 4 + i:5 + i], 0.0), reads=BARS, writes=[("bar2", eng)])

        P.add("sp", lambda e: e.dma_start(out=gbc[:, :], in_=gqf_d.partition_broadcast(128)),
              reads=BARS, writes=["gbc"], dsem="gbc2")
        prep_weight("Wup", wup_d, 8, 2 * DFF, Wup, scale_col=8, extra_reads=BARS)
        prep_weight("Wdn", wdn_d, 24, D, Wdn, extra_reads=BARS)

        def p2_front(G):
            hi = G["hi"]
            for i, T in enumerate(G["tiles"]):
                ntok = T["ntok"]
                sl = cnt["xin"] % 4
                cnt["xin"] += 1
                T["sl"] = sl
                row = T["x1row"]
                P.add("sp", lambda e, sl=sl, row=row, ntok=ntok: e.dma_start(out=xin[sl][0:ntok, :],
                                                                            in_=x1s_d[row:row + ntok, :]),
                      reads=[("x1s", row)], writes=[("xin", sl)], dsem=("xin", sl))
                cnt["xn2"] = cnt.get("xn2", 0) + 1
                T["xi"] = cnt["xn2"] % 2

        def p2_front_a(G):
            for i, T in enumerate(G["tiles"]):
                xi = T["xi"]
                norm_a(xin[T["sl"]], T["sl"], T["ntok"], xn[xi], ("xn", xi), col=i)

        def p2_front_b(G):
            hi = G["hi"]
            for i, T in enumerate(G["tiles"]):
                xi = T["xi"]
                norm_b(T["ntok"], h2T[hi][:, :, i * 128:(i + 1) * 128], ("h2T", hi, i), xn[xi], ("xn", xi))

        GR = int(os.environ.get("KGR", "2"))

        def p2_back(G, mid=None, pre=None, early=None):
            tl = G["tiles"]
            pre = pre if pre is not None else []
            hi = G["hi"]
            samp = tl[0]["samp"]
            N = sum(T["ntok"] for T in tl)
            halo = (not samp) and tl[0]["t"] == T_HALO
            H = hists if samp else hist
            hk = "hists" if samp else "hist"
            hkeys = [("h2T", hi, i) for i in range(len(tl))]

            def tail(j):
                cbf = cbuf[j % 2]
                bg = 4 + j % GR
                P.add("act", lambda e: e.activation(out=cbf[:, 0:N], in_=cbf[:, 0:N], func=AF.Gelu_apprx_tanh),
                      reads=[("cb", j % 2)], writes=[("cb", j % 2)])
                P.add("dve", lambda e: e.tensor_tensor(out=actT[:, j, 0:N], in0=cbf[:, 0:N],
                                                       in1=bk(bg)[:, 0:N], op=ALU.mult),
                      reads=[("cb", j % 2), BK(bg)], writes=[("actT", j)])

            for j in range(24):
                ub, cbf = ubuf[j % 2], cbuf[j % 2]
                bu, bg = 2 + j % 2, 4 + j % GR
                if pre and j in (0, 1):
                    pre.pop(0)()
                if early is not None and j == 3:
                    early[0]()
                if early is not None and j == 13:
                    early[1]()
                for kc in range(8):
                    P.add("pe", lambda e, kc=kc, j=j, bu=bu: e.matmul(
                        bk(bu)[:, 0:N], lhsT=Wup[:, kc, j * 128:(j + 1) * 128], rhs=h2T[hi][:, kc, 0:N],
                        start=(kc == 0), stop=(kc == 7)),
                        reads=hkeys + wkeys("Wup", kc, j * 128, (j + 1) * 128), writes=[BK(bu)])
                if not halo:
                    for kc in range(8):
                        P.add("pe", lambda e, kc=kc, j=j, bg=bg: e.matmul(
                            bk(bg)[:, 0:N], lhsT=Wup[:, kc, DFF + j * 128:DFF + (j + 1) * 128],
                            rhs=h2T[hi][:, kc, 0:N], start=(kc == 0), stop=(kc == 7)),
                            reads=hkeys + wkeys("Wup", kc, DFF + j * 128, DFF + (j + 1) * 128),
                            writes=[BK(bg)])
                P.add("act", copy_op("act", ub[:, 2:2 + N], bk(bu)[:, 0:N]), reads=[BK(bu)],
                      writes=[("ub", j % 2)])
                P.add("pool", lambda e, j=j, ub=ub: e.tensor_copy(out=ub[:, 0:2], in_=H[:, 2 * j:2 * j + 2]),
                      reads=[(hk, j)], writes=[("ubh", j % 2)])
                if halo:
                    P.add("pool", lambda e, j=j, ub=ub: e.tensor_scalar(out=H[:, 2 * j:2 * j + 2],
                                                                        in0=ub[:, N:N + 2], scalar1=hv[:, 0:1],
                                                                        scalar2=None, op0=ALU.mult),
                          reads=[("ub", j % 2), ("ubh", j % 2), "hv"], writes=[(hk, j)])
                    continue
                P.add("pool", lambda e, j=j, ub=ub: e.tensor_copy(out=H[:, 2 * j:2 * j + 2], in_=ub[:, N:N + 2]),
                      reads=[("ub", j % 2), ("ubh", j % 2)], writes=[(hk, j)])
                P.add("act", lambda e, j=j, ub=ub, cbf=cbf: e.activation(
                    out=cbf[:, 0:N], in_=ub[:, 2:2 + N], func=AF.Identity,
                    scale=cw[:, 48 + j:49 + j], bias=cbias[:, j:j + 1]),
                    reads=[("ub", j % 2), "cw", "cbias"], writes=[("cb", j % 2)])
                for tap in (1, 0):
                    P.add("dve", lambda e, j=j, ub=ub, cbf=cbf, tap=tap: e.scalar_tensor_tensor(
                        out=cbf[:, 0:N], in0=ub[:, tap:tap + N], scalar=cw[:, tap * 24 + j:tap * 24 + j + 1],
                        in1=cbf[:, 0:N], op0=ALU.mult, op1=ALU.add),
                        reads=[("ub", j % 2), ("ubh", j % 2), ("cb", j % 2), "cw"], writes=[("cb", j % 2)])
                if j >= 1:
                    tail(j - 1)
            while pre:
                pre.pop(0)()
            if halo:
                if mid is not None:
                    mid()
                return []
            tail(23)
            if mid is not None:
                mid()
            deferred = []
            for i, T in enumerate(tl):
                ntok, sl, t = T["ntok"], T["sl"], T["t"]
                bp = (0, 1) if i == 0 else (6, 7)
                for half in range(2):
                    b = bp[half]
                    for j in range(24):
                        P.add("pe", lambda e, j=j, b=b, half=half, i=i, ntok=ntok: e.matmul(
                            bk(b)[0:ntok, :], lhsT=actT[:, j, i * 128:i * 128 + ntok],
                            rhs=Wdn[:, j, half * 512:(half + 1) * 512], start=(j == 0), stop=(j == 23)),
                            reads=[("actT", j), ("Wdn", j, 0)], writes=[BK(b)])
                def fin(bp=bp, sl=sl, ntok=ntok, i=i, t=t):
                    post_norm_residual(bp, sl, ntok, "gbc", xin[sl], ("xin", sl), cb=4 + 4 * i)
                    dst = ys_d if samp else yp_d[(t - T0) * 128:(t - T0 + 1) * 128, :]
                    P.add("sp", lambda e: e.dma_start(out=dst, in_=xin[sl][0:ntok, :]),
                          reads=[("xin", sl)], dsem=("xst", sl))
                deferred.append(fin)
            if samp or tl[-1]["t"] == NTILES - 1:
                od = scv_d if samp else ocv_d
                for c in range(24):
                    P.add("sp", lambda e, c=c: e.dma_start(out=od[:, c * 128:(c + 1) * 128].rearrange("t p -> p t"),
                                                           in_=H[:, 2 * c:2 * c + 2]),
                          reads=[(hk, c)], dsem="cvout")
            return deferred

        full = [T for T in tiles if T["kind"] == "full"]
        pt = [T for T in full if not T["samp"]]
        groups = [dict(tiles=[pt[0]])]
        for i in range(1, len(pt), 2):
            groups.append(dict(tiles=pt[i:i + 2]))
        groups.append(dict(tiles=[T for T in full if T["samp"]]))
        for i, G in enumerate(groups):
            G["hi"] = i % 2
        p2_front(groups[0])
        p2_front_a(groups[0])
        p2_front_b(groups[0])
        pend = []
        for i, G in enumerate(groups):
            nxt = (lambda Gn=groups[i + 1]: p2_front_b(Gn)) if i + 1 < len(groups) else None
            early = ((lambda Gn=groups[i + 1]: p2_front(Gn)), (lambda Gn=groups[i + 1]: p2_front_a(Gn))) \
                if i + 1 < len(groups) else None
            pend = p2_back(G, nxt, pend, early)
        for f_ in pend:
            f_()

        P.finalize()
        P.emit(nc)
    return nc


def _t5_bucket_np(n):
    import jax
    import jax.numpy as jnp
    try:
        cpu = jax.devices("cpu")[0]
    except Exception:
        cpu = None
    ctx = jax.default_device(cpu) if cpu is not None else contextlib.nullcontext()
    with ctx:
        n = jnp.asarray(n, dtype=jnp.int32)
        half = 16
        max_exact = 8
        ret = jnp.where(n < 0, half, 0)
        a = jnp.abs(n)
        af = jnp.maximum(a, 1).astype(jnp.float32)
        large = max_exact + (jnp.log(af / max_exact) / math.log(128 / max_exact)
                             * (half - max_exact)).astype(jnp.int32)
        large = jnp.minimum(large, half - 1)
        out = ret + jnp.where(a < max_exact, a, large)
        return np.asarray(out)


def _bias_tables(t5_table, rel_b):
    q = np.arange(128)
    kk = np.arange(256)
    rel = (128 + q)[None, :] - kk[:, None]
    cq = 2 + q // 64
    ck = kk // 64
    vis = ((cq[None, :] - ck[:, None]) >= 0) & ((cq[None, :] - ck[:, None]) <= 2)
    idx = _t5_bucket_np(rel)
    A = t5_table[:, idx]
    A = np.where(vis[None], A, np.float32(NEG)).astype(np.float32)
    A = A.reshape(8, 2, 128, 128).transpose(2, 1, 0, 3)
    kk = np.arange(640)
    rel = (512 + q)[None, :] - kk[:, None]
    cq = 8 + q // 64
    ck = kk // 64
    vis = ((cq[None, :] - ck[:, None]) >= 0) & ((cq[None, :] - ck[:, None]) <= 8)
    idx = np.clip(rel, -128, 128) + 128
    B = rel_b[:, idx]
    B = np.where(vis[None], B, np.float32(NEG)).astype(np.float32)
    perm = [0, 2, 4, 6, 1, 3, 5, 7]
    B = B.reshape(8, 5, 128, 128)[perm].transpose(2, 1, 0, 3)
    qs = np.arange(16)
    kk = np.arange(256)
    rel = qs[None, :] - kk[:, None] + 128
    As = t5_table[:, _t5_bucket_np(rel)].astype(np.float32)
    As = As.reshape(8, 2, 128, 16).transpose(2, 1, 0, 3)
    kk = np.arange(640)
    rel = qs[None, :] - kk[:, None] + 512
    Bs = rel_b[:, np.clip(rel, -128, 128) + 128].astype(np.float32)
    Bs = Bs.reshape(8, 5, 128, 16)[perm].transpose(2, 1, 0, 3)
    c = np.ascontiguousarray
    return (c(A).reshape(128, -1), c(B).reshape(128, -1), c(As).reshape(128, -1), c(Bs).reshape(128, -1))


_NC_CACHE = {}


def kernel(x_prompt, x_sample, cache_a_k, cache_a_v, cache_b_k, cache_b_v, state_conv,
           w_in, w_oa, w_ob, w_out, sink_a, t5_table, rel_table_b,
           g_pre_mix, g_post_mix, g_pre_ffn, g_post_ffn, w_upg, conv_w, conv_b, w_down):
    f = lambda a: np.ascontiguousarray(np.asarray(a, dtype=np.float32))
    x_prompt, x_sample = f(x_prompt), f(x_sample)
    bA, bB, bAs, bBs = _bias_tables(f(t5_table), f(rel_table_b)[0])
    shared = {
        "w_in": f(w_in)[0], "w_oa": f(w_oa)[0], "w_ob": f(w_ob)[0], "w_out": f(w_out)[0],
        "sink": f(sink_a).reshape(1, 8), "g_pre_mix": f(g_pre_mix).reshape(1, D),
        "g_post_mix": f(g_post_mix).reshape(1, D), "g_pre_ffn": f(g_pre_ffn).reshape(1, D),
        "g_post_ffn": f(g_post_ffn).reshape(1, D), "w_upg": f(w_upg)[0], "conv_w": f(conv_w)[0],
        "conv_b": f(conv_b).reshape(1, DFF), "w_down": f(w_down)[0],
        "biasA": bA, "biasB": bB, "biasAs": bAs, "biasBs": bBs,
        "ident": np.eye(128, dtype=np.float32),
    }
    halo = T0 * 128
    in_maps = []
    for c in range(8):
        b, half = c // 2, c % 2
        start = half * HALF
        xc = np.zeros((NTILES * 128, D), np.float32)
        if half == 0:
            xc[halo:] = x_prompt[b, 0:HALF]
        else:
            xc[:] = x_prompt[b, start - halo:start + HALF]
        m = dict(shared)
        m.update({
            "xc": xc, "xs": x_sample[c],
            "cak": f(cache_a_k)[0, c].reshape(128, 128), "cav": f(cache_a_v)[0, c].reshape(128, 128),
            "cbk": f(cache_b_k)[0, c].reshape(512, 512), "cbv": f(cache_b_v)[0, c].reshape(512, 512),
            "sconv": f(state_conv)[0, c],
            "hv": np.full((128, 1), float(half), np.float32),
        })
        in_maps.append(m)
    if "nc" not in _NC_CACHE:
        _NC_CACHE["nc"] = build_nc()
    nc = _NC_CACHE["nc"]
    res = run_bass_kernel_spmd(nc, in_maps, core_ids=list(range(8)))
    R = res.results
    y_prompt = np.stack([np.concatenate([R[2 * b]["yp"], R[2 * b + 1]["yp"]], axis=0) for b in range(4)], 0)
    y_sample = np.stack([R[c]["ys"] for c in range(8)], 0)
    odd = [R[2 * b + 1] for b in range(4)]
    nak = np.stack([r["oak"].reshape(128, 2, 64) for r in odd], 0)[None]
    nav = np.stack([r["oav"].reshape(128, 2, 64) for r in odd], 0)[None]
    nbk = np.stack([r["obk"].reshape(512, 8, 64) for r in odd], 0)[None]
    nbv = np.stack([r["obv"].reshape(512, 8, 64) for r in odd], 0)[None]
    ncv = np.stack([r["ocv"] for r in odd], 0)[None]
    sak = np.stack([R[c]["sak"].reshape(16, 2, 64) for c in range(8)], 0)[None]
    sav = np.stack([R[c]["sav"].reshape(16, 2, 64) for c in range(8)], 0)[None]
    sbk = np.stack([R[c]["sbk"].reshape(16, 8, 64) for c in range(8)], 0)[None]
    sbv = np.stack([R[c]["sbv"].reshape(16, 8, 64) for c in range(8)], 0)[None]
    scv = np.stack([R[c]["scv"] for c in range(8)], 0)[None]
    out = (y_prompt, y_sample, nak, nav, nbk, nbv, ncv, sak, sav, sbk, sbv, scv)
    return tuple(np.ascontiguousarray(o.astype(np.float32)) for o in out)
```

```python
import math
import contextlib
import numpy as np
import concourse.bass as bass
import concourse.mybir as mybir
from concourse.bass_utils import run_bass_kernel_spmd

F32 = mybir.dt.float32
BF16 = mybir.dt.bfloat16
AF = mybir.ActivationFunctionType
ALU = mybir.AluOpType

D = 1024
DFF = 3072
INW = 4352
SEQ = 8192
HALF = 4096
NT_MAIN = 32
T_HALO_KV = 4
T_HALO = 4
T0 = 5
NTILES = T0 + NT_MAIN
RT = 8
EPS = 1e-6
NEG = -30000.0
SAMP_SLOT = 4
C_QA, C_KA, C_VA, C_QB, C_KB, C_VB, C_GA, C_GB = 0, 512, 640, 768, 1280, 1792, 2304, 3328


class _Op:
    __slots__ = ("eng", "fn", "reads", "writes", "dsem", "deps", "mark", "rank", "waits", "clock")

    def __init__(self, eng, fn, reads, writes, dsem):
        self.eng = eng
        self.fn = fn
        self.reads = reads
        self.writes = writes
        self.dsem = dsem
        self.deps = None
        self.mark = False
        self.rank = 0
        self.waits = None
        self.clock = None


class Prog:
    ENGS = ("pe", "act", "dve", "pool", "sp")

    def __init__(self):
        self.ops = []
        self.group_final = set()

    def add(self, eng, fn, reads=(), writes=(), dsem=None):
        self.ops.append(_Op(eng, fn, tuple(reads), list(writes), dsem))

    def finalize(self):
        last_w = {}
        readers = {}
        ops = self.ops
        for i, op in enumerate(ops):
            deps = {}
            for k in op.reads:
                w = last_w.get(k)
                if w is not None:
                    deps[w] = "raw"
            for k in op.writes:
                w = last_w.get(k)
                if w is not None and w not in deps:
                    deps[w] = "waw"
                last_by_eng = {}
                for r in readers.get(k, ()):
                    if r == i:
                        continue
                    if ops[r].dsem is not None:
                        if r not in deps:
                            deps[r] = "war"
                    elif last_by_eng.get(ops[r].eng, -1) < r:
                        last_by_eng[ops[r].eng] = r
                for r in last_by_eng.values():
                    if r not in deps:
                        deps[r] = "war"
            for k in op.reads:
                readers.setdefault(k, []).append(i)
            for k in op.writes:
                last_w[k] = i
                readers[k] = []
            need = []
            for d, kind in deps.items():
                dop = ops[d]
                if dop.dsem is not None and dop.dsem == op.dsem and op.dsem in self.group_final:
                    continue
                if dop.dsem is not None or op.dsem is not None:
                    need.append(d)
                elif dop.eng != op.eng:
                    need.append(d)
                elif kind == "raw" and op.eng != "pe":
                    need.append(d)
            op.deps = need
            for d in need:
                ops[d].mark = True
        cnt = {e: 0 for e in self.ENGS}
        dcnt = {}
        for op in ops:
            if op.dsem is not None:
                dcnt[op.dsem] = dcnt.get(op.dsem, 0) + 16
                op.rank = dcnt[op.dsem]
            elif op.mark:
                cnt[op.eng] += 1
                op.rank = cnt[op.eng]
        for op in ops:
            if op.dsem is not None and op.dsem in self.group_final:
                op.rank = dcnt[op.dsem]
        clock = {e: {} for e in self.ENGS}
        nw = 0
        for op in ops:
            ck = clock[op.eng]
            waits = {}
            for d in sorted(op.deps, key=lambda d: -ops[d].rank):
                dop = ops[d]
                sk = ("d", dop.dsem) if dop.dsem is not None else ("e", dop.eng)
                if ck.get(sk, 0) >= dop.rank:
                    continue
                if waits.get(sk, 0) < dop.rank:
                    waits[sk] = dop.rank
                ck[sk] = dop.rank
                if dop.clock is not None:
                    for k2, v2 in dop.clock.items():
                        if ck.get(k2, 0) < v2:
                            ck[k2] = v2
            op.waits = waits
            nw += len(waits)
            if op.dsem is not None:
                op.clock = dict(ck)
            elif op.mark:
                c2 = dict(ck)
                c2[("e", op.eng)] = op.rank
                op.clock = c2
        self.n_waits = nw
        self.counts = cnt
        self.dcounts = dcnt
        import os
        pr = os.environ.get("KPRINT")
        if pr:
            a, b = [int(x) for x in pr.split(":")]
            for i in range(a, min(b, len(ops))):
                op = ops[i]
                print("W", i, op.eng, "rank", op.rank if (op.mark or op.dsem) else "-", "deps", [(d, ops[d].eng, ops[d].rank) for d in op.deps],
                      "waits", op.waits, "r", op.reads, "w", op.writes)

    def emit(self, nc):
        per_eng = {e: [] for e in self.ENGS}
        for op in self.ops:
            per_eng[op.eng].append(op)
        EP = 16384
        with contextlib.ExitStack() as st:
            esem = {e: [st.enter_context(nc.semaphore("es_%s_%d" % (e, j)))
                        for j in range((self.counts[e] + EP - 1) // EP + 1)] for e in self.ENGS}
            dsem = {k: st.enter_context(nc.semaphore("ds_%d" % i)) for i, k in enumerate(self.dcounts)}
            block = st.enter_context(nc.Block())

            def sem_of(sk, v):
                if sk[0] == "d":
                    return dsem[sk[1]], v
                return esem[sk[1]][(v - 1) // EP], (v - 1) % EP + 1

            def run(engname, eng):
                for op in per_eng[engname]:
                    for sk, v in op.waits.items():
                        eng.wait_ge(*sem_of(sk, v))
                    ins = op.fn(eng)
                    if op.dsem is not None:
                        ins.then_inc(dsem[op.dsem], 16)
                    elif op.mark:
                        ins.then_inc(esem[op.eng][(op.rank - 1) // EP], 1)
                if engname == "sp":
                    for k, v in self.dcounts.items():
                        eng.wait_ge(dsem[k], v)

            @block.tensor
            def _(e):
                run("pe", e)

            @block.scalar
            def _(e):
                run("act", e)

            @block.vector
            def _(e):
                run("dve", e)

            @block.gpsimd
            def _(e):
                run("pool", e)

            @block.sync
            def _(e):
                run("sp", e)


def build_nc(stage=9, ntiles_p1=None):
    nc = bass.Bass("TRN2", target_bir_lowering=False)

    def din(name, shape):
        return nc.dram_tensor(name, list(shape), F32, kind="ExternalInput").ap()

    def dout(name, shape):
        return nc.dram_tensor(name, list(shape), F32, kind="ExternalOutput").ap()

    xc_d = din("xc", [NTILES * 128, D])
    xs_d = din("xs", [16, D])
    cak_d = din("cak", [128, 128])
    cav_d = din("cav", [128, 128])
    cbk_d = din("cbk", [512, 512])
    cbv_d = din("cbv", [512, 512])
    sconv_d = din("sconv", [2, DFF])
    win_d = din("w_in", [D, INW])
    woa_d = din("w_oa", [512, D])
    wob_d = din("w_ob", [512, D])
    wout_d = din("w_out", [D, D])
    sink_d = din("sink", [1, 8])
    gpm_d = din("g_pre_mix", [1, D])
    gqm_d = din("g_post_mix", [1, D])
    gpf_d = din("g_pre_ffn", [1, D])
    gqf_d = din("g_post_ffn", [1, D])
    wup_d = din("w_upg", [D, 2 * DFF])
    cw_d = din("conv_w", [3, DFF])
    cb_d = din("conv_b", [1, DFF])
    wdn_d = din("w_down", [DFF, D])
    ba_d = din("biasA", [128, 2 * 8 * 128])
    bb_d = din("biasB", [128, 5 * 8 * 128])
    bas_d = din("biasAs", [128, 2 * 8 * 16])
    bbs_d = din("biasBs", [128, 5 * 8 * 16])
    ident_d = din("ident", [128, 128])
    hv_d = din("hv", [128, 1])

    yp_d = dout("yp", [HALF, D])
    ys_d = dout("ys", [16, D])
    oak_d = dout("oak", [128, 128])
    oav_d = dout("oav", [128, 128])
    obk_d = dout("obk", [512, 512])
    obv_d = dout("obv", [512, 512])
    ocv_d = dout("ocv", [2, DFF])
    sak_d = dout("sak", [16, 128])
    sav_d = dout("sav", [16, 128])
    sbk_d = dout("sbk", [16, 512])
    sbv_d = dout("sbv", [16, 512])
    scv_d = dout("scv", [2, DFF])

    import os as _os
    if _os.environ.get("KDBG"):
        x1s_d = nc.dram_tensor("x1s", [(NTILES + 1) * 128, D], F32, kind="ExternalOutput").ap()
    else:
        x1s_d = nc.dram_tensor("x1s", [(NTILES + 1) * 128, D], F32).ap()

    P = Prog()
    P.group_final.add("setup")
    KDBG = bool(_os.environ.get("KDBG"))

    def dbg_dump(name, ap, shape, dt, reads):
        if not KDBG:
            return
        dd = nc.dram_tensor("dbg_" + name, list(shape), dt, kind="ExternalOutput").ap()
        P.add("sp", lambda e: e.dma_start(out=dd, in_=ap), reads=reads, dsem="dbg")

    with contextlib.ExitStack() as st:
        def sb(name, shape, dt):
            return st.enter_context(nc.sbuf_tensor("s_" + name, list(shape), dt))

        st.enter_context(nc.allow_non_contiguous_dma(reason="small constant / transposing loads"))

        WAR = sb("warena", [128, 73728], BF16)
        PAR = sb("parena", [128, 13440], BF16)
        FAR = sb("farena", [128, 1040], F32)
        xin = [sb("xin%d" % i, [128, D], F32) for i in range(4)]
        xn = [sb("xn%d" % i, [128, D], BF16) for i in range(2)]
        tmp = sb("tmp", [128, 1280], F32)
        gbc = sb("gbc", [128, D], F32)
        identf = sb("identf", [128, 128], F32)
        ident = sb("ident", [128, 128], BF16)
        gcol = sb("gcol", [128, 16], F32)
        cw = sb("cw", [128, 3 * 24], F32)
        cbias = sb("cbias", [128, 24], F32)
        hv = sb("hv", [128, 1], F32)
        hv10 = sb("hv10", [128, 10], BF16)
        epst = sb("epst", [128, 1], F32)
        sinkt = sb("sinkt", [128, 8], F32)
        exps = sb("exps", [128, 8], F32)
        ss = sb("ss", [128, 16], F32)
        sd = sb("sd", [128, 16], F32)
        rstd = sb("rstd", [128, 16], F32)
        junk = sb("junk", [128, 512], BF16)
        den = sb("den", [128, 8], F32)
        rden = sb("rden", [128, 8], F32)
        hist = sb("hist", [128, 48], F32)
        hists = sb("hists", [128, 48], F32)
        bars = sb("bars", [128, 8], F32)

        Win = WAR[:, 0:34816].rearrange("p (k c) -> p k c", k=8)
        Woa = WAR[:, 34816:38912].rearrange("p (k c) -> p k c", k=4)
        Wob = WAR[:, 38912:43008].rearrange("p (k c) -> p k c", k=4)
        Wout = WAR[:, 43008:51200].rearrange("p (k c) -> p k c", k=8)
        o = 51200
        BA = WAR[:, o:o + 4096].bitcast(F32).rearrange("p (t h q) -> p t h q", t=2, h=8)
        o += 4096
        BB = WAR[:, o:o + 10240].bitcast(F32).rearrange("p (t h q) -> p t h q", t=5, h=8)
        o += 10240
        BAs = WAR[:, o:o + 512].bitcast(F32).rearrange("p (t h q) -> p t h q", t=2, h=8)
        o += 512
        BBs = WAR[:, o:o + 1280].bitcast(F32).rearrange("p (t h q) -> p t h q", t=5, h=8)
        o += 1280
        KTa = WAR[:, o:o + RT * 128]
        o += RT * 128
        KTb = WAR[:, o:o + 4 * RT * 128].rearrange("p (c k) -> p c k", c=4)
        o += 4 * RT * 128
        assert o <= 73728
        Wup = WAR[:, 0:49152].rearrange("p (k c) -> p k c", k=8)
        Wdn = WAR[:, 49152:73728].rearrange("p (k c) -> p k c", k=24)

        hT = [PAR[:, i * 1024:(i + 1) * 1024].rearrange("p (k t) -> p k t", k=8) for i in range(2)]
        QT = PAR[:, 2048:3072].rearrange("p (k t) -> p k t", k=8)
        OT = PAR[:, 3072:4096].rearrange("p (k t) -> p k t", k=8)
        mT = PAR[:, 4096:5120].rearrange("p (k t) -> p k t", k=8)
        Vr = PAR[:, 5120:5120 + RT * 650].rearrange("p (s h d) -> p s h d", s=RT, h=10)
        o = 5120 + RT * 650
        pT = [PAR[:, o + i * 512:o + (i + 1) * 512] for i in range(4)]
        o += 2048
        Otok = PAR[:, o:o + 1024]
        o += 1024
        assert o <= 13440
        gtmp = [FAR[:, i * 256:(i + 1) * 256] for i in range(2)]
        h2T = [PAR[:, i * 2048:(i + 1) * 2048].rearrange("p (k t) -> p k t", k=8) for i in range(2)]
        actT = PAR[:, 4096:4096 + 24 * 256].rearrange("p (k t) -> p k t", k=24)
        ubuf = [FAR[:, i * 258:(i + 1) * 258] for i in range(2)]
        cbuf = [FAR[:, 516 + i * 256:516 + (i + 1) * 256] for i in range(2)]

        banks = [st.enter_context(nc.psum_tensor("bank%d" % i, [128, 512], F32)) for i in range(8)]

        def bk(b):
            return banks[b][:, :]

        def bkb(b):
            return banks[b][:, :].bitcast(BF16)

        def BK(b):
            return ("bank", b)

        cnt = {"xin": 0, "cast": 0, "ev": 0}

        def evac_eng():
            cnt["ev"] += 1
            return "act" if cnt["ev"] % 2 else "dve"

        def copy_op(eng, out, in_, scale=None):
            if eng == "act":
                if scale is None:
                    return lambda e: e.activation(out=out, in_=in_, func=AF.Copy)
                return lambda e: e.activation(out=out, in_=in_, func=AF.Copy, scale=scale)
            if scale is None:
                return lambda e: e.tensor_copy(out=out, in_=in_)
            return lambda e: e.tensor_scalar(out=out, in0=in_, scalar1=scale, scalar2=None, op0=ALU.mult)

        def wkeys(name, kc, c0, c1):
            return [(name, kc, c) for c in range(c0 // 1024, (c1 - 1) // 1024 + 1)]

        import os
        SKIP = os.environ.get("KSKIP", "")

        def setup_dma(out, in_, key):
            k0 = key[0] if isinstance(key, tuple) else key
            if k0 in SKIP.split(","):
                return
            P.add("sp", lambda e: e.dma_start(out=out, in_=in_), writes=[key], dsem="setup")

        setup_dma(identf[:, :], ident_d, "identf")
        setup_dma(hv[:, :], hv_d, "hv")
        setup_dma(sinkt[:, :], sink_d.partition_broadcast(128), "sinkt")
        setup_dma(gcol[:, 0:8], gpm_d.rearrange("o (k p) -> p (o k)", p=128), "gcol")
        setup_dma(gcol[:, 8:16], gpf_d.rearrange("o (k p) -> p (o k)", p=128), "gcol")
        for tap in range(3):
            setup_dma(cw[:, tap * 24:(tap + 1) * 24], cw_d[tap:tap + 1, :].rearrange("o (c p) -> p (o c)", p=128), "cw")
        setup_dma(cbias[:, :], cb_d.rearrange("o (c p) -> p (o c)", p=128), "cbias")
        setup_dma(BA.rearrange("p t h q -> p (t h q)"), ba_d, "BA")
        setup_dma(BB.rearrange("p t h q -> p (t h q)"), bb_d, "BB")
        setup_dma(BAs.rearrange("p t h q -> p (t h q)"), bas_d, "BAs")
        setup_dma(BBs.rearrange("p t h q -> p (t h q)"), bbs_d, "BBs")
        setup_dma(gbc[:, :], gqm_d.partition_broadcast(128), "gbc")
        for c in range(24):
            setup_dma(hists[:, 2 * c:2 * c + 2], sconv_d[:, c * 128:(c + 1) * 128].rearrange("t p -> p t"), ("hists", c))

        P.add("dve", lambda e: e.tensor_copy(out=ident[:, :], in_=identf[:, :]), reads=["identf"], writes=["ident"])
        P.add("dve", lambda e: e.memset(epst[:, :], EPS), writes=["epst"])
        P.add("dve", lambda e: e.memset(hist[:, :], 0.0), writes=[("hist", j) for j in range(24)])
        P.add("dve", lambda e: e.memset(den[:, 0:8], 1.0), writes=["den0"])
        P.add("dve", lambda e: e.tensor_scalar(out=rden[:, 0:8], in0=den[:, 0:8], scalar1=hv[:, 0:1], scalar2=None,
                                                op0=ALU.mult), reads=["den0", "hv"], writes=["rden0"])
        P.add("dve", lambda e: e.tensor_copy(out=hv10[:, 0:8], in_=rden[:, 0:8]), reads=["rden0"], writes=["hv10a"])
        P.add("dve", lambda e: e.tensor_copy(out=hv10[:, 8:10], in_=rden[:, 0:2]), reads=["rden0"], writes=["hv10"])
        P.add("act", lambda e: e.activation(out=exps[:, :], in_=sinkt[:, :], func=AF.Exp),
              reads=["sinkt"], writes=["exps"])

        def prep_weight(name, dram, nk, ncols, dest, scale_col=None, scale_const=None, extra_reads=()):
            for kc in range(nk):
                for c0 in range(0, ncols, 1024):
                    w = min(1024, ncols - c0)
                    sl = cnt["xin"] % 4
                    cnt["xin"] += 1
                    stg = xin[sl]
                    P.add("sp", lambda e, stg=stg, kc=kc, c0=c0, w=w: e.dma_start(
                        out=stg[:, 0:w], in_=dram[kc * 128:(kc + 1) * 128, c0:c0 + w]),
                        writes=[("xin", sl)], dsem=("xin", sl))
                    cnt["cast"] += 1
                    eng = "act" if cnt["cast"] % 2 else "dve"
                    if scale_col is not None:
                        sc = gcol[:, scale_col + kc:scale_col + kc + 1]
                        rd = [("xin", sl), "gcol"]
                    else:
                        sc = scale_const
                        rd = [("xin", sl)]
                    rd = rd + list(extra_reads)
                    key = (name, kc, c0 // 1024)
                    if name == "Win" and c0 == 0:
                        dq = dest[:, kc, 0:512].rearrange("p (i two d) -> p i two d", two=2, d=64)
                        for two in range(2):
                            P.add(eng, copy_op(eng, dq[:, :, two, :],
                                               stg[:, two * 256:(two + 1) * 256].rearrange("p (i d) -> p i d", d=64), sc),
                                  reads=rd, writes=[key])
                        P.add(eng, copy_op(eng, dest[:, kc, 512:1024], stg[:, 512:1024], sc), reads=rd, writes=[key])
                    else:
                        P.add(eng, copy_op(eng, dest[:, kc, c0:c0 + w], stg[:, 0:w], sc), reads=rd, writes=[key])

        if stage >= 0:
            prep_weight("Win", win_d, 8, INW, Win, scale_col=0)
            prep_weight("Woa", woa_d, 4, D, Woa)
            prep_weight("Wob", wob_d, 4, D, Wob)
            prep_weight("Wout", wout_d, 8, D, Wout, scale_const=0.5)

        def norm_a(x_ap_tile, sl, ntok, xnb, xnkey, col=0):
            c1 = slice(col, col + 1)
            P.add("act", lambda e: e.activation(out=xnb[0:ntok, :], in_=x_ap_tile[0:ntok, :], func=AF.Square,
                                                accum_out=ss[0:ntok, c1]),
                  reads=[("xin", sl)], writes=[xnkey, ("ss", col)])
            P.add("act", lambda e: e.activation(out=sd[0:ntok, c1], in_=ss[0:ntok, c1], func=AF.Sqrt,
                                                scale=1.0 / D, bias=epst[0:ntok, :]),
                  reads=[("ss", col), "epst"], writes=[("sd", col)])
            P.add("dve", lambda e: e.reciprocal(out=rstd[0:ntok, c1], in_=sd[0:ntok, c1]),
                  reads=[("sd", col)], writes=[("rstd", col)])
            P.add("act", lambda e: e.activation(out=xnb[0:ntok, :], in_=x_ap_tile[0:ntok, :], func=AF.Copy,
                                                scale=rstd[0:ntok, c1]),
                  reads=[("xin", sl), ("rstd", col)], writes=[xnkey])

        def norm_b(ntok, hTd, hkey, xnb, xnkey, b=0, eng=None):
            for kc in range(8):
                P.add("pe", lambda e, kc=kc, b=b: e.transpose(out=bkb(b)[:, kc * 128:kc * 128 + ntok],
                                                         in_=xnb[0:ntok, kc * 128:(kc + 1) * 128],
                                                         identity=ident[0:ntok, 0:ntok]),
                      reads=[xnkey, "ident"], writes=[BK(b)])
            eng = eng or evac_eng()
            P.add(eng, copy_op(eng, hTd[:, :, 0:ntok],
                               bkb(b).rearrange("p (k t) -> p k t", k=8)[:, :, 0:ntok]),
                  reads=[BK(b)], writes=[hkey])

        def post_norm_residual(bankpair, sl, ntok, gkey, out_ap, out_key, cb=4):
            b0, b1 = bankpair
            for i, b in enumerate((b0, b1)):
                P.add("act", lambda e, b=b, i=i: e.activation(out=junk[0:ntok, :],
                                                              in_=bk(b)[0:ntok, :], func=AF.Square,
                                                              accum_out=ss[0:ntok, cb + i:cb + i + 1]),
                      reads=[BK(b)], writes=["junk", ("ss", cb + i)])
            P.add("dve", lambda e: e.tensor_tensor(out=ss[0:ntok, cb + 2:cb + 3], in0=ss[0:ntok, cb:cb + 1],
                                                   in1=ss[0:ntok, cb + 1:cb + 2], op=ALU.add),
                  reads=[("ss", cb), ("ss", cb + 1)], writes=[("ss", cb + 2)])
            P.add("act", lambda e: e.activation(out=sd[0:ntok, cb:cb + 1], in_=ss[0:ntok, cb + 2:cb + 3], func=AF.Sqrt,
                                                scale=1.0 / D, bias=epst[0:ntok, :]),
                  reads=[("ss", cb + 2), "epst"], writes=[("sd", cb)])
            P.add("dve", lambda e: e.reciprocal(out=rstd[0:ntok, cb:cb + 1], in_=sd[0:ntok, cb:cb + 1]),
                  reads=[("sd", cb)], writes=[("rstd", cb)])
            for i, b in enumerate((b0, b1)):
                P.add("dve", lambda e, b=b, i=i: e.scalar_tensor_tensor(
                    out=tmp[0:ntok, i * 512:(i + 1) * 512], in0=bk(b)[0:ntok, :], scalar=rstd[0:ntok, cb:cb + 1],
                    in1=gbc[0:ntok, i * 512:(i + 1) * 512], op0=ALU.mult, op1=ALU.mult),
                    reads=[BK(b), ("rstd", cb), gkey], writes=["tmp"])
            P.add("pool", lambda e: e.tensor_tensor(out=out_ap[0:ntok, 0:D], in0=tmp[0:ntok, 0:D],
                                                    in1=xin[sl][0:ntok, :], op=ALU.add),
                  reads=["tmp", ("xin", sl)], writes=[out_key])

        pj = {"i": 0}

        def next_pj():
            pj["i"] += 1
            return 3 + pj["i"] % 5

        def p1_front(T):
            t, kind, ntok, samp, slot = T["t"], T["kind"], T["ntok"], T["samp"], T["slot"]
            sl = cnt["xin"] % 4
            cnt["xin"] += 1
            T["sl"] = sl
            hi = T["hi"]
            src = xs_d if samp else xc_d[t * 128:(t + 1) * 128, :]
            P.add("sp", lambda e: e.dma_start(out=xin[sl][0:ntok, :], in_=src),
                  writes=[("xin", sl)], dsem=("xin", sl))

        def p1_front_a(T):
            norm_a(xin[T["sl"]], T["sl"], T["ntok"], xn[T["hi"]], ("xn", T["hi"]))

        def p1_front_b(T):
            ntok, hi = T["ntok"], T["hi"]
            norm_b(ntok, hT[hi], ("hT", hi), xn[hi], ("xn", hi), b=2, eng="dve")

        def fm_chunk(T, col0, dest, dkey, scale):
            ntok, hi = T["ntok"], T["hi"]
            b = next_pj()
            for kc in range(8):
                P.add("pe", lambda e, kc=kc: e.matmul(bk(b)[:, 0:ntok], lhsT=Win[:, kc, col0:col0 + 128],
                                                      rhs=hT[hi][:, kc, 0:ntok], start=(kc == 0), stop=(kc == 7)),
                      reads=[("hT", hi)] + wkeys("Win", kc, col0, col0 + 128), writes=[BK(b)])
            eng = evac_eng()
            P.add(eng, copy_op(eng, dest, bk(b)[:, 0:ntok], scale), reads=[BK(b)], writes=[dkey])

        def p1_proj(T):
            t, kind, ntok, samp, slot, hi = T["t"], T["kind"], T["ntok"], T["samp"], T["slot"], T["hi"]
            ks = slice(slot * 128, slot * 128 + ntok)
            if kind == "full":
                for ci in range(4):
                    fm_chunk(T, C_QA + ci * 128, QT[:, ci, 0:ntok], ("QT", ci), 0.125)
                for ci in range(4):
                    fm_chunk(T, C_QB + ci * 128, QT[:, 4 + ci, 0:ntok], ("QT", 4 + ci), 0.125)
            fm_chunk(T, C_KA, KTa[:, ks], ("KTa", slot), None)
            for ci in range(4):
                fm_chunk(T, C_KB + ci * 128, KTb[:, ci, ks], ("KTb", slot, ci), None)
            kvout = T["kvout"]
            if not kvout:
                b = next_pj()
                for kc in range(8):
                    P.add("pe", lambda e, kc=kc, b=b: e.matmul(bk(b)[0:ntok, 0:128], lhsT=hT[hi][:, kc, 0:ntok],
                                                          rhs=Win[:, kc, C_VA:C_VA + 128], start=(kc == 0),
                                                          stop=(kc == 7)),
                          reads=[("hT", hi)] + wkeys("Win", kc, C_VA, C_VA + 128), writes=[BK(b)])
                P.add("act", copy_op("act", Vr[0:ntok, slot, 0:2, 0:64],
                                     bk(b)[0:ntok, 0:128].rearrange("p (h d) -> p h d", h=2)),
                      reads=[BK(b)], writes=[("Va", slot)])
                b = next_pj()
                for kc in range(8):
                    P.add("pe", lambda e, kc=kc, b=b: e.matmul(bk(b)[0:ntok, 0:512], lhsT=hT[hi][:, kc, 0:ntok],
                                                          rhs=Win[:, kc, C_VB:C_VB + 512], start=(kc == 0),
                                                          stop=(kc == 7)),
                          reads=[("hT", hi)] + wkeys("Win", kc, C_VB, C_VB + 512), writes=[BK(b)])
                P.add("dve", copy_op("dve", Vr[0:ntok, slot, 2:10, 0:64],
                                     bk(b)[0:ntok, 0:512].rearrange("p (h d) -> p h d", h=8)),
                      reads=[BK(b)], writes=[("Vb", slot)])
            else:
                segs = [(3, C_KA, 256, 0), (4, C_KB, 512, 256), (3, C_VB, 512, 768)]
                for (b, c0, w, to) in segs:
                    for kc in range(8):
                        P.add("pe", lambda e, kc=kc, b=b, c0=c0, w=w: e.matmul(
                            bk(b)[0:ntok, 0:w], lhsT=hT[hi][:, kc, 0:ntok], rhs=Win[:, kc, c0:c0 + w],
                            start=(kc == 0), stop=(kc == 7)),
                            reads=[("hT", hi)] + wkeys("Win", kc, c0, c0 + w), writes=[BK(b)])
                    eng = evac_eng()
                    P.add(eng, copy_op(eng, tmp[0:ntok, to:to + w], bk(b)[0:ntok, 0:w]),
                          reads=[BK(b)], writes=["tmp"])
                P.add("act", copy_op("act", Vr[0:ntok, slot, 0:2, 0:64],
                                     tmp[0:ntok, 128:256].rearrange("p (h d) -> p h d", h=2)),
                      reads=["tmp"], writes=[("Va", slot)])
                P.add("dve", copy_op("dve", Vr[0:ntok, slot, 2:10, 0:64],
                                     tmp[0:ntok, 768:1280].rearrange("p (h d) -> p h d", h=8)),
                      reads=["tmp"], writes=[("Vb", slot)])
                for (dst, c0, w) in T["kvdst"]:
                    P.add("sp", lambda e, dst=dst, c0=c0, w=w: e.dma_start(out=dst, in_=tmp[0:ntok, c0:c0 + w]),
                          reads=["tmp"], dsem="kvst")
            if (not samp) and t <= T_HALO:
                P.add("pool", lambda e: e.tensor_copy(out=Vr[0:ntok, slot, :, 64], in_=hv10[0:ntok, :]),
                      reads=["hv10"], writes=[("V1", slot)])
            else:
                P.add("pool", lambda e: e.memset(Vr[0:ntok, slot, :, 64], 1.0), writes=[("V1", slot)])

        def attention(T):
            ntok = T["ntok"]
            a_tiles, b_tiles, TA, TB, ka, kb_ = T["a_tiles"], T["b_tiles"], T["TA"], T["TB"], T["TAk"], T["TBk"]
            units = []
            for g in range(2):
                for i, (slot, nk, bi) in enumerate(a_tiles):
                    units.append(dict(kind="A", g=g, slot=slot, nk=nk, bi=bi, first=(i == 0),
                                      last=(i == len(a_tiles) - 1)))
            for g in range(2):
                for i, (slot, nk, bi) in enumerate(b_tiles):
                    units.append(dict(kind="B", g=g, slot=slot, nk=nk, bi=bi, first=(i == 0),
                                      last=(i == len(b_tiles) - 1)))
            sbanks = [2, 3, 4, 5]
            W4 = 4 * ntok

            def emit_qk(ui, u):
                b = sbanks[ui % 4]
                u["b"] = b
                u["p"] = ui % 4
                slot, nk, g = u["slot"], u["nk"], u["g"]
                if u["kind"] == "A":
                    P.add("pe", lambda e: e.matmul(
                        bk(b)[0:nk, 0:W4].rearrange("p (h q) -> p h q", h=4),
                        lhsT=KTa[g * 64:(g + 1) * 64, slot * 128:slot * 128 + nk],
                        rhs=QT[g * 64:(g + 1) * 64, 0:4, 0:ntok], start=True, stop=True),
                        reads=[("KTa", slot)] + [("QT", c) for c in range(4)], writes=[BK(b)])
                else:
                    for hh in range(4):
                        pb = g * 64
                        P.add("pe", lambda e, hh=hh, pb=pb: e.matmul(
                            bk(b)[0:nk, hh * ntok:(hh + 1) * ntok],
                            lhsT=KTb[pb:pb + 64, hh, slot * 128:slot * 128 + nk],
                            rhs=QT[pb:pb + 64, 4 + hh, 0:ntok], start=True, stop=True),
                            reads=[("KTb", slot, hh), ("QT", 4 + hh)], writes=[BK(b)])
                tab, tkey = (TA, ka) if u["kind"] == "A" else (TB, kb_)
                bi = u["bi"]
                P.add("dve", lambda e: e.tensor_tensor(
                    out=bk(b)[0:nk, 0:W4].rearrange("p (h q) -> p h q", h=4),
                    in0=bk(b)[0:nk, 0:W4].rearrange("p (h q) -> p h q", h=4),
                    in1=tab[0:nk, bi, 4 * g:4 * g + 4, :], op=ALU.add),
                    reads=[BK(b), tkey], writes=[BK(b)])
                P.add("act", lambda e: e.activation(out=pT[u["p"]][0:nk, 0:W4], in_=bk(b)[0:nk, 0:W4], func=AF.Exp),
                      reads=[BK(b)], writes=[("pT", u["p"])])

            def emit_pv(u):
                slot, nk, g = u["slot"], u["nk"], u["g"]
                ob = 6 + g
                for hh in range(4):
                    vh = g if u["kind"] == "A" else 2 + 2 * hh + g
                    P.add("pe", lambda e, hh=hh, vh=vh: e.matmul(
                        bk(ob)[0:ntok, hh * 65:(hh + 1) * 65], lhsT=pT[u["p"]][0:nk, hh * ntok:(hh + 1) * ntok],
                        rhs=Vr[0:nk, slot, vh, 0:65], start=(u["first"] and hh == 0), stop=(u["last"] and hh == 3)),
                        reads=[("pT", u["p"]), ("Va", slot) if u["kind"] == "A" else ("Vb", slot), ("V1", slot)],
                        writes=[BK(ob)])
                if u["last"]:
                    o3 = bk(ob)[0:ntok, 0:260].rearrange("p (h d) -> p h d", h=4)
                    dsl = slice(4 * g, 4 * g + 4)
                    if u["kind"] == "A":
                        P.add("dve", lambda e: e.tensor_tensor(out=den[0:ntok, dsl], in0=o3[:, :, 64],
                                                               in1=exps[0:ntok, dsl], op=ALU.add),
                              reads=[BK(ob), "exps"], writes=[("den", g)])
                    else:
                        P.add("dve", lambda e: e.tensor_scalar(out=den[0:ntok, dsl], in0=o3[:, :, 64],
                                                               scalar1=1e-30, scalar2=None, op0=ALU.add),
                              reads=[BK(ob)], writes=[("den", g)])
                    P.add("dve", lambda e: e.reciprocal(out=rden[0:ntok, dsl], in_=den[0:ntok, dsl]),
                          reads=[("den", g)], writes=[("rden", g)])
                    for hh in range(4):
                        c = (4 * g + hh) * 64 if u["kind"] == "A" else 512 + (2 * hh + g) * 64
                        P.add("dve", lambda e, hh=hh, c=c: e.tensor_scalar(
                            out=Otok[0:ntok, c:c + 64], in0=bk(ob)[0:ntok, hh * 65:hh * 65 + 64],
                            scalar1=rden[0:ntok, 4 * g + hh:4 * g + hh + 1], scalar2=None, op0=ALU.mult),
                            reads=[BK(ob), ("rden", g)], writes=[("Otok", c // 128)])

            L = 3
            for i in range(len(units) + L):
                if i < len(units):
                    emit_qk(i, units[i])
                if i - L >= 0:
                    emit_pv(units[i - L])

        def p1_back(T, mid=None, early=None, late=None):
            t, kind, ntok, samp, slot, hi, sl = T["t"], T["kind"], T["ntok"], T["samp"], T["slot"], T["hi"], T["sl"]
            if early is not None:
                early()
            if kind != "full":
                if mid is not None:
                    mid()
                if late is not None:
                    late()
                return None
            attention(T)
            if mid is not None:
                mid()
            if late is not None:
                late()
            if samp:
                dbg_dump("QT", QT[:, :, 0:16], [128, 8, 16], BF16, [("QT", c) for c in range(8)])
                dbg_dump("hT", hT[hi][:, :, 0:16], [128, 8, 16], BF16, [("hT", hi)])
                dbg_dump("Otok", Otok[0:16, :], [16, 1024], BF16, [("Otok", c) for c in range(8)])
                dbg_dump("den", den[0:16, :], [16, 8], F32, [("den", 0), ("den", 1)])
                dbg_dump("KTb", KTb[:, :, 0:640], [128, 4, 640], BF16, [("KTb", s_, c) for s_ in range(5) for c in range(4)])
                dbg_dump("Vr", Vr[:, 0:5, :, :], [128, 5, 10, 65], BF16, [("Vb", s_) for s_ in range(5)] + [("Va", 3), ("Va", 4)] + [("V1", s_) for s_ in range(5)])
            b = 7
            for kc in range(8):
                P.add("pe", lambda e, kc=kc, b=b: e.transpose(out=bkb(b)[:, kc * 128:kc * 128 + ntok],
                                                         in_=Otok[0:ntok, kc * 128:(kc + 1) * 128],
                                                         identity=ident[0:ntok, 0:ntok]),
                      reads=[("Otok", kc), "ident"], writes=[BK(b)])
            eng = "dve"
            P.add(eng, copy_op(eng, OT[:, :, 0:ntok], bkb(b).rearrange("p (k t) -> p k t", k=8)[:, :, 0:ntok]),
                  reads=[BK(b)], writes=["OT"])
            for fc in range(8):
                bs = [0, 1, 2, 3] if fc % 2 == 0 else [4, 5, 6, 7]
                gt = gtmp[fc % 2]
                for (b, c0) in ((bs[0], C_GA + fc * 128), (bs[1], C_GB + fc * 128)):
                    for kc in range(8):
                        P.add("pe", lambda e, kc=kc, b=b, c0=c0: e.matmul(
                            bk(b)[:, 0:ntok], lhsT=Win[:, kc, c0:c0 + 128], rhs=hT[hi][:, kc, 0:ntok],
                            start=(kc == 0), stop=(kc == 7)),
                            reads=[("hT", hi)] + wkeys("Win", kc, c0, c0 + 128), writes=[BK(b)])
                for (b, W, wn, off) in ((bs[2], Woa, "Woa", 0), (bs[3], Wob, "Wob", 4)):
                    for kc in range(4):
                        P.add("pe", lambda e, kc=kc, b=b, W=W, off=off, fc=fc: e.matmul(
                            bk(b)[:, 0:ntok], lhsT=W[:, kc, fc * 128:(fc + 1) * 128], rhs=OT[:, off + kc, 0:ntok],
                            start=(kc == 0), stop=(kc == 3)),
                            reads=["OT", (wn, kc, 0)], writes=[BK(b)])
                for i in range(2):
                    P.add("act", lambda e, i=i, gt=gt, bs=bs: e.activation(out=gt[:, i * 128:i * 128 + ntok],
                                                             in_=bk(bs[i])[:, 0:ntok], func=AF.Tanh, scale=0.5),
                          reads=[BK(bs[i])], writes=[("gt", fc % 2, i)])
                for i in range(2):
                    P.add("dve", lambda e, i=i, gt=gt, bs=bs: e.scalar_tensor_tensor(
                        out=gt[:, i * 128:i * 128 + ntok], in0=gt[:, i * 128:i * 128 + ntok], scalar=1.0,
                        in1=bk(bs[2 + i])[:, 0:ntok], op0=ALU.add, op1=ALU.mult),
                        reads=[("gt", fc % 2, i), BK(bs[2 + i])], writes=[("gt", fc % 2, i)])
                P.add("pool", lambda e, fc=fc, gt=gt: e.tensor_tensor(out=mT[:, fc, 0:ntok], in0=gt[:, 0:ntok],
                                                        in1=gt[:, 128:128 + ntok], op=ALU.add),
                      reads=[("gt", fc % 2, 0), ("gt", fc % 2, 1)], writes=[("mT", fc)])
            for half in range(2):
                b = half
                for kc in range(8):
                    P.add("pe", lambda e, kc=kc, b=b, half=half: e.matmul(
                        bk(b)[0:ntok, :], lhsT=mT[:, kc, 0:ntok], rhs=Wout[:, kc, half * 512:(half + 1) * 512],
                        start=(kc == 0), stop=(kc == 7)),
                        reads=[("mT", kc), ("Wout", kc, 0)], writes=[BK(b)])
            if samp:
                dbg_dump("mT", mT[:, :, 0:16], [128, 8, 16], BF16, [("mT", c) for c in range(8)])
                dbg_dump("OT", OT[:, :, 0:16], [128, 8, 16], BF16, ["OT"])
            def fin():
                post_norm_residual((0, 1), sl, ntok, "gbc", xin[sl], ("xin", sl))
                row = T["x1row"]
                P.add("sp", lambda e: e.dma_start(out=x1s_d[row:row + ntok, :], in_=xin[sl][0:ntok, :]),
                      reads=[("xin", sl)], writes=[("x1s", row)], dsem=("xst", sl))
            return fin

        def load_caches():
            for kt in range(4):
                sl = cnt["xin"] % 4
                cnt["xin"] += 1
                P.add("sp", lambda e, kt=kt, sl=sl: e.dma_start(out=xin[sl][:, 0:512],
                                                                in_=cbk_d[kt * 128:(kt + 1) * 128, :]),
                      writes=[("xin", sl)], dsem=("xin", sl))
                P.add("sp", lambda e, kt=kt, sl=sl: e.dma_start(out=xin[sl][:, 512:1024],
                                                                in_=cbv_d[kt * 128:(kt + 1) * 128, :]),
                      writes=[("xin", sl)], dsem=("xin", sl))
                xb = xn[kt % 2]
                P.add("dve", copy_op("dve", xb[:, 0:512], xin[sl][:, 0:512]), reads=[("xin", sl)],
                      writes=[("xn", kt % 2)])
                P.add("act", copy_op("act", Vr[:, kt, 2:10, 0:64],
                                     xin[sl][:, 512:1024].rearrange("p (h d) -> p h d", h=8)),
                      reads=[("xin", sl)], writes=[("Vb", kt)])
                P.add("pool", lambda e, kt=kt: e.memset(Vr[:, kt, :, 64], 1.0), writes=[("V1", kt)])
                b = kt % 2
                for c in range(4):
                    P.add("pe", lambda e, c=c, b=b, xb=xb: e.transpose(out=bkb(b)[:, c * 128:(c + 1) * 128],
                                                                       in_=xb[:, c * 128:(c + 1) * 128],
                                                                       identity=ident[:, :]),
                          reads=[("xn", kt % 2), "ident"], writes=[BK(b)])
                P.add("dve", copy_op("dve", KTb[:, :, kt * 128:(kt + 1) * 128],
                                     bkb(b)[:, 0:512].rearrange("p (c k) -> p c k", c=4)),
                      reads=[BK(b)], writes=[("KTb", kt, c) for c in range(4)])
            sl = cnt["xin"] % 4
            cnt["xin"] += 1
            P.add("sp", lambda e: e.dma_start(out=xin[sl][:, 0:128], in_=cak_d), writes=[("xin", sl)],
                  dsem=("xin", sl))
            P.add("sp", lambda e: e.dma_start(out=xin[sl][:, 128:256], in_=cav_d), writes=[("xin", sl)],
                  dsem=("xin", sl))
            xb = xn[0]
            P.add("dve", copy_op("dve", xb[:, 0:128], xin[sl][:, 0:128]), reads=[("xin", sl)], writes=[("xn", 0)])
            P.add("act", copy_op("act", Vr[:, 3, 0:2, 0:64], xin[sl][:, 128:256].rearrange("p (h d) -> p h d", h=2)),
                  reads=[("xin", sl)], writes=[("Va", 3)])
            P.add("pe", lambda e: e.transpose(out=bkb(0)[:, 0:128], in_=xb[:, 0:128], identity=ident[:, :]),
                  reads=[("xn", 0), "ident"], writes=[BK(0)])
            P.add("dve", copy_op("dve", KTa[:, 3 * 128:4 * 128], bkb(0)[:, 0:128]), reads=[BK(0)],
                  writes=[("KTa", 3)])

        tiles = []
        samp = dict(t=-1, kind="full", ntok=16, samp=True, slot=SAMP_SLOT, kvout=True,
                    kvdst=[(sak_d, 0, 128), (sav_d, 128, 128), (sbk_d, 256, 512), (sbv_d, 768, 512)],
                    a_tiles=[(3, 128, 0), (SAMP_SLOT, 16, 1)],
                    b_tiles=[(0, 128, 0), (1, 128, 1), (2, 128, 2), (3, 128, 3), (SAMP_SLOT, 16, 4)],
                    TA=BAs, TB=BBs, TAk="BAs", TBk="BBs", x1row=NTILES * 128)
        tiles.append(samp)
        for t in range(NTILES):
            kind = "kv" if t < T_HALO_KV else "full"
            T = dict(t=t, kind=kind, ntok=128, samp=False, slot=t % RT, kvout=(t >= NTILES - 4), x1row=t * 128)
            if T["kvout"]:
                r = (t - (NTILES - 4)) * 128
                dst = [(obk_d[r:r + 128, :], 256, 512), (obv_d[r:r + 128, :], 768, 512)]
                if t == NTILES - 1:
                    dst += [(oak_d, 0, 128), (oav_d, 128, 128)]
                T["kvdst"] = dst
            if kind == "full":
                T["a_tiles"] = [((t - 1) % RT, 128, 0), (t % RT, 128, 1)]
                T["b_tiles"] = [((t - 4 + i) % RT, 128, i) for i in range(5)]
                T.update(TA=BA, TB=BB, TAk="BA", TBk="BB")
            tiles.append(T)
        for i, T in enumerate(tiles):
            T["hi"] = i % 2

        if ntiles_p1 is not None:
            tiles = tiles[:ntiles_p1]
        if stage >= 1:
            load_caches()
        if stage >= 2:
            def front_proj(Tn, between=None):
                p1_front_b(Tn)
                if between is not None:
                    between()
                p1_proj(Tn)

            p1_front(tiles[0])
            p1_front_a(tiles[0])
            front_proj(tiles[0])
            if len(tiles) > 1:
                p1_front(tiles[1])
                p1_front_a(tiles[1])
            pend1 = [None]
            for i, T in enumerate(tiles):
                def early(i=i):
                    if i + 2 < len(tiles):
                        p1_front(tiles[i + 2])

                def mid(i=i):
                    def flush():
                        if pend1[0] is not None:
                            pend1[0]()
                            pend1[0] = None
                    if i + 1 < len(tiles):
                        front_proj(tiles[i + 1], flush)
                    else:
                        flush()

                def late(i=i):
                    if i + 2 < len(tiles):
                        p1_front_a(tiles[i + 2])

                r = p1_back(T, mid, early, late)
                if r is not None:
                    pend1[0] = r
            if pend1[0] is not None:
                pend1[0]()
        if stage < 3:
            mo = int(os.environ.get("KMAXOPS", "0"))
            if mo:
                for i, op in enumerate(P.ops[:mo]):
                    print("OP", i, op.eng, op.reads, op.writes, op.dsem)
                P.ops = P.ops[:mo]
            P.finalize()
            P.emit(nc)
            return nc

        for op in reversed(P.ops):
            if op.eng == "pe":
                op.writes.append(("bar", "pe"))
                break
        for i, eng in enumerate(("act", "dve", "pool")):
            if eng == "act":
                P.add(eng, lambda e, i=i: e.activation(out=bars[:, i:i + 1], in_=epst[:, 0:1], func=AF.Copy),
                      writes=[("bar", eng)])
            else:
                P.add(eng, lambda e, i=i: e.memset(bars[:, i:i + 1], 0.0), writes=[("bar", eng)])
        BARS = [("bar", e) for e in ("pe", "act", "dve", "pool")]
        for i, eng in enumerate(("act", "dve", "pool")):
            if eng == "act":
                P.add(eng, lambda e, i=i: e.activation(out=bars[:, 4 + i:5 + i], in_=epst[:, 0:1], func=AF.Copy),
                      reads=BARS, writes=[("bar2", eng)])
            else:
                P.add(eng, lambda e, i=i: e.memset(bars[:, 4 + i:5 + i], 0.0), reads=BARS, writes=[("bar2", eng)])

        P.add("sp", lambda e: e.dma_start(out=gbc[:, :], in_=gqf_d.partition_broadcast(128)),
              reads=BARS, writes=["gbc"], dsem="gbc2")
        prep_weight("Wup", wup_d, 8, 2 * DFF, Wup, scale_col=8, extra_reads=BARS)
        prep_weight("Wdn", wdn_d, 24, D, Wdn, extra_reads=BARS)

        def p2_front(G):
            hi = G["hi"]
            for i, T in enumerate(G["tiles"]):
                ntok = T["ntok"]
                sl = cnt["xin"] % 4
                cnt["xin"] += 1
                T["sl"] = sl
                row = T["x1row"]
                P.add("sp", lambda e, sl=sl, row=row, ntok=ntok: e.dma_start(out=xin[sl][0:ntok, :],
                                                                            in_=x1s_d[row:row + ntok, :]),
                      reads=[("x1s", row)], writes=[("xin", sl)], dsem=("xin", sl))
                cnt["xn2"] = cnt.get("xn2", 0) + 1
                T["xi"] = cnt["xn2"] % 2

        def p2_front_a(G):
            for i, T in enumerate(G["tiles"]):
                xi = T["xi"]
                norm_a(xin[T["sl"]], T["sl"], T["ntok"], xn[xi], ("xn", xi), col=i)

        def p2_front_b(G):
            hi = G["hi"]
            for i, T in enumerate(G["tiles"]):
                xi = T["xi"]
                norm_b(T["ntok"], h2T[hi][:, :, i * 128:(i + 1) * 128], ("h2T", hi, i), xn[xi], ("xn", xi))

        GR = int(os.environ.get("KGR", "2"))

        def p2_back(G, mid=None, pre=None, early=None):
            tl = G["tiles"]
            pre = pre if pre is not None else []
            hi = G["hi"]
            samp = tl[0]["samp"]
            N = sum(T["ntok"] for T in tl)
            halo = (not samp) and tl[0]["t"] == T_HALO
            H = hists if samp else hist
            hk = "hists" if samp else "hist"
            hkeys = [("h2T", hi, i) for i in range(len(tl))]

            def tail(j):
                cbf = cbuf[j % 2]
                bg = 4 + j % GR
                P.add("act", lambda e: e.activation(out=cbf[:, 0:N], in_=cbf[:, 0:N], func=AF.Gelu_apprx_tanh),
                      reads=[("cb", j % 2)], writes=[("cb", j % 2)])
                P.add("dve", lambda e: e.tensor_tensor(out=actT[:, j, 0:N], in0=cbf[:, 0:N],
                                                       in1=bk(bg)[:, 0:N], op=ALU.mult),
                      reads=[("cb", j % 2), BK(bg)], writes=[("actT", j)])

            for j in range(24):
                ub, cbf = ubuf[j % 2], cbuf[j % 2]
                bu, bg = 2 + j % 2, 4 + j % GR
                if pre and j in (0, 1):
                    pre.pop(0)()
                if early is not None and j == 3:
                    early[0]()
                if early is not None and j == 13:
                    early[1]()
                for kc in range(8):
                    P.add("pe", lambda e, kc=kc, j=j, bu=bu: e.matmul(
                        bk(bu)[:, 0:N], lhsT=Wup[:, kc, j * 128:(j + 1) * 128], rhs=h2T[hi][:, kc, 0:N],
                        start=(kc == 0), stop=(kc == 7)),
                        reads=hkeys + wkeys("Wup", kc, j * 128, (j + 1) * 128), writes=[BK(bu)])
                if not halo:
                    for kc in range(8):
                        P.add("pe", lambda e, kc=kc, j=j, bg=bg: e.matmul(
                            bk(bg)[:, 0:N], lhsT=Wup[:, kc, DFF + j * 128:DFF + (j + 1) * 128],
                            rhs=h2T[hi][:, kc, 0:N], start=(kc == 0), stop=(kc == 7)),
                            reads=hkeys + wkeys("Wup", kc, DFF + j * 128, DFF + (j + 1) * 128),
                            writes=[BK(bg)])
                P.add("act", copy_op("act", ub[:, 2:2 + N], bk(bu)[:, 0:N]), reads=[BK(bu)],
                      writes=[("ub", j % 2)])
                P.add("pool", lambda e, j=j, ub=ub: e.tensor_copy(out=ub[:, 0:2], in_=H[:, 2 * j:2 * j + 2]),
                      reads=[(hk, j)], writes=[("ubh", j % 2)])
                if halo:
                    P.add("pool", lambda e, j=j, ub=ub: e.tensor_scalar(out=H[:, 2 * j:2 * j + 2],
                                                                        in0=ub[:, N:N + 2], scalar1=hv[:, 0:1],
                                                                        scalar2=None, op0=ALU.mult),
                          reads=[("ub", j % 2), ("ubh", j % 2), "hv"], writes=[(hk, j)])
                    continue
                P.add("pool", lambda e, j=j, ub=ub: e.tensor_copy(out=H[:, 2 * j:2 * j + 2], in_=ub[:, N:N + 2]),
                      reads=[("ub", j % 2), ("ubh", j % 2)], writes=[(hk, j)])
                if j >= 1:
                    tail(j - 1)
                P.add("act", lambda e, j=j, ub=ub, cbf=cbf: e.activation(
                    out=cbf[:, 0:N], in_=ub[:, 2:2 + N], func=AF.Identity,
                    scale=cw[:, 48 + j:49 + j], bias=cbias[:, j:j + 1]),
                    reads=[("ub", j % 2), "cw", "cbias"], writes=[("cb", j % 2)])
                for tap in (1, 0):
                    P.add("dve", lambda e, j=j, ub=ub, cbf=cbf, tap=tap: e.scalar_tensor_tensor(
                        out=cbf[:, 0:N], in0=ub[:, tap:tap + N], scalar=cw[:, tap * 24 + j:tap * 24 + j + 1],
                        in1=cbf[:, 0:N], op0=ALU.mult, op1=ALU.add),
                        reads=[("ub", j % 2), ("ubh", j % 2), ("cb", j % 2), "cw"], writes=[("cb", j % 2)])
            while pre:
                pre.pop(0)()
            if halo:
                if mid is not None:
                    mid()
                return []
            tail(23)
            if mid is not None:
                mid()
            deferred = []
            for i, T in enumerate(tl):
                ntok, sl, t = T["ntok"], T["sl"], T["t"]
                bp = (0, 1) if i == 0 else (6, 7)
                for half in range(2):
                    b = bp[half]
                    for j in range(24):
                        P.add("pe", lambda e, j=j, b=b, half=half, i=i, ntok=ntok: e.matmul(
                            bk(b)[0:ntok, :], lhsT=actT[:, j, i * 128:i * 128 + ntok],
                            rhs=Wdn[:, j, half * 512:(half + 1) * 512], start=(j == 0), stop=(j == 23)),
                            reads=[("actT", j), ("Wdn", j, 0)], writes=[BK(b)])
                def fin(bp=bp, sl=sl, ntok=ntok, i=i, t=t):
                    post_norm_residual(bp, sl, ntok, "gbc", xin[sl], ("xin", sl), cb=4 + 4 * i)
                    dst = ys_d if samp else yp_d[(t - T0) * 128:(t - T0 + 1) * 128, :]
                    P.add("sp", lambda e: e.dma_start(out=dst, in_=xin[sl][0:ntok, :]),
                          reads=[("xin", sl)], dsem=("xst", sl))
                deferred.append(fin)
            if samp or tl[-1]["t"] == NTILES - 1:
                od = scv_d if samp else ocv_d
                for c in range(24):
                    P.add("sp", lambda e, c=c: e.dma_start(out=od[:, c * 128:(c + 1) * 128].rearrange("t p -> p t"),
                                                           in_=H[:, 2 * c:2 * c + 2]),
                          reads=[(hk, c)], dsem="cvout")
            return deferred

        full = [T for T in tiles if T["kind"] == "full"]
        pt = [T for T in full if not T["samp"]]
        groups = [dict(tiles=[pt[0]])]
        for i in range(1, len(pt), 2):
            groups.append(dict(tiles=pt[i:i + 2]))
        groups.append(dict(tiles=[T for T in full if T["samp"]]))
        for i, G in enumerate(groups):
            G["hi"] = i % 2
        p2_front(groups[0])
        p2_front_a(groups[0])
        p2_front_b(groups[0])
        pend = []
        for i, G in enumerate(groups):
            nxt = (lambda Gn=groups[i + 1]: p2_front_b(Gn)) if i + 1 < len(groups) else None
            early = ((lambda Gn=groups[i + 1]: p2_front(Gn)), (lambda Gn=groups[i + 1]: p2_front_a(Gn))) \
                if i + 1 < len(groups) else None
            pend = p2_back(G, nxt, pend, early)
        for f_ in pend:
            f_()

        P.finalize()
        P.emit(nc)
    return nc


def _t5_bucket_np(n):
    import jax
    import jax.numpy as jnp
    try:
        cpu = jax.devices("cpu")[0]
    except Exception:
        cpu = None
    ctx = jax.default_device(cpu) if cpu is not None else contextlib.nullcontext()
    with ctx:
        n = jnp.asarray(n, dtype=jnp.int32)
        half = 16
        max_exact = 8
        ret = jnp.where(n < 0, half, 0)
        a = jnp.abs(n)
        af = jnp.maximum(a, 1).astype(jnp.float32)
        large = max_exact + (jnp.log(af / max_exact) / math.log(128 / max_exact)
                             * (half - max_exact)).astype(jnp.int32)
        large = jnp.minimum(large, half - 1)
        out = ret + jnp.where(a < max_exact, a, large)
        return np.asarray(out)


def _bias_tables(t5_table, rel_b):
    q = np.arange(128)
    kk = np.arange(256)
    rel = (128 + q)[None, :] - kk[:, None]
    cq = 2 + q // 64
    ck = kk // 64
    vis = ((cq[None, :] - ck[:, None]) >= 0) & ((cq[None, :] - ck[:, None]) <= 2)
    idx = _t5_bucket_np(rel)
    A = t5_table[:, idx]
    A = np.where(vis[None], A, np.float32(NEG)).astype(np.float32)
    A = A.reshape(8, 2, 128, 128).transpose(2, 1, 0, 3)
    kk = np.arange(640)
    rel = (512 + q)[None, :] - kk[:, None]
    cq = 8 + q // 64
    ck = kk // 64
    vis = ((cq[None, :] - ck[:, None]) >= 0) & ((cq[None, :] - ck[:, None]) <= 8)
    idx = np.clip(rel, -128, 128) + 128
    B = rel_b[:, idx]
    B = np.where(vis[None], B, np.float32(NEG)).astype(np.float32)
    perm = [0, 2, 4, 6, 1, 3, 5, 7]
    B = B.reshape(8, 5, 128, 128)[perm].transpose(2, 1, 0, 3)
    qs = np.arange(16)
    kk = np.arange(256)
    rel = qs[None, :] - kk[:, None] + 128
    As = t5_table[:, _t5_bucket_np(rel)].astype(np.float32)
    As = As.reshape(8, 2, 128, 16).transpose(2, 1, 0, 3)
    kk = np.arange(640)
    rel = qs[None, :] - kk[:, None] + 512
    Bs = rel_b[:, np.clip(rel, -128, 128) + 128].astype(np.float32)
    Bs = Bs.reshape(8, 5, 128, 16)[perm].transpose(2, 1, 0, 3)
    c = np.ascontiguousarray
    return (c(A).reshape(128, -1), c(B).reshape(128, -1), c(As).reshape(128, -1), c(Bs).reshape(128, -1))


_NC_CACHE = {}


def kernel(x_prompt, x_sample, cache_a_k, cache_a_v, cache_b_k, cache_b_v, state_conv,
           w_in, w_oa, w_ob, w_out, sink_a, t5_table, rel_table_b,
           g_pre_mix, g_post_mix, g_pre_ffn, g_post_ffn, w_upg, conv_w, conv_b, w_down):
    f = lambda a: np.ascontiguousarray(np.asarray(a, dtype=np.float32))
    x_prompt, x_sample = f(x_prompt), f(x_sample)
    bA, bB, bAs, bBs = _bias_tables(f(t5_table), f(rel_table_b)[0])
    shared = {
        "w_in": f(w_in)[0], "w_oa": f(w_oa)[0], "w_ob": f(w_ob)[0], "w_out": f(w_out)[0],
        "sink": f(sink_a).reshape(1, 8), "g_pre_mix": f(g_pre_mix).reshape(1, D),
        "g_post_mix": f(g_post_mix).reshape(1, D), "g_pre_ffn": f(g_pre_ffn).reshape(1, D),
        "g_post_ffn": f(g_post_ffn).reshape(1, D), "w_upg": f(w_upg)[0], "conv_w": f(conv_w)[0],
        "conv_b": f(conv_b).reshape(1, DFF), "w_down": f(w_down)[0],
        "biasA": bA, "biasB": bB, "biasAs": bAs, "biasBs": bBs,
        "ident": np.eye(128, dtype=np.float32),
    }
    halo = T0 * 128
    in_maps = []
    for c in range(8):
        b, half = c // 2, c % 2
        start = half * HALF
        xc = np.zeros((NTILES * 128, D), np.float32)
        if half == 0:
            xc[halo:] = x_prompt[b, 0:HALF]
        else:
            xc[:] = x_prompt[b, start - halo:start + HALF]
        m = dict(shared)
        m.update({
            "xc": xc, "xs": x_sample[c],
            "cak": f(cache_a_k)[0, c].reshape(128, 128), "cav": f(cache_a_v)[0, c].reshape(128, 128),
            "cbk": f(cache_b_k)[0, c].reshape(512, 512), "cbv": f(cache_b_v)[0, c].reshape(512, 512),
            "sconv": f(state_conv)[0, c],
            "hv": np.full((128, 1), float(half), np.float32),
        })
        in_maps.append(m)
    if "nc" not in _NC_CACHE:
        _NC_CACHE["nc"] = build_nc()
    nc = _NC_CACHE["nc"]
    res = run_bass_kernel_spmd(nc, in_maps, core_ids=list(range(8)))
    R = res.results
    y_prompt = np.stack([np.concatenate([R[2 * b]["yp"], R[2 * b + 1]["yp"]], axis=0) for b in range(4)], 0)
    y_sample = np.stack([R[c]["ys"] for c in range(8)], 0)
    odd = [R[2 * b + 1] for b in range(4)]
    nak = np.stack([r["oak"].reshape(128, 2, 64) for r in odd], 0)[None]
    nav = np.stack([r["oav"].reshape(128, 2, 64) for r in odd], 0)[None]
    nbk = np.stack([r["obk"].reshape(512, 8, 64) for r in odd], 0)[None]
    nbv = np.stack([r["obv"].reshape(512, 8, 64) for r in odd], 0)[None]
    ncv = np.stack([r["ocv"] for r in odd], 0)[None]
    sak = np.stack([R[c]["sak"].reshape(16, 2, 64) for c in range(8)], 0)[None]
    sav = np.stack([R[c]["sav"].reshape(16, 2, 64) for c in range(8)], 0)[None]
    sbk = np.stack([R[c]["sbk"].reshape(16, 8, 64) for c in range(8)], 0)[None]
    sbv = np.stack([R[c]["sbv"].reshape(16, 8, 64) for c in range(8)], 0)[None]
    scv = np.stack([R[c]["scv"] for c in range(8)], 0)[None]
    out = (y_prompt, y_sample, nak, nav, nbk, nbv, ncv, sak, sav, sbk, sbv, scv)
    return tuple(np.ascontiguousarray(o.astype(np.float32)) for o in out)
```

```python
import math
import contextlib
import numpy as np
import concourse.bass as bass
import concourse.mybir as mybir
from concourse.bass_utils import run_bass_kernel_spmd

F32 = mybir.dt.float32
BF16 = mybir.dt.bfloat16
AF = mybir.ActivationFunctionType
ALU = mybir.AluOpType

D = 1024
DFF = 3072
INW = 4352
SEQ = 8192
HALF = 4096
NT_MAIN = 32
T_HALO_KV = 4
T_HALO = 4
T0 = 5
NTILES = T0 + NT_MAIN
RT = 8
EPS = 1e-6
NEG = -30000.0
SAMP_SLOT = 4
C_QA, C_KA, C_VA, C_QB, C_KB, C_VB, C_GA, C_GB = 0, 512, 640, 768, 1280, 1792, 2304, 3328


class _Op:
    __slots__ = ("eng", "fn", "reads", "writes", "dsem", "deps", "mark", "rank", "waits", "clock")

    def __init__(self, eng, fn, reads, writes, dsem):
        self.eng = eng
        self.fn = fn
        self.reads = reads
        self.writes = writes
        self.dsem = dsem
        self.deps = None
        self.mark = False
        self.rank = 0
        self.waits = None
        self.clock = None


class Prog:
    ENGS = ("pe", "act", "dve", "pool", "sp")

    def __init__(self):
        self.ops = []
        self.group_final = set()

    def add(self, eng, fn, reads=(), writes=(), dsem=None):
        self.ops.append(_Op(eng, fn, tuple(reads), list(writes), dsem))

    def finalize(self):
        last_w = {}
        readers = {}
        ops = self.ops
        for i, op in enumerate(ops):
            deps = {}
            for k in op.reads:
                w = last_w.get(k)
                if w is not None:
                    deps[w] = "raw"
            for k in op.writes:
                w = last_w.get(k)
                if w is not None and w not in deps:
                    deps[w] = "waw"
                last_by_eng = {}
                for r in readers.get(k, ()):
                    if r == i:
                        continue
                    if ops[r].dsem is not None:
                        if r not in deps:
                            deps[r] = "war"
                    elif last_by_eng.get(ops[r].eng, -1) < r:
                        last_by_eng[ops[r].eng] = r
                for r in last_by_eng.values():
                    if r not in deps:
                        deps[r] = "war"
            for k in op.reads:
                readers.setdefault(k, []).append(i)
            for k in op.writes:
                last_w[k] = i
                readers[k] = []
            need = []
            for d, kind in deps.items():
                dop = ops[d]
                if dop.dsem is not None and dop.dsem == op.dsem and op.dsem in self.group_final:
                    continue
                if dop.dsem is not None or op.dsem is not None:
                    need.append(d)
                elif dop.eng != op.eng:
                    need.append(d)
                elif kind == "raw" and op.eng != "pe":
                    need.append(d)
            op.deps = need
            for d in need:
                ops[d].mark = True
        cnt = {e: 0 for e in self.ENGS}
        dcnt = {}
        for op in ops:
            if op.dsem is not None:
                dcnt[op.dsem] = dcnt.get(op.dsem, 0) + 16
                op.rank = dcnt[op.dsem]
            elif op.mark:
                cnt[op.eng] += 1
                op.rank = cnt[op.eng]
        for op in ops:
            if op.dsem is not None and op.dsem in self.group_final:
                op.rank = dcnt[op.dsem]
        clock = {e: {} for e in self.ENGS}
        nw = 0
        for op in ops:
            ck = clock[op.eng]
            waits = {}
            for d in sorted(op.deps, key=lambda d: -ops[d].rank):
                dop = ops[d]
                sk = ("d", dop.dsem) if dop.dsem is not None else ("e", dop.eng)
                if ck.get(sk, 0) >= dop.rank:
                    continue
                if waits.get(sk, 0) < dop.rank:
                    waits[sk] = dop.rank
                ck[sk] = dop.rank
                if dop.clock is not None:
                    for k2, v2 in dop.clock.items():
                        if ck.get(k2, 0) < v2:
                            ck[k2] = v2
            op.waits = waits
            nw += len(waits)
            if op.dsem is not None:
                op.clock = dict(ck)
            elif op.mark:
                c2 = dict(ck)
                c2[("e", op.eng)] = op.rank
                op.clock = c2
        self.n_waits = nw
        self.counts = cnt
        self.dcounts = dcnt
        import os
        pr = os.environ.get("KPRINT")
        if pr:
            a, b = [int(x) for x in pr.split(":")]
            for i in range(a, min(b, len(ops))):
                op = ops[i]
                print("W", i, op.eng, "rank", op.rank if (op.mark or op.dsem) else "-", "deps", [(d, ops[d].eng, ops[d].rank) for d in op.deps],
                      "waits", op.waits, "r", op.reads, "w", op.writes)

    def emit(self, nc):
        per_eng = {e: [] for e in self.ENGS}
        for op in self.ops:
            per_eng[op.eng].append(op)
        EP = 16384
        with contextlib.ExitStack() as st:
            esem = {e: [st.enter_context(nc.semaphore("es_%s_%d" % (e, j)))
                        for j in range((self.counts[e] + EP - 1) // EP + 1)] for e in self.ENGS}
            dsem = {k: st.enter_context(nc.semaphore("ds_%d" % i)) for i, k in enumerate(self.dcounts)}
            block = st.enter_context(nc.Block())

            def sem_of(sk, v):
                if sk[0] == "d":
                    return dsem[sk[1]], v
                return esem[sk[1]][(v - 1) // EP], (v - 1) % EP + 1

            def run(engname, eng):
                for op in per_eng[engname]:
                    for sk, v in op.waits.items():
                        eng.wait_ge(*sem_of(sk, v))
                    ins = op.fn(eng)
                    if op.dsem is not None:
                        ins.then_inc(dsem[op.dsem], 16)
                    elif op.mark:
                        ins.then_inc(esem[op.eng][(op.rank - 1) // EP], 1)
                if engname == "sp":
                    for k, v in self.dcounts.items():
                        eng.wait_ge(dsem[k], v)

            @block.tensor
            def _(e):
                run("pe", e)

            @block.scalar
            def _(e):
                run("act", e)

            @block.vector
            def _(e):
                run("dve", e)

            @block.gpsimd
            def _(e):
                run("pool", e)

            @block.sync
            def _(e):
                run("sp", e)


def build_nc(stage=9, ntiles_p1=None):
    nc = bass.Bass("TRN2", target_bir_lowering=False)

    def din(name, shape):
        return nc.dram_tensor(name, list(shape), F32, kind="ExternalInput").ap()

    def dout(name, shape):
        return nc.dram_tensor(name, list(shape), F32, kind="ExternalOutput").ap()

    xc_d = din("xc", [NTILES * 128, D])
    xs_d = din("xs", [16, D])
    cak_d = din("cak", [128, 128])
    cav_d = din("cav", [128, 128])
    cbk_d = din("cbk", [512, 512])
    cbv_d = din("cbv", [512, 512])
    sconv_d = din("sconv", [2, DFF])
    win_d = din("w_in", [D, INW])
    woa_d = din("w_oa", [512, D])
    wob_d = din("w_ob", [512, D])
    wout_d = din("w_out", [D, D])
    sink_d = din("sink", [1, 8])
    gpm_d = din("g_pre_mix", [1, D])
    gqm_d = din("g_post_mix", [1, D])
    gpf_d = din("g_pre_ffn", [1, D])
    gqf_d = din("g_post_ffn", [1, D])
    wup_d = din("w_upg", [D, 2 * DFF])
    cw_d = din("conv_w", [3, DFF])
    cb_d = din("conv_b", [1, DFF])
    wdn_d = din("w_down", [DFF, D])
    ba_d = din("biasA", [128, 2 * 8 * 128])
    bb_d = din("biasB", [128, 5 * 8 * 128])
    bas_d = din("biasAs", [128, 2 * 8 * 16])
    bbs_d = din("biasBs", [128, 5 * 8 * 16])
    ident_d = din("ident", [128, 128])
    hv_d = din("hv", [128, 1])

    yp_d = dout("yp", [HALF, D])
    ys_d = dout("ys", [16, D])
    oak_d = dout("oak", [128, 128])
    oav_d = dout("oav", [128, 128])
    obk_d = dout("obk", [512, 512])
    obv_d = dout("obv", [512, 512])
    ocv_d = dout("ocv", [2, DFF])
    sak_d = dout("sak", [16, 128])
    sav_d = dout("sav", [16, 128])
    sbk_d = dout("sbk", [16, 512])
    sbv_d = dout("sbv", [16, 512])
    scv_d = dout("scv", [2, DFF])

    import os as _os
    if _os.environ.get("KDBG"):
        x1s_d = nc.dram_tensor("x1s", [(NTILES + 1) * 128, D], F32, kind="ExternalOutput").ap()
    else:
        x1s_d = nc.dram_tensor("x1s", [(NTILES + 1) * 128, D], F32).ap()

    P = Prog()
    P.group_final.add("setup")
    KDBG = bool(_os.environ.get("KDBG"))

    def dbg_dump(name, ap, shape, dt, reads):
        if not KDBG:
            return
        dd = nc.dram_tensor("dbg_" + name, list(shape), dt, kind="ExternalOutput").ap()
        P.add("sp", lambda e: e.dma_start(out=dd, in_=ap), reads=reads, dsem="dbg")

    with contextlib.ExitStack() as st:
        def sb(name, shape, dt):
            return st.enter_context(nc.sbuf_tensor("s_" + name, list(shape), dt))

        st.enter_context(nc.allow_non_contiguous_dma(reason="small constant / transposing loads"))

        WAR = sb("warena", [128, 73728], BF16)
        PAR = sb("parena", [128, 13440], BF16)
        FAR = sb("farena", [128, 1040], F32)
        xin = [sb("xin%d" % i, [128, D], F32) for i in range(4)]
        xn = [sb("xn%d" % i, [128, D], BF16) for i in range(2)]
        tmp = sb("tmp", [128, 1280], F32)
        gbc = sb("gbc", [128, D], F32)
        identf = sb("identf", [128, 128], F32)
        ident = sb("ident", [128, 128], BF16)
        gcol = sb("gcol", [128, 16], F32)
        cw = sb("cw", [128, 3 * 24], F32)
        cbias = sb("cbias", [128, 24], F32)
        hv = sb("hv", [128, 1], F32)
        hv10 = sb("hv10", [128, 10], BF16)
        epst = sb("epst", [128, 1], F32)
        sinkt = sb("sinkt", [128, 8], F32)
        exps = sb("exps", [128, 8], F32)
        ss = sb("ss", [128, 16], F32)
        sd = sb("sd", [128, 16], F32)
        rstd = sb("rstd", [128, 16], F32)
        junk = sb("junk", [128, 512], BF16)
        den = sb("den", [128, 8], F32)
        rden = sb("rden", [128, 8], F32)
        hist = sb("hist", [128, 48], F32)
        hists = sb("hists", [128, 48], F32)
        bars = sb("bars", [128, 8], F32)

        Win = WAR[:, 0:34816].rearrange("p (k c) -> p k c", k=8)
        Woa = WAR[:, 34816:38912].rearrange("p (k c) -> p k c", k=4)
        Wob = WAR[:, 38912:43008].rearrange("p (k c) -> p k c", k=4)
        Wout = WAR[:, 43008:51200].rearrange("p (k c) -> p k c", k=8)
        o = 51200
        BA = WAR[:, o:o + 4096].bitcast(F32).rearrange("p (t h q) -> p t h q", t=2, h=8)
        o += 4096
        BB = WAR[:, o:o + 10240].bitcast(F32).rearrange("p (t h q) -> p t h q", t=5, h=8)
        o += 10240
        BAs = WAR[:, o:o + 512].bitcast(F32).rearrange("p (t h q) -> p t h q", t=2, h=8)
        o += 512
        BBs = WAR[:, o:o + 1280].bitcast(F32).rearrange("p (t h q) -> p t h q", t=5, h=8)
        o += 1280
        KTa = WAR[:, o:o + RT * 128]
        o += RT * 128
        KTb = WAR[:, o:o + 4 * RT * 128].rearrange("p (c k) -> p c k", c=4)
        o += 4 * RT * 128
        assert o <= 73728
        Wup = WAR[:, 0:49152].rearrange("p (k c) -> p k c", k=8)
        Wdn = WAR[:, 49152:73728].rearrange("p (k c) -> p k c", k=24)

        hT = [PAR[:, i * 1024:(i + 1) * 1024].rearrange("p (k t) -> p k t", k=8) for i in range(2)]
        QT = PAR[:, 2048:3072].rearrange("p (k t) -> p k t", k=8)
        OT = PAR[:, 3072:4096].rearrange("p (k t) -> p k t", k=8)
        mT = PAR[:, 4096:5120].rearrange("p (k t) -> p k t", k=8)
        Vr = PAR[:, 5120:5120 + RT * 650].rearrange("p (s h d) -> p s h d", s=RT, h=10)
        o = 5120 + RT * 650
        pT = [PAR[:, o + i * 512:o + (i + 1) * 512] for i in range(4)]
        o += 2048
        Otok = PAR[:, o:o + 1024]
        o += 1024
        assert o <= 13440
        gtmp = [FAR[:, i * 256:(i + 1) * 256] for i in range(2)]
        h2T = [PAR[:, i * 2048:(i + 1) * 2048].rearrange("p (k t) -> p k t", k=8) for i in range(2)]
        actT = PAR[:, 4096:4096 + 24 * 256].rearrange("p (k t) -> p k t", k=24)
        ubuf = [FAR[:, i * 258:(i + 1) * 258] for i in range(2)]
        cbuf = [FAR[:, 516 + i * 256:516 + (i + 1) * 256] for i in range(2)]

        banks = [st.enter_context(nc.psum_tensor("bank%d" % i, [128, 512], F32)) for i in range(8)]

        def bk(b):
            return banks[b][:, :]

        def bkb(b):
            return banks[b][:, :].bitcast(BF16)

        def BK(b):
            return ("bank", b)

        cnt = {"xin": 0, "cast": 0, "ev": 0}

        def evac_eng():
            cnt["ev"] += 1
            return "act" if cnt["ev"] % 2 else "dve"

        def copy_op(eng, out, in_, scale=None):
            if eng == "act":
                if scale is None:
                    return lambda e: e.activation(out=out, in_=in_, func=AF.Copy)
                return lambda e: e.activation(out=out, in_=in_, func=AF.Copy, scale=scale)
            if scale is None:
                return lambda e: e.tensor_copy(out=out, in_=in_)
            return lambda e: e.tensor_scalar(out=out, in0=in_, scalar1=scale, scalar2=None, op0=ALU.mult)

        def wkeys(name, kc, c0, c1):
            return [(name, kc, c) for c in range(c0 // 1024, (c1 - 1) // 1024 + 1)]

        import os
        SKIP = os.environ.get("KSKIP", "")

        def setup_dma(out, in_, key):
            k0 = key[0] if isinstance(key, tuple) else key
            if k0 in SKIP.split(","):
                return
            P.add("sp", lambda e: e.dma_start(out=out, in_=in_), writes=[key], dsem="setup")

        setup_dma(identf[:, :], ident_d, "identf")
        setup_dma(hv[:, :], hv_d, "hv")
        setup_dma(sinkt[:, :], sink_d.partition_broadcast(128), "sinkt")
        setup_dma(gcol[:, 0:8], gpm_d.rearrange("o (k p) -> p (o k)", p=128), "gcol")
        setup_dma(gcol[:, 8:16], gpf_d.rearrange("o (k p) -> p (o k)", p=128), "gcol")
        for tap in range(3):
            setup_dma(cw[:, tap * 24:(tap + 1) * 24], cw_d[tap:tap + 1, :].rearrange("o (c p) -> p (o c)", p=128), "cw")
        setup_dma(cbias[:, :], cb_d.rearrange("o (c p) -> p (o c)", p=128), "cbias")
        setup_dma(BA.rearrange("p t h q -> p (t h q)"), ba_d, "BA")
        setup_dma(BB.rearrange("p t h q -> p (t h q)"), bb_d, "BB")
        setup_dma(BAs.rearrange("p t h q -> p (t h q)"), bas_d, "BAs")
        setup_dma(BBs.rearrange("p t h q -> p (t h q)"), bbs_d, "BBs")
        setup_dma(gbc[:, :], gqm_d.partition_broadcast(128), "gbc")
        for c in range(24):
            setup_dma(hists[:, 2 * c:2 * c + 2], sconv_d[:, c * 128:(c + 1) * 128].rearrange("t p -> p t"), ("hists", c))

        P.add("dve", lambda e: e.tensor_copy(out=ident[:, :], in_=identf[:, :]), reads=["identf"], writes=["ident"])
        P.add("dve", lambda e: e.memset(epst[:, :], EPS), writes=["epst"])
        P.add("dve", lambda e: e.memset(hist[:, :], 0.0), writes=[("hist", j) for j in range(24)])
        P.add("dve", lambda e: e.memset(den[:, 0:8], 1.0), writes=["den0"])
        P.add("dve", lambda e: e.tensor_scalar(out=rden[:, 0:8], in0=den[:, 0:8], scalar1=hv[:, 0:1], scalar2=None,
                                                op0=ALU.mult), reads=["den0", "hv"], writes=["rden0"])
        P.add("dve", lambda e: e.tensor_copy(out=hv10[:, 0:8], in_=rden[:, 0:8]), reads=["rden0"], writes=["hv10a"])
        P.add("dve", lambda e: e.tensor_copy(out=hv10[:, 8:10], in_=rden[:, 0:2]), reads=["rden0"], writes=["hv10"])
        P.add("act", lambda e: e.activation(out=exps[:, :], in_=sinkt[:, :], func=AF.Exp),
              reads=["sinkt"], writes=["exps"])

        def prep_weight(name, dram, nk, ncols, dest, scale_col=None, scale_const=None, extra_reads=()):
            for kc in range(nk):
                for c0 in range(0, ncols, 1024):
                    w = min(1024, ncols - c0)
                    sl = cnt["xin"] % 4
                    cnt["xin"] += 1
                    stg = xin[sl]
                    P.add("sp", lambda e, stg=stg, kc=kc, c0=c0, w=w: e.dma_start(
                        out=stg[:, 0:w], in_=dram[kc * 128:(kc + 1) * 128, c0:c0 + w]),
                        writes=[("xin", sl)], dsem=("xin", sl))
                    cnt["cast"] += 1
                    eng = "act" if cnt["cast"] % 2 else "dve"
                    if scale_col is not None:
                        sc = gcol[:, scale_col + kc:scale_col + kc + 1]
                        rd = [("xin", sl), "gcol"]
                    else:
                        sc = scale_const
                        rd = [("xin", sl)]
                    rd = rd + list(extra_reads)
                    key = (name, kc, c0 // 1024)
                    if name == "Win" and c0 == 0:
                        dq = dest[:, kc, 0:512].rearrange("p (i two d) -> p i two d", two=2, d=64)
                        for two in range(2):
                            P.add(eng, copy_op(eng, dq[:, :, two, :],
                                               stg[:, two * 256:(two + 1) * 256].rearrange("p (i d) -> p i d", d=64), sc),
                                  reads=rd, writes=[key])
                        P.add(eng, copy_op(eng, dest[:, kc, 512:1024], stg[:, 512:1024], sc), reads=rd, writes=[key])
                    else:
                        P.add(eng, copy_op(eng, dest[:, kc, c0:c0 + w], stg[:, 0:w], sc), reads=rd, writes=[key])

        if stage >= 0:
            prep_weight("Win", win_d, 8, INW, Win, scale_col=0)
            prep_weight("Woa", woa_d, 4, D, Woa)
            prep_weight("Wob", wob_d, 4, D, Wob)
            prep_weight("Wout", wout_d, 8, D, Wout, scale_const=0.5)

        def norm_a(x_ap_tile, sl, ntok, xnb, xnkey, col=0):
            c1 = slice(col, col + 1)
            P.add("act", lambda e: e.activation(out=xnb[0:ntok, :], in_=x_ap_tile[0:ntok, :], func=AF.Square,
                                                accum_out=ss[0:ntok, c1]),
                  reads=[("xin", sl)], writes=[xnkey, ("ss", col)])
            P.add("act", lambda e: e.activation(out=sd[0:ntok, c1], in_=ss[0:ntok, c1], func=AF.Sqrt,
                                                scale=1.0 / D, bias=epst[0:ntok, :]),
                  reads=[("ss", col), "epst"], writes=[("sd", col)])
            P.add("dve", lambda e: e.reciprocal(out=rstd[0:ntok, c1], in_=sd[0:ntok, c1]),
                  reads=[("sd", col)], writes=[("rstd", col)])
            P.add("act", lambda e: e.activation(out=xnb[0:ntok, :], in_=x_ap_tile[0:ntok, :], func=AF.Copy,
                                                scale=rstd[0:ntok, c1]),
                  reads=[("xin", sl), ("rstd", col)], writes=[xnkey])

        def norm_b(ntok, hTd, hkey, xnb, xnkey, b=0, eng=None):
            for kc in range(8):
                P.add("pe", lambda e, kc=kc, b=b: e.transpose(out=bkb(b)[:, kc * 128:kc * 128 + ntok],
                                                         in_=xnb[0:ntok, kc * 128:(kc + 1) * 128],
                                                         identity=ident[0:ntok, 0:ntok]),
                      reads=[xnkey, "ident"], writes=[BK(b)])
            eng = eng or evac_eng()
            P.add(eng, copy_op(eng, hTd[:, :, 0:ntok],
                               bkb(b).rearrange("p (k t) -> p k t", k=8)[:, :, 0:ntok]),
                  reads=[BK(b)], writes=[hkey])

        def post_norm_residual(bankpair, sl, ntok, gkey, out_ap, out_key, cb=4):
            b0, b1 = bankpair
            for i, b in enumerate((b0, b1)):
                P.add("act", lambda e, b=b, i=i: e.activation(out=junk[0:ntok, :],
                                                              in_=bk(b)[0:ntok, :], func=AF.Square,
                                                              accum_out=ss[0:ntok, cb + i:cb + i + 1]),
                      reads=[BK(b)], writes=["junk", ("ss", cb + i)])
            P.add("dve", lambda e: e.tensor_tensor(out=ss[0:ntok, cb + 2:cb + 3], in0=ss[0:ntok, cb:cb + 1],
                                                   in1=ss[0:ntok, cb + 1:cb + 2], op=ALU.add),
                  reads=[("ss", cb), ("ss", cb + 1)], writes=[("ss", cb + 2)])
            P.add("act", lambda e: e.activation(out=sd[0:ntok, cb:cb + 1], in_=ss[0:ntok, cb + 2:cb + 3], func=AF.Sqrt,
                                                scale=1.0 / D, bias=epst[0:ntok, :]),
                  reads=[("ss", cb + 2), "epst"], writes=[("sd", cb)])
            P.add("dve", lambda e: e.reciprocal(out=rstd[0:ntok, cb:cb + 1], in_=sd[0:ntok, cb:cb + 1]),
                  reads=[("sd", cb)], writes=[("rstd", cb)])
            for i, b in enumerate((b0, b1)):
                P.add("dve", lambda e, b=b, i=i: e.scalar_tensor_tensor(
                    out=tmp[0:ntok, i * 512:(i + 1) * 512], in0=bk(b)[0:ntok, :], scalar=rstd[0:ntok, cb:cb + 1],
                    in1=gbc[0:ntok, i * 512:(i + 1) * 512], op0=ALU.mult, op1=ALU.mult),
                    reads=[BK(b), ("rstd", cb), gkey], writes=["tmp"])
            P.add("pool", lambda e: e.tensor_tensor(out=out_ap[0:ntok, 0:D], in0=tmp[0:ntok, 0:D],
                                                    in1=xin[sl][0:ntok, :], op=ALU.add),
                  reads=["tmp", ("xin", sl)], writes=[out_key])

        pj = {"i": 0}

        def next_pj():
            pj["i"] += 1
            return 3 + pj["i"] % 5

        def p1_front(T):
            t, kind, ntok, samp, slot = T["t"], T["kind"], T["ntok"], T["samp"], T["slot"]
            sl = cnt["xin"] % 4
            cnt["xin"] += 1
            T["sl"] = sl
            hi = T["hi"]
            src = xs_d if samp else xc_d[t * 128:(t + 1) * 128, :]
            P.add("sp", lambda e: e.dma_start(out=xin[sl][0:ntok, :], in_=src),
                  writes=[("xin", sl)], dsem=("xin", sl))

        def p1_front_a(T):
            norm_a(xin[T["sl"]], T["sl"], T["ntok"], xn[T["hi"]], ("xn", T["hi"]))

        def p1_front_b(T):
            ntok, hi = T["ntok"], T["hi"]
            norm_b(ntok, hT[hi], ("hT", hi), xn[hi], ("xn", hi), b=2, eng="dve")

        def fm_chunk(T, col0, dest, dkey, scale):
            ntok, hi = T["ntok"], T["hi"]
            b = next_pj()
            for kc in range(8):
                P.add("pe", lambda e, kc=kc: e.matmul(bk(b)[:, 0:ntok], lhsT=Win[:, kc, col0:col0 + 128],
                                                      rhs=hT[hi][:, kc, 0:ntok], start=(kc == 0), stop=(kc == 7)),
                      reads=[("hT", hi)] + wkeys("Win", kc, col0, col0 + 128), writes=[BK(b)])
            eng = evac_eng()
            P.add(eng, copy_op(eng, dest, bk(b)[:, 0:ntok], scale), reads=[BK(b)], writes=[dkey])

        def p1_proj(T):
            t, kind, ntok, samp, slot, hi = T["t"], T["kind"], T["ntok"], T["samp"], T["slot"], T["hi"]
            ks = slice(slot * 128, slot * 128 + ntok)
            if kind == "full":
                for ci in range(4):
                    fm_chunk(T, C_QA + ci * 128, QT[:, ci, 0:ntok], ("QT", ci), 0.125)
                for ci in range(4):
                    fm_chunk(T, C_QB + ci * 128, QT[:, 4 + ci, 0:ntok], ("QT", 4 + ci), 0.125)
            fm_chunk(T, C_KA, KTa[:, ks], ("KTa", slot), None)
            for ci in range(4):
                fm_chunk(T, C_KB + ci * 128, KTb[:, ci, ks], ("KTb", slot, ci), None)
            kvout = T["kvout"]
            if not kvout:
                b = next_pj()
                for kc in range(8):
                    P.add("pe", lambda e, kc=kc, b=b: e.matmul(bk(b)[0:ntok, 0:128], lhsT=hT[hi][:, kc, 0:ntok],
                                                          rhs=Win[:, kc, C_VA:C_VA + 128], start=(kc == 0),
                                                          stop=(kc == 7)),
                          reads=[("hT", hi)] + wkeys("Win", kc, C_VA, C_VA + 128), writes=[BK(b)])
                P.add("act", copy_op("act", Vr[0:ntok, slot, 0:2, 0:64],
                                     bk(b)[0:ntok, 0:128].rearrange("p (h d) -> p h d", h=2)),
                      reads=[BK(b)], writes=[("Va", slot)])
                b = next_pj()
                for kc in range(8):
                    P.add("pe", lambda e, kc=kc, b=b: e.matmul(bk(b)[0:ntok, 0:512], lhsT=hT[hi][:, kc, 0:ntok],
                                                          rhs=Win[:, kc, C_VB:C_VB + 512], start=(kc == 0),
                                                          stop=(kc == 7)),
                          reads=[("hT", hi)] + wkeys("Win", kc, C_VB, C_VB + 512), writes=[BK(b)])
                P.add("dve", copy_op("dve", Vr[0:ntok, slot, 2:10, 0:64],
                                     bk(b)[0:ntok, 0:512].rearrange("p (h d) -> p h d", h=8)),
                      reads=[BK(b)], writes=[("Vb", slot)])
            else:
                segs = [(3, C_KA, 256, 0), (4, C_KB, 512, 256), (3, C_VB, 512, 768)]
                for (b, c0, w, to) in segs:
                    for kc in range(8):
                        P.add("pe", lambda e, kc=kc, b=b, c0=c0, w=w: e.matmul(
                            bk(b)[0:ntok, 0:w], lhsT=hT[hi][:, kc, 0:ntok], rhs=Win[:, kc, c0:c0 + w],
                            start=(kc == 0), stop=(kc == 7)),
                            reads=[("hT", hi)] + wkeys("Win", kc, c0, c0 + w), writes=[BK(b)])
                    eng = evac_eng()
                    P.add(eng, copy_op(eng, tmp[0:ntok, to:to + w], bk(b)[0:ntok, 0:w]),
                          reads=[BK(b)], writes=["tmp"])
                P.add("act", copy_op("act", Vr[0:ntok, slot, 0:2, 0:64],
                                     tmp[0:ntok, 128:256].rearrange("p (h d) -> p h d", h=2)),
                      reads=["tmp"], writes=[("Va", slot)])
                P.add("dve", copy_op("dve", Vr[0:ntok, slot, 2:10, 0:64],
                                     tmp[0:ntok, 768:1280].rearrange("p (h d) -> p h d", h=8)),
                      reads=["tmp"], writes=[("Vb", slot)])
                for (dst, c0, w) in T["kvdst"]:
                    P.add("sp", lambda e, dst=dst, c0=c0, w=w: e.dma_start(out=dst, in_=tmp[0:ntok, c0:c0 + w]),
                          reads=["tmp"], dsem="kvst")
            if (not samp) and t <= T_HALO:
                P.add("pool", lambda e: e.tensor_copy(out=Vr[0:ntok, slot, :, 64], in_=hv10[0:ntok, :]),
                      reads=["hv10"], writes=[("V1", slot)])
            else:
                P.add("pool", lambda e: e.memset(Vr[0:ntok, slot, :, 64], 1.0), writes=[("V1", slot)])

        def attention(T):
            ntok = T["ntok"]
            a_tiles, b_tiles, TA, TB, ka, kb_ = T["a_tiles"], T["b_tiles"], T["TA"], T["TB"], T["TAk"], T["TBk"]
            units = []
            for g in range(2):
                for i, (slot, nk, bi) in enumerate(a_tiles):
                    units.append(dict(kind="A", g=g, slot=slot, nk=nk, bi=bi, first=(i == 0),
                                      last=(i == len(a_tiles) - 1)))
            for g in range(2):
                for i, (slot, nk, bi) in enumerate(b_tiles):
                    units.append(dict(kind="B", g=g, slot=slot, nk=nk, bi=bi, first=(i == 0),
                                      last=(i == len(b_tiles) - 1)))
            sbanks = [2, 3, 4, 5]
            W4 = 4 * ntok

            def emit_qk(ui, u):
                b = sbanks[ui % 4]
                u["b"] = b
                u["p"] = ui % 4
                slot, nk, g = u["slot"], u["nk"], u["g"]
                if u["kind"] == "A":
                    P.add("pe", lambda e: e.matmul(
                        bk(b)[0:nk, 0:W4].rearrange("p (h q) -> p h q", h=4),
                        lhsT=KTa[g * 64:(g + 1) * 64, slot * 128:slot * 128 + nk],
                        rhs=QT[g * 64:(g + 1) * 64, 0:4, 0:ntok], start=True, stop=True),
                        reads=[("KTa", slot)] + [("QT", c) for c in range(4)], writes=[BK(b)])
                else:
                    for hh in range(4):
                        pb = g * 64
                        P.add("pe", lambda e, hh=hh, pb=pb: e.matmul(
                            bk(b)[0:nk, hh * ntok:(hh + 1) * ntok],
                            lhsT=KTb[pb:pb + 64, hh, slot * 128:slot * 128 + nk],
                            rhs=QT[pb:pb + 64, 4 + hh, 0:ntok], start=True, stop=True),
                            reads=[("KTb", slot, hh), ("QT", 4 + hh)], writes=[BK(b)])
                tab, tkey = (TA, ka) if u["kind"] == "A" else (TB, kb_)
                bi = u["bi"]
                P.add("dve", lambda e: e.tensor_tensor(
                    out=bk(b)[0:nk, 0:W4].rearrange("p (h q) -> p h q", h=4),
                    in0=bk(b)[0:nk, 0:W4].rearrange("p (h q) -> p h q", h=4),
                    in1=tab[0:nk, bi, 4 * g:4 * g + 4, :], op=ALU.add),
                    reads=[BK(b), tkey], writes=[BK(b)])
                P.add("act", lambda e: e.activation(out=pT[u["p"]][0:nk, 0:W4], in_=bk(b)[0:nk, 0:W4], func=AF.Exp),
                      reads=[BK(b)], writes=[("pT", u["p"])])

            def emit_pv(u):
                slot, nk, g = u["slot"], u["nk"], u["g"]
                ob = 6 + g
                for hh in range(4):
                    vh = g if u["kind"] == "A" else 2 + 2 * hh + g
                    P.add("pe", lambda e, hh=hh, vh=vh: e.matmul(
                        bk(ob)[0:ntok, hh * 65:(hh + 1) * 65], lhsT=pT[u["p"]][0:nk, hh * ntok:(hh + 1) * ntok],
                        rhs=Vr[0:nk, slot, vh, 0:65], start=(u["first"] and hh == 0), stop=(u["last"] and hh == 3)),
                        reads=[("pT", u["p"]), ("Va", slot) if u["kind"] == "A" else ("Vb", slot), ("V1", slot)],
                        writes=[BK(ob)])
                if u["last"]:
                    o3 = bk(ob)[0:ntok, 0:260].rearrange("p (h d) -> p h d", h=4)
                    dsl = slice(4 * g, 4 * g + 4)
                    if u["kind"] == "A":
                        P.add("dve", lambda e: e.tensor_tensor(out=den[0:ntok, dsl], in0=o3[:, :, 64],
                                                               in1=exps[0:ntok, dsl], op=ALU.add),
                              reads=[BK(ob), "exps"], writes=[("den", g)])
                    else:
                        P.add("dve", lambda e: e.tensor_scalar(out=den[0:ntok, dsl], in0=o3[:, :, 64],
                                                               scalar1=1e-30, scalar2=None, op0=ALU.add),
                              reads=[BK(ob)], writes=[("den", g)])
                    P.add("dve", lambda e: e.reciprocal(out=rden[0:ntok, dsl], in_=den[0:ntok, dsl]),
                          reads=[("den", g)], writes=[("rden", g)])
                    for hh in range(4):
                        c = (4 * g + hh) * 64 if u["kind"] == "A" else 512 + (2 * hh + g) * 64
                        P.add("dve", lambda e, hh=hh, c=c: e.tensor_scalar(
                            out=Otok[0:ntok, c:c + 64], in0=bk(ob)[0:ntok, hh * 65:hh * 65 + 64],
                            scalar1=rden[0:ntok, 4 * g + hh:4 * g + hh + 1], scalar2=None, op0=ALU.mult),
                            reads=[BK(ob), ("rden", g)], writes=[("Otok", c // 128)])

            L = 3
            for i in range(len(units) + L):
                if i < len(units):
                    emit_qk(i, units[i])
                if i - L >= 0:
                    emit_pv(units[i - L])

        def p1_back(T, mid=None, early=None, late=None):
            t, kind, ntok, samp, slot, hi, sl = T["t"], T["kind"], T["ntok"], T["samp"], T["slot"], T["hi"], T["sl"]
            if early is not None:
                early()
            if kind != "full":
                if mid is not None:
                    mid()
                if late is not None:
                    late()
                return None
            attention(T)
            if mid is not None:
                mid()
            if late is not None:
                late()
            if samp:
                dbg_dump("QT", QT[:, :, 0:16], [128, 8, 16], BF16, [("QT", c) for c in range(8)])
                dbg_dump("hT", hT[hi][:, :, 0:16], [128, 8, 16], BF16, [("hT", hi)])
                dbg_dump("Otok", Otok[0:16, :], [16, 1024], BF16, [("Otok", c) for c in range(8)])
                dbg_dump("den", den[0:16, :], [16, 8], F32, [("den", 0), ("den", 1)])
                dbg_dump("KTb", KTb[:, :, 0:640], [128, 4, 640], BF16, [("KTb", s_, c) for s_ in range(5) for c in range(4)])
                dbg_dump("Vr", Vr[:, 0:5, :, :], [128, 5, 10, 65], BF16, [("Vb", s_) for s_ in range(5)] + [("Va", 3), ("Va", 4)] + [("V1", s_) for s_ in range(5)])
            b = 7
            for kc in range(8):
                P.add("pe", lambda e, kc=kc, b=b: e.transpose(out=bkb(b)[:, kc * 128:kc * 128 + ntok],
                                                         in_=Otok[0:ntok, kc * 128:(kc + 1) * 128],
                                                         identity=ident[0:ntok, 0:ntok]),
                      reads=[("Otok", kc), "ident"], writes=[BK(b)])
            eng = "dve"
            P.add(eng, copy_op(eng, OT[:, :, 0:ntok], bkb(b).rearrange("p (k t) -> p k t", k=8)[:, :, 0:ntok]),
                  reads=[BK(b)], writes=["OT"])
            for fc in range(8):
                bs = [0, 1, 2, 3] if fc % 2 == 0 else [4, 5, 6, 7]
                gt = gtmp[fc % 2]
                for (b, c0) in ((bs[0], C_GA + fc * 128), (bs[1], C_GB + fc * 128)):
                    for kc in range(8):
                        P.add("pe", lambda e, kc=kc, b=b, c0=c0: e.matmul(
                            bk(b)[:, 0:ntok], lhsT=Win[:, kc, c0:c0 + 128], rhs=hT[hi][:, kc, 0:ntok],
                            start=(kc == 0), stop=(kc == 7)),
                            reads=[("hT", hi)] + wkeys("Win", kc, c0, c0 + 128), writes=[BK(b)])
                for (b, W, wn, off) in ((bs[2], Woa, "Woa", 0), (bs[3], Wob, "Wob", 4)):
                    for kc in range(4):
                        P.add("pe", lambda e, kc=kc, b=b, W=W, off=off, fc=fc: e.matmul(
                            bk(b)[:, 0:ntok], lhsT=W[:, kc, fc * 128:(fc + 1) * 128], rhs=OT[:, off + kc, 0:ntok],
                            start=(kc == 0), stop=(kc == 3)),
                            reads=["OT", (wn, kc, 0)], writes=[BK(b)])
                for i in range(2):
                    P.add("act", lambda e, i=i, gt=gt, bs=bs: e.activation(out=gt[:, i * 128:i * 128 + ntok],
                                                             in_=bk(bs[i])[:, 0:ntok], func=AF.Tanh, scale=0.5),
                          reads=[BK(bs[i])], writes=[("gt", fc % 2, i)])
                for i in range(2):
                    P.add("dve", lambda e, i=i, gt=gt, bs=bs: e.scalar_tensor_tensor(
                        out=gt[:, i * 128:i * 128 + ntok], in0=gt[:, i * 128:i * 128 + ntok], scalar=1.0,
                        in1=bk(bs[2 + i])[:, 0:ntok], op0=ALU.add, op1=ALU.mult),
                        reads=[("gt", fc % 2, i), BK(bs[2 + i])], writes=[("gt", fc % 2, i)])
                P.add("pool", lambda e, fc=fc, gt=gt: e.tensor_tensor(out=mT[:, fc, 0:ntok], in0=gt[:, 0:ntok],
                                                        in1=gt[:, 128:128 + ntok], op=ALU.add),
                      reads=[("gt", fc % 2, 0), ("gt", fc % 2, 1)], writes=[("mT", fc)])
            for half in range(2):
                b = half
                for kc in range(8):
                    P.add("pe", lambda e, kc=kc, b=b, half=half: e.matmul(
                        bk(b)[0:ntok, :], lhsT=mT[:, kc, 0:ntok], rhs=Wout[:, kc, half * 512:(half + 1) * 512],
                        start=(kc == 0), stop=(kc == 7)),
                        reads=[("mT", kc), ("Wout", kc, 0)], writes=[BK(b)])
            if samp:
                dbg_dump("mT", mT[:, :, 0:16], [128, 8, 16], BF16, [("mT", c) for c in range(8)])
                dbg_dump("OT", OT[:, :, 0:16], [128, 8, 16], BF16, ["OT"])
            def fin():
                post_norm_residual((0, 1), sl, ntok, "gbc", xin[sl], ("xin", sl))
                row = T["x1row"]
                P.add("sp", lambda e: e.dma_start(out=x1s_d[row:row + ntok, :], in_=xin[sl][0:ntok, :]),
                      reads=[("xin", sl)], writes=[("x1s", row)], dsem=("xst", sl))
            return fin

        def load_caches():
            for kt in range(4):
                sl = cnt["xin"] % 4
                cnt["xin"] += 1
                P.add("sp", lambda e, kt=kt, sl=sl: e.dma_start(out=xin[sl][:, 0:512],
                                                                in_=cbk_d[kt * 128:(kt + 1) * 128, :]),
                      writes=[("xin", sl)], dsem=("xin", sl))
                P.add("sp", lambda e, kt=kt, sl=sl: e.dma_start(out=xin[sl][:, 512:1024],
                                                                in_=cbv_d[kt * 128:(kt + 1) * 128, :]),
                      writes=[("xin", sl)], dsem=("xin", sl))
                xb = xn[kt % 2]
                P.add("dve", copy_op("dve", xb[:, 0:512], xin[sl][:, 0:512]), reads=[("xin", sl)],
                      writes=[("xn", kt % 2)])
                P.add("act", copy_op("act", Vr[:, kt, 2:10, 0:64],
                                     xin[sl][:, 512:1024].rearrange("p (h d) -> p h d", h=8)),
                      reads=[("xin", sl)], writes=[("Vb", kt)])
                P.add("pool", lambda e, kt=kt: e.memset(Vr[:, kt, :, 64], 1.0), writes=[("V1", kt)])
                b = kt % 2
                for c in range(4):
                    P.add("pe", lambda e, c=c, b=b, xb=xb: e.transpose(out=bkb(b)[:, c * 128:(c + 1) * 128],
                                                                       in_=xb[:, c * 128:(c + 1) * 128],
                                                                       identity=ident[:, :]),
                          reads=[("xn", kt % 2), "ident"], writes=[BK(b)])
                P.add("dve", copy_op("dve", KTb[:, :, kt * 128:(kt + 1) * 128],
                                     bkb(b)[:, 0:512].rearrange("p (c k) -> p c k", c=4)),
                      reads=[BK(b)], writes=[("KTb", kt, c) for c in range(4)])
            sl = cnt["xin"] % 4
            cnt["xin"] += 1
            P.add("sp", lambda e: e.dma_start(out=xin[sl][:, 0:128], in_=cak_d), writes=[("xin", sl)],
                  dsem=("xin", sl))
            P.add("sp", lambda e: e.dma_start(out=xin[sl][:, 128:256], in_=cav_d), writes=[("xin", sl)],
                  dsem=("xin", sl))
            xb = xn[0]
            P.add("dve", copy_op("dve", xb[:, 0:128], xin[sl][:, 0:128]), reads=[("xin", sl)], writes=[("xn", 0)])
            P.add("act", copy_op("act", Vr[:, 3, 0:2, 0:64], xin[sl][:, 128:256].rearrange("p (h d) -> p h d", h=2)),
                  reads=[("xin", sl)], writes=[("Va", 3)])
            P.add("pe", lambda e: e.transpose(out=bkb(0)[:, 0:128], in_=xb[:, 0:128], identity=ident[:, :]),
                  reads=[("xn", 0), "ident"], writes=[BK(0)])
            P.add("dve", copy_op("dve", KTa[:, 3 * 128:4 * 128], bkb(0)[:, 0:128]), reads=[BK(0)],
                  writes=[("KTa", 3)])

        tiles = []
        samp = dict(t=-1, kind="full", ntok=16, samp=True, slot=SAMP_SLOT, kvout=True,
                    kvdst=[(sak_d, 0, 128), (sav_d, 128, 128), (sbk_d, 256, 512), (sbv_d, 768, 512)],
                    a_tiles=[(3, 128, 0), (SAMP_SLOT, 16, 1)],
                    b_tiles=[(0, 128, 0), (1, 128, 1), (2, 128, 2), (3, 128, 3), (SAMP_SLOT, 16, 4)],
                    TA=BAs, TB=BBs, TAk="BAs", TBk="BBs", x1row=NTILES * 128)
        tiles.append(samp)
        for t in range(NTILES):
            kind = "kv" if t < T_HALO_KV else "full"
            T = dict(t=t, kind=kind, ntok=128, samp=False, slot=t % RT, kvout=(t >= NTILES - 4), x1row=t * 128)
            if T["kvout"]:
                r = (t - (NTILES - 4)) * 128
                dst = [(obk_d[r:r + 128, :], 256, 512), (obv_d[r:r + 128, :], 768, 512)]
                if t == NTILES - 1:
                    dst += [(oak_d, 0, 128), (oav_d, 128, 128)]
                T["kvdst"] = dst
            if kind == "full":
                T["a_tiles"] = [((t - 1) % RT, 128, 0), (t % RT, 128, 1)]
                T["b_tiles"] = [((t - 4 + i) % RT, 128, i) for i in range(5)]
                T.update(TA=BA, TB=BB, TAk="BA", TBk="BB")
            tiles.append(T)
        for i, T in enumerate(tiles):
            T["hi"] = i % 2

        if ntiles_p1 is not None:
            tiles = tiles[:ntiles_p1]
        if stage >= 1:
            load_caches()
        if stage >= 2:
            def front_proj(Tn, between=None):
                p1_front_b(Tn)
                if between is not None:
                    between()
                p1_proj(Tn)

            p1_front(tiles[0])
            p1_front_a(tiles[0])
            front_proj(tiles[0])
            if len(tiles) > 1:
                p1_front(tiles[1])
                p1_front_a(tiles[1])
            pend1 = [None]
            for i, T in enumerate(tiles):
                def early(i=i):
                    if i + 2 < len(tiles):
                        p1_front(tiles[i + 2])

                def mid(i=i):
                    def flush():
                        if pend1[0] is not None:
                            pend1[0]()
                            pend1[0] = None
                    if i + 1 < len(tiles):
                        front_proj(tiles[i + 1], flush)
                    else:
                        flush()

                def late(i=i):
                    if i + 2 < len(tiles):
                        p1_front_a(tiles[i + 2])

                r = p1_back(T, mid, early, late)
                if r is not None:
                    pend1[0] = r
            if pend1[0] is not None:
                pend1[0]()
        if stage < 3:
            mo = int(os.environ.get("KMAXOPS", "0"))
            if mo:
                for i, op in enumerate(P.ops[:mo]):
                    print("OP", i, op.eng, op.reads, op.writes, op.dsem)
                P.ops = P.ops[:mo]
            P.finalize()
            P.emit(nc)
            return nc

        for op in reversed(P.ops):
            if op.eng == "pe":
                op.writes.append(("bar", "pe"))
                break
        for i, eng in enumerate(("act", "dve", "pool")):
            if eng == "act":
                P.add(eng, lambda e, i=i: e.activation(out=bars[:, i:i + 1], in_=epst[:, 0:1], func=AF.Copy),
                      writes=[("bar", eng)])
            else:
                P.add(eng, lambda e, i=i: e.memset(bars[:, i:i + 1], 0.0), writes=[("bar", eng)])
        BARS = [("bar", e) for e in ("pe", "act", "dve", "pool")]
        for i, eng in enumerate(("act", "dve", "pool")):
            if eng == "act":
                P.add(eng, lambda e, i=i: e.activation(out=bars[:, 4 + i:5 + i], in_=epst[:, 0:1], func=AF.Copy),
                      reads=BARS, writes=[("bar2", eng)])
            else:
                P.add(eng, lambda e, i=i: e.memset(bars[:, 4 + i:5 + i], 0.0), reads=BARS, writes=[("bar2", eng)])

        P.add("sp", lambda e: e.dma_start(out=gbc[:, :], in_=gqf_d.partition_broadcast(128)),
              reads=BARS, writes=["gbc"], dsem="gbc2")
        prep_weight("Wup", wup_d, 8, 2 * DFF, Wup, scale_col=8, extra_reads=BARS)
        prep_weight("Wdn", wdn_d, 24, D, Wdn, extra_reads=BARS)

        def p2_front(G):
            hi = G["hi"]
            for i, T in enumerate(G["tiles"]):
                ntok = T["ntok"]
                sl = cnt["xin"] % 4
                cnt["xin"] += 1
                T["sl"] = sl
                row = T["x1row"]
                P.add("sp", lambda e, sl=sl, row=row, ntok=ntok: e.dma_start(out=xin[sl][0:ntok, :],
                                                                            in_=x1s_d[row:row + ntok, :]),
                      reads=[("x1s", row)], writes=[("xin", sl)], dsem=("xin", sl))
                cnt["xn2"] = cnt.get("xn2", 0) + 1
                T["xi"] = cnt["xn2"] % 2

        def p2_front_a(G, step=None):
            tl_ = G["tiles"]
            ntok = tl_[0]["ntok"]
            nt_ = len(tl_)
            for st_ in ([0, 1, 2, 3, 4] if step is None else [step]):
                if st_ in (0, 1) and st_ < nt_:
                    T = tl_[st_]
                    xi, sl = T["xi"], T["sl"]
                    P.add("act", lambda e, xi=xi, sl=sl, c=st_: e.activation(
                        out=xn[xi][0:ntok, :], in_=xin[sl][0:ntok, :], func=AF.Square,
                        accum_out=ss[0:ntok, c:c + 1]),
                        reads=[("xin", sl)], writes=[("xn", xi), ("ss", st_)])
                elif st_ == 2:
                    P.add("act", lambda e: e.activation(out=sd[0:ntok, 0:nt_], in_=ss[0:ntok, 0:nt_], func=AF.Sqrt,
                                                        scale=1.0 / D, bias=epst[0:ntok, :]),
                          reads=[("ss", c) for c in range(nt_)] + ["epst"], writes=[("sd", c) for c in range(nt_)])
                    P.add("dve", lambda e: e.reciprocal(out=rstd[0:ntok, 0:nt_], in_=sd[0:ntok, 0:nt_]),
                          reads=[("sd", c) for c in range(nt_)], writes=[("rstd", c) for c in range(nt_)])
                elif st_ in (3, 4) and st_ - 3 < nt_:
                    T = tl_[st_ - 3]
                    xi, sl, c = T["xi"], T["sl"], st_ - 3
                    P.add("act", lambda e, xi=xi, sl=sl, c=c: e.activation(
                        out=xn[xi][0:ntok, :], in_=xin[sl][0:ntok, :], func=AF.Copy, scale=rstd[0:ntok, c:c + 1]),
                        reads=[("xin", sl), ("rstd", c)], writes=[("xn", xi)])

        def p2_front_b(G):
            hi = G["hi"]
            for i, T in enumerate(G["tiles"]):
                xi = T["xi"]
                norm_b(T["ntok"], h2T[hi][:, :, i * 128:(i + 1) * 128], ("h2T", hi, i), xn[xi], ("xn", xi))

        GR = int(os.environ.get("KGR", "2"))

        def p2_back(G, mid=None, pre=None, early=None):
            tl = G["tiles"]
            pre = pre if pre is not None else []
            hi = G["hi"]
            samp = tl[0]["samp"]
            N = sum(T["ntok"] for T in tl)
            halo = (not samp) and tl[0]["t"] == T_HALO
            H = hists if samp else hist
            hk = "hists" if samp else "hist"
            hkeys = [("h2T", hi, i) for i in range(len(tl))]

            def tail(j):
                cbf = cbuf[j % 2]
                bg = 4 + j % GR
                P.add("act", lambda e: e.activation(out=cbf[:, 0:N], in_=cbf[:, 0:N], func=AF.Gelu_apprx_tanh),
                      reads=[("cb", j % 2)], writes=[("cb", j % 2)])
                P.add("dve", lambda e: e.tensor_tensor(out=actT[:, j, 0:N], in0=cbf[:, 0:N],
                                                       in1=bk(bg)[:, 0:N], op=ALU.mult),
                      reads=[("cb", j % 2), BK(bg)], writes=[("actT", j)])

            for j in range(24):
                ub, cbf = ubuf[j % 2], cbuf[j % 2]
                bu, bg = 2 + j % 2, 4 + j % GR
                if pre and j in (0, 1):
                    pre.pop(0)()
                if early is not None and j == 3:
                    early[0]()
                if early is not None and j in (6, 8, 10, 12, 14):
                    early[1]((j - 6) // 2)
                for kc in range(8):
                    P.add("pe", lambda e, kc=kc, j=j, bu=bu: e.matmul(
                        bk(bu)[:, 0:N], lhsT=Wup[:, kc, j * 128:(j + 1) * 128], rhs=h2T[hi][:, kc, 0:N],
                        start=(kc == 0), stop=(kc == 7)),
                        reads=hkeys + wkeys("Wup", kc, j * 128, (j + 1) * 128), writes=[BK(bu)])
                if not halo:
                    for kc in range(8):
                        P.add("pe", lambda e, kc=kc, j=j, bg=bg: e.matmul(
                            bk(bg)[:, 0:N], lhsT=Wup[:, kc, DFF + j * 128:DFF + (j + 1) * 128],
                            rhs=h2T[hi][:, kc, 0:N], start=(kc == 0), stop=(kc == 7)),
                            reads=hkeys + wkeys("Wup", kc, DFF + j * 128, DFF + (j + 1) * 128),
                            writes=[BK(bg)])
                P.add("act", copy_op("act", ub[:, 2:2 + N], bk(bu)[:, 0:N]), reads=[BK(bu)],
                      writes=[("ub", j % 2)])
                P.add("pool", lambda e, j=j, ub=ub: e.tensor_copy(out=ub[:, 0:2], in_=H[:, 2 * j:2 * j + 2]),
                      reads=[(hk, j)], writes=[("ubh", j % 2)])
                if halo:
                    P.add("pool", lambda e, j=j, ub=ub: e.tensor_scalar(out=H[:, 2 * j:2 * j + 2],
                                                                        in0=ub[:, N:N + 2], scalar1=hv[:, 0:1],
                                                                        scalar2=None, op0=ALU.mult),
                          reads=[("ub", j % 2), ("ubh", j % 2), "hv"], writes=[(hk, j)])
                    continue
                P.add("pool", lambda e, j=j, ub=ub: e.tensor_copy(out=H[:, 2 * j:2 * j + 2], in_=ub[:, N:N + 2]),
                      reads=[("ub", j % 2), ("ubh", j % 2)], writes=[(hk, j)])
                if j >= 1:
                    tail(j - 1)
                P.add("act", lambda e, j=j, ub=ub, cbf=cbf: e.activation(
                    out=cbf[:, 0:N], in_=ub[:, 2:2 + N], func=AF.Identity,
                    scale=cw[:, 48 + j:49 + j], bias=cbias[:, j:j + 1]),
                    reads=[("ub", j % 2), "cw", "cbias"], writes=[("cb", j % 2)])
                for tap in (1, 0):
                    P.add("dve", lambda e, j=j, ub=ub, cbf=cbf, tap=tap: e.scalar_tensor_tensor(
                        out=cbf[:, 0:N], in0=ub[:, tap:tap + N], scalar=cw[:, tap * 24 + j:tap * 24 + j + 1],
                        in1=cbf[:, 0:N], op0=ALU.mult, op1=ALU.add),
                        reads=[("ub", j % 2), ("ubh", j % 2), ("cb", j % 2), "cw"], writes=[("cb", j % 2)])
            while pre:
                pre.pop(0)()
            if halo:
                if mid is not None:
                    mid()
                return []
            tail(23)
            if mid is not None:
                mid()
            deferred = []
            for i, T in enumerate(tl):
                ntok, sl, t = T["ntok"], T["sl"], T["t"]
                bp = (0, 1) if i == 0 else (6, 7)
                for half in range(2):
                    b = bp[half]
                    for j in range(24):
                        P.add("pe", lambda e, j=j, b=b, half=half, i=i, ntok=ntok: e.matmul(
                            bk(b)[0:ntok, :], lhsT=actT[:, j, i * 128:i * 128 + ntok],
                            rhs=Wdn[:, j, half * 512:(half + 1) * 512], start=(j == 0), stop=(j == 23)),
                            reads=[("actT", j), ("Wdn", j, 0)], writes=[BK(b)])
                def fin(bp=bp, sl=sl, ntok=ntok, i=i, t=t):
                    post_norm_residual(bp, sl, ntok, "gbc", xin[sl], ("xin", sl), cb=4 + 4 * i)
                    dst = ys_d if samp else yp_d[(t - T0) * 128:(t - T0 + 1) * 128, :]
                    P.add("sp", lambda e: e.dma_start(out=dst, in_=xin[sl][0:ntok, :]),
                          reads=[("xin", sl)], dsem=("xst", sl))
                deferred.append(fin)
            if samp or tl[-1]["t"] == NTILES - 1:
                od = scv_d if samp else ocv_d
                for c in range(24):
                    P.add("sp", lambda e, c=c: e.dma_start(out=od[:, c * 128:(c + 1) * 128].rearrange("t p -> p t"),
                                                           in_=H[:, 2 * c:2 * c + 2]),
                          reads=[(hk, c)], dsem="cvout")
            return deferred

        full = [T for T in tiles if T["kind"] == "full"]
        pt = [T for T in full if not T["samp"]]
        groups = [dict(tiles=[pt[0]])]
        for i in range(1, len(pt), 2):
            groups.append(dict(tiles=pt[i:i + 2]))
        groups.append(dict(tiles=[T for T in full if T["samp"]]))
        for i, G in enumerate(groups):
            G["hi"] = i % 2
        p2_front(groups[0])
        p2_front_a(groups[0])
        p2_front_b(groups[0])
        pend = []
        for i, G in enumerate(groups):
            nxt = (lambda Gn=groups[i + 1]: p2_front_b(Gn)) if i + 1 < len(groups) else None
            early = ((lambda Gn=groups[i + 1]: p2_front(Gn)), (lambda st_, Gn=groups[i + 1]: p2_front_a(Gn, st_))) \
                if i + 1 < len(groups) else None
            pend = p2_back(G, nxt, pend, early)
        for f_ in pend:
            f_()

        P.finalize()
        P.emit(nc)
    return nc


def _t5_bucket_np(n):
    import jax
    import jax.numpy as jnp
    try:
        cpu = jax.devices("cpu")[0]
    except Exception:
        cpu = None
    ctx = jax.default_device(cpu) if cpu is not None else contextlib.nullcontext()
    with ctx:
        n = jnp.asarray(n, dtype=jnp.int32)
        half = 16
        max_exact = 8
        ret = jnp.where(n < 0, half, 0)
        a = jnp.abs(n)
        af = jnp.maximum(a, 1).astype(jnp.float32)
        large = max_exact + (jnp.log(af / max_exact) / math.log(128 / max_exact)
                             * (half - max_exact)).astype(jnp.int32)
        large = jnp.minimum(large, half - 1)
        out = ret + jnp.where(a < max_exact, a, large)
        return np.asarray(out)


def _bias_tables(t5_table, rel_b):
    q = np.arange(128)
    kk = np.arange(256)
    rel = (128 + q)[None, :] - kk[:, None]
    cq = 2 + q // 64
    ck = kk // 64
    vis = ((cq[None, :] - ck[:, None]) >= 0) & ((cq[None, :] - ck[:, None]) <= 2)
    idx = _t5_bucket_np(rel)
    A = t5_table[:, idx]
    A = np.where(vis[None], A, np.float32(NEG)).astype(np.float32)
    A = A.reshape(8, 2, 128, 128).transpose(2, 1, 0, 3)
    kk = np.arange(640)
    rel = (512 + q)[None, :] - kk[:, None]
    cq = 8 + q // 64
    ck = kk // 64
    vis = ((cq[None, :] - ck[:, None]) >= 0) & ((cq[None, :] - ck[:, None]) <= 8)
    idx = np.clip(rel, -128, 128) + 128
    B = rel_b[:, idx]
    B = np.where(vis[None], B, np.float32(NEG)).astype(np.float32)
    perm = [0, 2, 4, 6, 1, 3, 5, 7]
    B = B.reshape(8, 5, 128, 128)[perm].transpose(2, 1, 0, 3)
    qs = np.arange(16)
    kk = np.arange(256)
    rel = qs[None, :] - kk[:, None] + 128
    As = t5_table[:, _t5_bucket_np(rel)].astype(np.float32)
    As = As.reshape(8, 2, 128, 16).transpose(2, 1, 0, 3)
    kk = np.arange(640)
    rel = qs[None, :] - kk[:, None] + 512
    Bs = rel_b[:, np.clip(rel, -128, 128) + 128].astype(np.float32)
    Bs = Bs.reshape(8, 5, 128, 16)[perm].transpose(2, 1, 0, 3)
    c = np.ascontiguousarray
    return (c(A).reshape(128, -1), c(B).reshape(128, -1), c(As).reshape(128, -1), c(Bs).reshape(128, -1))


_NC_CACHE = {}


def kernel(x_prompt, x_sample, cache_a_k, cache_a_v, cache_b_k, cache_b_v, state_conv,
           w_in, w_oa, w_ob, w_out, sink_a, t5_table, rel_table_b,
           g_pre_mix, g_post_mix, g_pre_ffn, g_post_ffn, w_upg, conv_w, conv_b, w_down):
    f = lambda a: np.ascontiguousarray(np.asarray(a, dtype=np.float32))
    x_prompt, x_sample = f(x_prompt), f(x_sample)
    bA, bB, bAs, bBs = _bias_tables(f(t5_table), f(rel_table_b)[0])
    shared = {
        "w_in": f(w_in)[0], "w_oa": f(w_oa)[0], "w_ob": f(w_ob)[0], "w_out": f(w_out)[0],
        "sink": f(sink_a).reshape(1, 8), "g_pre_mix": f(g_pre_mix).reshape(1, D),
        "g_post_mix": f(g_post_mix).reshape(1, D), "g_pre_ffn": f(g_pre_ffn).reshape(1, D),
        "g_post_ffn": f(g_post_ffn).reshape(1, D), "w_upg": f(w_upg)[0], "conv_w": f(conv_w)[0],
        "conv_b": f(conv_b).reshape(1, DFF), "w_down": f(w_down)[0],
        "biasA": bA, "biasB": bB, "biasAs": bAs, "biasBs": bBs,
        "ident": np.eye(128, dtype=np.float32),
    }
    halo = T0 * 128
    in_maps = []
    for c in range(8):
        b, half = c // 2, c % 2
        start = half * HALF
        xc = np.zeros((NTILES * 128, D), np.float32)
        if half == 0:
            xc[halo:] = x_prompt[b, 0:HALF]
        else:
            xc[:] = x_prompt[b, start - halo:start + HALF]
        m = dict(shared)
        m.update({
            "xc": xc, "xs": x_sample[c],
            "cak": f(cache_a_k)[0, c].reshape(128, 128), "cav": f(cache_a_v)[0, c].reshape(128, 128),
            "cbk": f(cache_b_k)[0, c].reshape(512, 512), "cbv": f(cache_b_v)[0, c].reshape(512, 512),
            "sconv": f(state_conv)[0, c],
            "hv": np.full((128, 1), float(half), np.float32),
        })
        in_maps.append(m)
    if "nc" not in _NC_CACHE:
        _NC_CACHE["nc"] = build_nc()
    nc = _NC_CACHE["nc"]
    res = run_bass_kernel_spmd(nc, in_maps, core_ids=list(range(8)))
    R = res.results
    y_prompt = np.stack([np.concatenate([R[2 * b]["yp"], R[2 * b + 1]["yp"]], axis=0) for b in range(4)], 0)
    y_sample = np.stack([R[c]["ys"] for c in range(8)], 0)
    odd = [R[2 * b + 1] for b in range(4)]
    nak = np.stack([r["oak"].reshape(128, 2, 64) for r in odd], 0)[None]
    nav = np.stack([r["oav"].reshape(128, 2, 64) for r in odd], 0)[None]
    nbk = np.stack([r["obk"].reshape(512, 8, 64) for r in odd], 0)[None]
    nbv = np.stack([r["obv"].reshape(512, 8, 64) for r in odd], 0)[None]
    ncv = np.stack([r["ocv"] for r in odd], 0)[None]
    sak = np.stack([R[c]["sak"].reshape(16, 2, 64) for c in range(8)], 0)[None]
    sav = np.stack([R[c]["sav"].reshape(16, 2, 64) for c in range(8)], 0)[None]
    sbk = np.stack([R[c]["sbk"].reshape(16, 8, 64) for c in range(8)], 0)[None]
    sbv = np.stack([R[c]["sbv"].reshape(16, 8, 64) for c in range(8)], 0)[None]
    scv = np.stack([R[c]["scv"] for c in range(8)], 0)[None]
    out = (y_prompt, y_sample, nak, nav, nbk, nbv, ncv, sak, sav, sbk, sbv, scv)
    return tuple(np.ascontiguousarray(o.astype(np.float32)) for o in out)
```

```python
import math
import contextlib
import numpy as np
import concourse.bass as bass
import concourse.mybir as mybir
from concourse.bass_utils import run_bass_kernel_spmd

F32 = mybir.dt.float32
BF16 = mybir.dt.bfloat16
AF = mybir.ActivationFunctionType
ALU = mybir.AluOpType

D = 1024
DFF = 3072
INW = 4352
SEQ = 8192
HALF = 4096
NT_MAIN = 32
T_HALO_KV = 4
T_HALO = 4
T0 = 5
NTILES = T0 + NT_MAIN
RT = 8
EPS = 1e-6
NEG = -30000.0
SAMP_SLOT = 4
C_QA, C_KA, C_VA, C_QB, C_KB, C_VB, C_GA, C_GB = 0, 512, 640, 768, 1280, 1792, 2304, 3328


class _Op:
    __slots__ = ("eng", "fn", "reads", "writes", "dsem", "deps", "mark", "rank", "waits", "clock")

    def __init__(self, eng, fn, reads, writes, dsem):
        self.eng = eng
        self.fn = fn
        self.reads = reads
        self.writes = writes
        self.dsem = dsem
        self.deps = None
        self.mark = False
        self.rank = 0
        self.waits = None
        self.clock = None


class Prog:
    ENGS = ("pe", "act", "dve", "pool", "sp")

    def __init__(self):
        self.ops = []
        self.group_final = set()

    def add(self, eng, fn, reads=(), writes=(), dsem=None):
        self.ops.append(_Op(eng, fn, tuple(reads), list(writes), dsem))

    def finalize(self):
        last_w = {}
        readers = {}
        ops = self.ops
        for i, op in enumerate(ops):
            deps = {}
            for k in op.reads:
                w = last_w.get(k)
                if w is not None:
                    deps[w] = "raw"
            for k in op.writes:
                w = last_w.get(k)
                if w is not None and w not in deps:
                    deps[w] = "waw"
                last_by_eng = {}
                for r in readers.get(k, ()):
                    if r == i:
                        continue
                    if ops[r].dsem is not None:
                        if r not in deps:
                            deps[r] = "war"
                    elif last_by_eng.get(ops[r].eng, -1) < r:
                        last_by_eng[ops[r].eng] = r
                for r in last_by_eng.values():
                    if r not in deps:
                        deps[r] = "war"
            for k in op.reads:
                readers.setdefault(k, []).append(i)
            for k in op.writes:
                last_w[k] = i
                readers[k] = []
            need = []
            for d, kind in deps.items():
                dop = ops[d]
                if dop.dsem is not None and dop.dsem == op.dsem and op.dsem in self.group_final:
                    continue
                if dop.dsem is not None or op.dsem is not None:
                    need.append(d)
                elif dop.eng != op.eng:
                    need.append(d)
                elif kind == "raw" and op.eng != "pe":
                    need.append(d)
            op.deps = need
            for d in need:
                ops[d].mark = True
        cnt = {e: 0 for e in self.ENGS}
        dcnt = {}
        for op in ops:
            if op.dsem is not None:
                dcnt[op.dsem] = dcnt.get(op.dsem, 0) + 16
                op.rank = dcnt[op.dsem]
            elif op.mark:
                cnt[op.eng] += 1
                op.rank = cnt[op.eng]
        for op in ops:
            if op.dsem is not None and op.dsem in self.group_final:
                op.rank = dcnt[op.dsem]
        clock = {e: {} for e in self.ENGS}
        nw = 0
        for op in ops:
            ck = clock[op.eng]
            waits = {}
            for d in sorted(op.deps, key=lambda d: -ops[d].rank):
                dop = ops[d]
                sk = ("d", dop.dsem) if dop.dsem is not None else ("e", dop.eng)
                if ck.get(sk, 0) >= dop.rank:
                    continue
                if waits.get(sk, 0) < dop.rank:
                    waits[sk] = dop.rank
                ck[sk] = dop.rank
                if dop.clock is not None:
                    for k2, v2 in dop.clock.items():
                        if ck.get(k2, 0) < v2:
                            ck[k2] = v2
            op.waits = waits
            nw += len(waits)
            if op.dsem is not None:
                op.clock = dict(ck)
            elif op.mark:
                c2 = dict(ck)
                c2[("e", op.eng)] = op.rank
                op.clock = c2
        self.n_waits = nw
        self.counts = cnt
        self.dcounts = dcnt
        import os
        pr = os.environ.get("KPRINT")
        if pr:
            a, b = [int(x) for x in pr.split(":")]
            for i in range(a, min(b, len(ops))):
                op = ops[i]
                print("W", i, op.eng, "rank", op.rank if (op.mark or op.dsem) else "-", "deps", [(d, ops[d].eng, ops[d].rank) for d in op.deps],
                      "waits", op.waits, "r", op.reads, "w", op.writes)

    def emit(self, nc):
        per_eng = {e: [] for e in self.ENGS}
        for op in self.ops:
            per_eng[op.eng].append(op)
        EP = 16384
        with contextlib.ExitStack() as st:
            esem = {e: [st.enter_context(nc.semaphore("es_%s_%d" % (e, j)))
                        for j in range((self.counts[e] + EP - 1) // EP + 1)] for e in self.ENGS}
            dsem = {k: st.enter_context(nc.semaphore("ds_%d" % i)) for i, k in enumerate(self.dcounts)}
            block = st.enter_context(nc.Block())

            def sem_of(sk, v):
                if sk[0] == "d":
                    return dsem[sk[1]], v
                return esem[sk[1]][(v - 1) // EP], (v - 1) % EP + 1

            def run(engname, eng):
                for op in per_eng[engname]:
                    for sk, v in op.waits.items():
                        eng.wait_ge(*sem_of(sk, v))
                    ins = op.fn(eng)
                    if op.dsem is not None:
                        ins.then_inc(dsem[op.dsem], 16)
                    elif op.mark:
                        ins.then_inc(esem[op.eng][(op.rank - 1) // EP], 1)
                if engname == "sp":
                    for k, v in self.dcounts.items():
                        eng.wait_ge(dsem[k], v)

            @block.tensor
            def _(e):
                run("pe", e)

            @block.scalar
            def _(e):
                run("act", e)

            @block.vector
            def _(e):
                run("dve", e)

            @block.gpsimd
            def _(e):
                run("pool", e)

            @block.sync
            def _(e):
                run("sp", e)


def build_nc(stage=9, ntiles_p1=None):
    nc = bass.Bass("TRN2", target_bir_lowering=False)

    def din(name, shape):
        return nc.dram_tensor(name, list(shape), F32, kind="ExternalInput").ap()

    def dout(name, shape):
        return nc.dram_tensor(name, list(shape), F32, kind="ExternalOutput").ap()

    xc_d = din("xc", [NTILES * 128, D])
    xs_d = din("xs", [16, D])
    cak_d = din("cak", [128, 128])
    cav_d = din("cav", [128, 128])
    cbk_d = din("cbk", [512, 512])
    cbv_d = din("cbv", [512, 512])
    sconv_d = din("sconv", [2, DFF])
    win_d = din("w_in", [D, INW])
    woa_d = din("w_oa", [512, D])
    wob_d = din("w_ob", [512, D])
    wout_d = din("w_out", [D, D])
    sink_d = din("sink", [1, 8])
    gpm_d = din("g_pre_mix", [1, D])
    gqm_d = din("g_post_mix", [1, D])
    gpf_d = din("g_pre_ffn", [1, D])
    gqf_d = din("g_post_ffn", [1, D])
    wup_d = din("w_upg", [D, 2 * DFF])
    cw_d = din("conv_w", [3, DFF])
    cb_d = din("conv_b", [1, DFF])
    wdn_d = din("w_down", [DFF, D])
    ba_d = din("biasA", [128, 2 * 8 * 128])
    bb_d = din("biasB", [128, 5 * 8 * 128])
    bas_d = din("biasAs", [128, 2 * 8 * 16])
    bbs_d = din("biasBs", [128, 5 * 8 * 16])
    ident_d = din("ident", [128, 128])
    hv_d = din("hv", [128, 1])

    yp_d = dout("yp", [HALF, D])
    ys_d = dout("ys", [16, D])
    oak_d = dout("oak", [128, 128])
    oav_d = dout("oav", [128, 128])
    obk_d = dout("obk", [512, 512])
    obv_d = dout("obv", [512, 512])
    ocv_d = dout("ocv", [2, DFF])
    sak_d = dout("sak", [16, 128])
    sav_d = dout("sav", [16, 128])
    sbk_d = dout("sbk", [16, 512])
    sbv_d = dout("sbv", [16, 512])
    scv_d = dout("scv", [2, DFF])

    import os as _os
    if _os.environ.get("KDBG"):
        x1s_d = nc.dram_tensor("x1s", [(NTILES + 1) * 128, D], F32, kind="ExternalOutput").ap()
    else:
        x1s_d = nc.dram_tensor("x1s", [(NTILES + 1) * 128, D], F32).ap()

    P = Prog()
    P.group_final.add("setup")
    KDBG = bool(_os.environ.get("KDBG"))

    def dbg_dump(name, ap, shape, dt, reads):
        if not KDBG:
            return
        dd = nc.dram_tensor("dbg_" + name, list(shape), dt, kind="ExternalOutput").ap()
        P.add("sp", lambda e: e.dma_start(out=dd, in_=ap), reads=reads, dsem="dbg")

    with contextlib.ExitStack() as st:
        def sb(name, shape, dt):
            return st.enter_context(nc.sbuf_tensor("s_" + name, list(shape), dt))

        st.enter_context(nc.allow_non_contiguous_dma(reason="small constant / transposing loads"))

        WAR = sb("warena", [128, 73728], BF16)
        PAR = sb("parena", [128, 13440], BF16)
        FAR = sb("farena", [128, 1040], F32)
        xin = [sb("xin%d" % i, [128, D], F32) for i in range(4)]
        xn = [sb("xn%d" % i, [128, D], BF16) for i in range(2)]
        tmp = sb("tmp", [128, 1280], F32)
        gbc = sb("gbc", [128, D], F32)
        identf = sb("identf", [128, 128], F32)
        ident = sb("ident", [128, 128], BF16)
        gcol = sb("gcol", [128, 16], F32)
        cw = sb("cw", [128, 3 * 24], F32)
        cbias = sb("cbias", [128, 24], F32)
        hv = sb("hv", [128, 1], F32)
        hv10 = sb("hv10", [128, 10], BF16)
        epst = sb("epst", [128, 1], F32)
        sinkt = sb("sinkt", [128, 8], F32)
        exps = sb("exps", [128, 8], F32)
        ss = sb("ss", [128, 16], F32)
        sd = sb("sd", [128, 16], F32)
        rstd = sb("rstd", [128, 16], F32)
        junk = sb("junk", [128, 512], BF16)
        den = sb("den", [128, 8], F32)
        rden = sb("rden", [128, 8], F32)
        hist = sb("hist", [128, 48], F32)
        hists = sb("hists", [128, 48], F32)
        bars = sb("bars", [128, 8], F32)

        Win = WAR[:, 0:34816].rearrange("p (k c) -> p k c", k=8)
        Woa = WAR[:, 34816:38912].rearrange("p (k c) -> p k c", k=4)
        Wob = WAR[:, 38912:43008].rearrange("p (k c) -> p k c", k=4)
        Wout = WAR[:, 43008:51200].rearrange("p (k c) -> p k c", k=8)
        o = 51200
        BA = WAR[:, o:o + 4096].bitcast(F32).rearrange("p (t h q) -> p t h q", t=2, h=8)
        o += 4096
        BB = WAR[:, o:o + 10240].bitcast(F32).rearrange("p (t h q) -> p t h q", t=5, h=8)
        o += 10240
        BAs = WAR[:, o:o + 512].bitcast(F32).rearrange("p (t h q) -> p t h q", t=2, h=8)
        o += 512
        BBs = WAR[:, o:o + 1280].bitcast(F32).rearrange("p (t h q) -> p t h q", t=5, h=8)
        o += 1280
        KTa = WAR[:, o:o + RT * 128]
        o += RT * 128
        KTb = WAR[:, o:o + 4 * RT * 128].rearrange("p (c k) -> p c k", c=4)
        o += 4 * RT * 128
        assert o <= 73728
        Wup = WAR[:, 0:49152].rearrange("p (k c) -> p k c", k=8)
        Wdn = WAR[:, 49152:73728].rearrange("p (k c) -> p k c", k=24)

        hT = [PAR[:, i * 1024:(i + 1) * 1024].rearrange("p (k t) -> p k t", k=8) for i in range(2)]
        QT = PAR[:, 2048:3072].rearrange("p (k t) -> p k t", k=8)
        OT = PAR[:, 3072:4096].rearrange("p (k t) -> p k t", k=8)
        mT = PAR[:, 4096:5120].rearrange("p (k t) -> p k t", k=8)
        Vr = PAR[:, 5120:5120 + RT * 650].rearrange("p (s h d) -> p s h d", s=RT, h=10)
        o = 5120 + RT * 650
        pT = [PAR[:, o + i * 512:o + (i + 1) * 512] for i in range(4)]
        o += 2048
        Otok = PAR[:, o:o + 1024]
        o += 1024
        assert o <= 13440
        gtmp = [FAR[:, i * 256:(i + 1) * 256] for i in range(2)]
        h2T = [PAR[:, i * 2048:(i + 1) * 2048].rearrange("p (k t) -> p k t", k=8) for i in range(2)]
        actT = PAR[:, 4096:4096 + 24 * 256].rearrange("p (k t) -> p k t", k=24)
        ubuf = [FAR[:, i * 258:(i + 1) * 258] for i in range(2)]
        cbuf = [FAR[:, 516 + i * 256:516 + (i + 1) * 256] for i in range(2)]

        banks = [st.enter_context(nc.psum_tensor("bank%d" % i, [128, 512], F32)) for i in range(8)]

        def bk(b):
            return banks[b][:, :]

        def bkb(b):
            return banks[b][:, :].bitcast(BF16)

        def BK(b):
            return ("bank", b)

        cnt = {"xin": 0, "cast": 0, "ev": 0}

        def evac_eng():
            cnt["ev"] += 1
            return "act" if cnt["ev"] % 2 else "dve"

        def copy_op(eng, out, in_, scale=None):
            if eng == "act":
                if scale is None:
                    return lambda e: e.activation(out=out, in_=in_, func=AF.Copy)
                return lambda e: e.activation(out=out, in_=in_, func=AF.Copy, scale=scale)
            if scale is None:
                return lambda e: e.tensor_copy(out=out, in_=in_)
            return lambda e: e.tensor_scalar(out=out, in0=in_, scalar1=scale, scalar2=None, op0=ALU.mult)

        def wkeys(name, kc, c0, c1):
            return [(name, kc, c) for c in range(c0 // 1024, (c1 - 1) // 1024 + 1)]

        import os
        SKIP = os.environ.get("KSKIP", "")

        def setup_dma(out, in_, key):
            k0 = key[0] if isinstance(key, tuple) else key
            if k0 in SKIP.split(","):
                return
            P.add("sp", lambda e: e.dma_start(out=out, in_=in_), writes=[key], dsem="setup")

        setup_dma(identf[:, :], ident_d, "identf")
        setup_dma(hv[:, :], hv_d, "hv")
        setup_dma(sinkt[:, :], sink_d.partition_broadcast(128), "sinkt")
        setup_dma(gcol[:, 0:8], gpm_d.rearrange("o (k p) -> p (o k)", p=128), "gcol")
        setup_dma(gcol[:, 8:16], gpf_d.rearrange("o (k p) -> p (o k)", p=128), "gcol")
        for tap in range(3):
            setup_dma(cw[:, tap * 24:(tap + 1) * 24], cw_d[tap:tap + 1, :].rearrange("o (c p) -> p (o c)", p=128), "cw")
        setup_dma(cbias[:, :], cb_d.rearrange("o (c p) -> p (o c)", p=128), "cbias")
        setup_dma(BA.rearrange("p t h q -> p (t h q)"), ba_d, "BA")
        setup_dma(BB.rearrange("p t h q -> p (t h q)"), bb_d, "BB")
        setup_dma(BAs.rearrange("p t h q -> p (t h q)"), bas_d, "BAs")
        setup_dma(BBs.rearrange("p t h q -> p (t h q)"), bbs_d, "BBs")
        setup_dma(gbc[:, :], gqm_d.partition_broadcast(128), "gbc")
        for c in range(24):
            setup_dma(hists[:, 2 * c:2 * c + 2], sconv_d[:, c * 128:(c + 1) * 128].rearrange("t p -> p t"), ("hists", c))

        P.add("dve", lambda e: e.tensor_copy(out=ident[:, :], in_=identf[:, :]), reads=["identf"], writes=["ident"])
        P.add("dve", lambda e: e.memset(epst[:, :], EPS), writes=["epst"])
        P.add("dve", lambda e: e.memset(hist[:, :], 0.0), writes=[("hist", j) for j in range(24)])
        P.add("dve", lambda e: e.memset(den[:, 0:8], 1.0), writes=["den0"])
        P.add("dve", lambda e: e.tensor_scalar(out=rden[:, 0:8], in0=den[:, 0:8], scalar1=hv[:, 0:1], scalar2=None,
                                                op0=ALU.mult), reads=["den0", "hv"], writes=["rden0"])
        P.add("dve", lambda e: e.tensor_copy(out=hv10[:, 0:8], in_=rden[:, 0:8]), reads=["rden0"], writes=["hv10a"])
        P.add("dve", lambda e: e.tensor_copy(out=hv10[:, 8:10], in_=rden[:, 0:2]), reads=["rden0"], writes=["hv10"])
        P.add("act", lambda e: e.activation(out=exps[:, :], in_=sinkt[:, :], func=AF.Exp),
              reads=["sinkt"], writes=["exps"])

        def prep_weight(name, dram, nk, ncols, dest, scale_col=None, scale_const=None, extra_reads=()):
            for kc in range(nk):
                for c0 in range(0, ncols, 1024):
                    w = min(1024, ncols - c0)
                    sl = cnt["xin"] % 4
                    cnt["xin"] += 1
                    stg = xin[sl]
                    P.add("sp", lambda e, stg=stg, kc=kc, c0=c0, w=w: e.dma_start(
                        out=stg[:, 0:w], in_=dram[kc * 128:(kc + 1) * 128, c0:c0 + w]),
                        writes=[("xin", sl)], dsem=("xin", sl))
                    cnt["cast"] += 1
                    eng = "act" if cnt["cast"] % 2 else "dve"
                    if scale_col is not None:
                        sc = gcol[:, scale_col + kc:scale_col + kc + 1]
                        rd = [("xin", sl), "gcol"]
                    else:
                        sc = scale_const
                        rd = [("xin", sl)]
                    rd = rd + list(extra_reads)
                    key = (name, kc, c0 // 1024)
                    if name == "Win" and c0 == 0:
                        dq = dest[:, kc, 0:512].rearrange("p (i two d) -> p i two d", two=2, d=64)
                        for two in range(2):
                            P.add(eng, copy_op(eng, dq[:, :, two, :],
                                               stg[:, two * 256:(two + 1) * 256].rearrange("p (i d) -> p i d", d=64), sc),
                                  reads=rd, writes=[key])
                        P.add(eng, copy_op(eng, dest[:, kc, 512:1024], stg[:, 512:1024], sc), reads=rd, writes=[key])
                    else:
                        P.add(eng, copy_op(eng, dest[:, kc, c0:c0 + w], stg[:, 0:w], sc), reads=rd, writes=[key])

        if stage >= 0:
            prep_weight("Win", win_d, 8, INW, Win, scale_col=0)
            prep_weight("Woa", woa_d, 4, D, Woa)
            prep_weight("Wob", wob_d, 4, D, Wob)
            prep_weight("Wout", wout_d, 8, D, Wout, scale_const=0.5)

        def norm_a(x_ap_tile, sl, ntok, xnb, xnkey, col=0):
            c1 = slice(col, col + 1)
            P.add("act", lambda e: e.activation(out=xnb[0:ntok, :], in_=x_ap_tile[0:ntok, :], func=AF.Square,
                                                accum_out=ss[0:ntok, c1]),
                  reads=[("xin", sl)], writes=[xnkey, ("ss", col)])
            P.add("act", lambda e: e.activation(out=sd[0:ntok, c1], in_=ss[0:ntok, c1], func=AF.Sqrt,
                                                scale=1.0 / D, bias=epst[0:ntok, :]),
                  reads=[("ss", col), "epst"], writes=[("sd", col)])
            P.add("dve", lambda e: e.reciprocal(out=rstd[0:ntok, c1], in_=sd[0:ntok, c1]),
                  reads=[("sd", col)], writes=[("rstd", col)])
            P.add("act", lambda e: e.activation(out=xnb[0:ntok, :], in_=x_ap_tile[0:ntok, :], func=AF.Copy,
                                                scale=rstd[0:ntok, c1]),
                  reads=[("xin", sl), ("rstd", col)], writes=[xnkey])

        def norm_b(ntok, hTd, hkey, xnb, xnkey, b=0, eng=None):
            for kc in range(8):
                P.add("pe", lambda e, kc=kc, b=b: e.transpose(out=bkb(b)[:, kc * 128:kc * 128 + ntok],
                                                         in_=xnb[0:ntok, kc * 128:(kc + 1) * 128],
                                                         identity=ident[0:ntok, 0:ntok]),
                      reads=[xnkey, "ident"], writes=[BK(b)])
            eng = eng or evac_eng()
            P.add(eng, copy_op(eng, hTd[:, :, 0:ntok],
                               bkb(b).rearrange("p (k t) -> p k t", k=8)[:, :, 0:ntok]),
                  reads=[BK(b)], writes=[hkey])

        def post_norm_residual(bankpair, sl, ntok, gkey, out_ap, out_key, cb=4):
            b0, b1 = bankpair
            for i, b in enumerate((b0, b1)):
                P.add("act", lambda e, b=b, i=i: e.activation(out=junk[0:ntok, :],
                                                              in_=bk(b)[0:ntok, :], func=AF.Square,
                                                              accum_out=ss[0:ntok, cb + i:cb + i + 1]),
                      reads=[BK(b)], writes=["junk", ("ss", cb + i)])
            P.add("dve", lambda e: e.tensor_tensor(out=ss[0:ntok, cb + 2:cb + 3], in0=ss[0:ntok, cb:cb + 1],
                                                   in1=ss[0:ntok, cb + 1:cb + 2], op=ALU.add),
                  reads=[("ss", cb), ("ss", cb + 1)], writes=[("ss", cb + 2)])
            P.add("act", lambda e: e.activation(out=sd[0:ntok, cb:cb + 1], in_=ss[0:ntok, cb + 2:cb + 3], func=AF.Sqrt,
                                                scale=1.0 / D, bias=epst[0:ntok, :]),
                  reads=[("ss", cb + 2), "epst"], writes=[("sd", cb)])
            P.add("dve", lambda e: e.reciprocal(out=rstd[0:ntok, cb:cb + 1], in_=sd[0:ntok, cb:cb + 1]),
                  reads=[("sd", cb)], writes=[("rstd", cb)])
            for i, b in enumerate((b0, b1)):
                P.add("dve", lambda e, b=b, i=i: e.scalar_tensor_tensor(
                    out=tmp[0:ntok, i * 512:(i + 1) * 512], in0=bk(b)[0:ntok, :], scalar=rstd[0:ntok, cb:cb + 1],
                    in1=gbc[0:ntok, i * 512:(i + 1) * 512], op0=ALU.mult, op1=ALU.mult),
                    reads=[BK(b), ("rstd", cb), gkey], writes=["tmp"])
            P.add("pool", lambda e: e.tensor_tensor(out=out_ap[0:ntok, 0:D], in0=tmp[0:ntok, 0:D],
                                                    in1=xin[sl][0:ntok, :], op=ALU.add),
                  reads=["tmp", ("xin", sl)], writes=[out_key])

        pj = {"i": 0}

        def next_pj():
            pj["i"] += 1
            return 3 + pj["i"] % 5

        def p1_front(T):
            t, kind, ntok, samp, slot = T["t"], T["kind"], T["ntok"], T["samp"], T["slot"]
            sl = cnt["xin"] % 4
            cnt["xin"] += 1
            T["sl"] = sl
            hi = T["hi"]
            src = xs_d if samp else xc_d[t * 128:(t + 1) * 128, :]
            P.add("sp", lambda e: e.dma_start(out=xin[sl][0:ntok, :], in_=src),
                  writes=[("xin", sl)], dsem=("xin", sl))

        def p1_front_a(T):
            norm_a(xin[T["sl"]], T["sl"], T["ntok"], xn[T["hi"]], ("xn", T["hi"]))

        def p1_front_b(T):
            ntok, hi = T["ntok"], T["hi"]
            norm_b(ntok, hT[hi], ("hT", hi), xn[hi], ("xn", hi), b=2, eng="dve")

        def fm_chunk(T, col0, dest, dkey, scale):
            ntok, hi = T["ntok"], T["hi"]
            b = next_pj()
            for kc in range(8):
                P.add("pe", lambda e, kc=kc: e.matmul(bk(b)[:, 0:ntok], lhsT=Win[:, kc, col0:col0 + 128],
                                                      rhs=hT[hi][:, kc, 0:ntok], start=(kc == 0), stop=(kc == 7)),
                      reads=[("hT", hi)] + wkeys("Win", kc, col0, col0 + 128), writes=[BK(b)])
            eng = evac_eng()
            P.add(eng, copy_op(eng, dest, bk(b)[:, 0:ntok], scale), reads=[BK(b)], writes=[dkey])

        def p1_proj(T):
            t, kind, ntok, samp, slot, hi = T["t"], T["kind"], T["ntok"], T["samp"], T["slot"], T["hi"]
            ks = slice(slot * 128, slot * 128 + ntok)
            if kind == "full":
                for ci in range(4):
                    fm_chunk(T, C_QA + ci * 128, QT[:, ci, 0:ntok], ("QT", ci), 0.125)
                for ci in range(4):
                    fm_chunk(T, C_QB + ci * 128, QT[:, 4 + ci, 0:ntok], ("QT", 4 + ci), 0.125)
            fm_chunk(T, C_KA, KTa[:, ks], ("KTa", slot), None)
            for ci in range(4):
                fm_chunk(T, C_KB + ci * 128, KTb[:, ci, ks], ("KTb", slot, ci), None)
            kvout = T["kvout"]
            if not kvout:
                b = next_pj()
                for kc in range(8):
                    P.add("pe", lambda e, kc=kc, b=b: e.matmul(bk(b)[0:ntok, 0:128], lhsT=hT[hi][:, kc, 0:ntok],
                                                          rhs=Win[:, kc, C_VA:C_VA + 128], start=(kc == 0),
                                                          stop=(kc == 7)),
                          reads=[("hT", hi)] + wkeys("Win", kc, C_VA, C_VA + 128), writes=[BK(b)])
                P.add("act", copy_op("act", Vr[0:ntok, slot, 0:2, 0:64],
                                     bk(b)[0:ntok, 0:128].rearrange("p (h d) -> p h d", h=2)),
                      reads=[BK(b)], writes=[("Va", slot)])
                b = next_pj()
                for kc in range(8):
                    P.add("pe", lambda e, kc=kc, b=b: e.matmul(bk(b)[0:ntok, 0:512], lhsT=hT[hi][:, kc, 0:ntok],
                                                          rhs=Win[:, kc, C_VB:C_VB + 512], start=(kc == 0),
                                                          stop=(kc == 7)),
                          reads=[("hT", hi)] + wkeys("Win", kc, C_VB, C_VB + 512), writes=[BK(b)])
                P.add("dve", copy_op("dve", Vr[0:ntok, slot, 2:10, 0:64],
                                     bk(b)[0:ntok, 0:512].rearrange("p (h d) -> p h d", h=8)),
                      reads=[BK(b)], writes=[("Vb", slot)])
            else:
                segs = [(3, C_KA, 256, 0), (4, C_KB, 512, 256), (3, C_VB, 512, 768)]
                for (b, c0, w, to) in segs:
                    for kc in range(8):
                        P.add("pe", lambda e, kc=kc, b=b, c0=c0, w=w: e.matmul(
                            bk(b)[0:ntok, 0:w], lhsT=hT[hi][:, kc, 0:ntok], rhs=Win[:, kc, c0:c0 + w],
                            start=(kc == 0), stop=(kc == 7)),
                            reads=[("hT", hi)] + wkeys("Win", kc, c0, c0 + w), writes=[BK(b)])
                    eng = evac_eng()
                    P.add(eng, copy_op(eng, tmp[0:ntok, to:to + w], bk(b)[0:ntok, 0:w]),
                          reads=[BK(b)], writes=["tmp"])
                P.add("act", copy_op("act", Vr[0:ntok, slot, 0:2, 0:64],
                                     tmp[0:ntok, 128:256].rearrange("p (h d) -> p h d", h=2)),
                      reads=["tmp"], writes=[("Va", slot)])
                P.add("dve", copy_op("dve", Vr[0:ntok, slot, 2:10, 0:64],
                                     tmp[0:ntok, 768:1280].rearrange("p (h d) -> p h d", h=8)),
                      reads=["tmp"], writes=[("Vb", slot)])
                for (dst, c0, w) in T["kvdst"]:
                    P.add("sp", lambda e, dst=dst, c0=c0, w=w: e.dma_start(out=dst, in_=tmp[0:ntok, c0:c0 + w]),
                          reads=["tmp"], dsem="kvst")
            if (not samp) and t <= T_HALO:
                P.add("pool", lambda e: e.tensor_copy(out=Vr[0:ntok, slot, :, 64], in_=hv10[0:ntok, :]),
                      reads=["hv10"], writes=[("V1", slot)])
            else:
                P.add("pool", lambda e: e.memset(Vr[0:ntok, slot, :, 64], 1.0), writes=[("V1", slot)])

        def attention(T):
            ntok = T["ntok"]
            a_tiles, b_tiles, TA, TB, ka, kb_ = T["a_tiles"], T["b_tiles"], T["TA"], T["TB"], T["TAk"], T["TBk"]
            units = []
            for g in range(2):
                for i, (slot, nk, bi) in enumerate(a_tiles):
                    units.append(dict(kind="A", g=g, slot=slot, nk=nk, bi=bi, first=(i == 0),
                                      last=(i == len(a_tiles) - 1)))
            for g in range(2):
                for i, (slot, nk, bi) in enumerate(b_tiles):
                    units.append(dict(kind="B", g=g, slot=slot, nk=nk, bi=bi, first=(i == 0),
                                      last=(i == len(b_tiles) - 1)))
            sbanks = [2, 3, 4, 5]
            W4 = 4 * ntok

            def emit_qk(ui, u):
                b = sbanks[ui % 4]
                u["b"] = b
                u["p"] = ui % 4
                slot, nk, g = u["slot"], u["nk"], u["g"]
                if u["kind"] == "A":
                    P.add("pe", lambda e: e.matmul(
                        bk(b)[0:nk, 0:W4].rearrange("p (h q) -> p h q", h=4),
                        lhsT=KTa[g * 64:(g + 1) * 64, slot * 128:slot * 128 + nk],
                        rhs=QT[g * 64:(g + 1) * 64, 0:4, 0:ntok], start=True, stop=True),
                        reads=[("KTa", slot)] + [("QT", c) for c in range(4)], writes=[BK(b)])
                else:
                    for hh in range(4):
                        pb = g * 64
                        P.add("pe", lambda e, hh=hh, pb=pb: e.matmul(
                            bk(b)[0:nk, hh * ntok:(hh + 1) * ntok],
                            lhsT=KTb[pb:pb + 64, hh, slot * 128:slot * 128 + nk],
                            rhs=QT[pb:pb + 64, 4 + hh, 0:ntok], start=True, stop=True),
                            reads=[("KTb", slot, hh), ("QT", 4 + hh)], writes=[BK(b)])
                tab, tkey = (TA, ka) if u["kind"] == "A" else (TB, kb_)
                bi = u["bi"]
                P.add("dve", lambda e: e.tensor_tensor(
                    out=bk(b)[0:nk, 0:W4].rearrange("p (h q) -> p h q", h=4),
                    in0=bk(b)[0:nk, 0:W4].rearrange("p (h q) -> p h q", h=4),
                    in1=tab[0:nk, bi, 4 * g:4 * g + 4, :], op=ALU.add),
                    reads=[BK(b), tkey], writes=[BK(b)])
                P.add("act", lambda e: e.activation(out=pT[u["p"]][0:nk, 0:W4], in_=bk(b)[0:nk, 0:W4], func=AF.Exp),
                      reads=[BK(b)], writes=[("pT", u["p"])])

            def emit_pv(u):
                slot, nk, g = u["slot"], u["nk"], u["g"]
                ob = 6 + g
                for hh in range(4):
                    vh = g if u["kind"] == "A" else 2 + 2 * hh + g
                    P.add("pe", lambda e, hh=hh, vh=vh: e.matmul(
                        bk(ob)[0:ntok, hh * 65:(hh + 1) * 65], lhsT=pT[u["p"]][0:nk, hh * ntok:(hh + 1) * ntok],
                        rhs=Vr[0:nk, slot, vh, 0:65], start=(u["first"] and hh == 0), stop=(u["last"] and hh == 3)),
                        reads=[("pT", u["p"]), ("Va", slot) if u["kind"] == "A" else ("Vb", slot), ("V1", slot)],
                        writes=[BK(ob)])
                if u["last"]:
                    o3 = bk(ob)[0:ntok, 0:260].rearrange("p (h d) -> p h d", h=4)
                    dsl = slice(4 * g, 4 * g + 4)
                    if u["kind"] == "A":
                        P.add("dve", lambda e: e.tensor_tensor(out=den[0:ntok, dsl], in0=o3[:, :, 64],
                                                               in1=exps[0:ntok, dsl], op=ALU.add),
                              reads=[BK(ob), "exps"], writes=[("den", g)])
                    else:
                        P.add("dve", lambda e: e.tensor_scalar(out=den[0:ntok, dsl], in0=o3[:, :, 64],
                                                               scalar1=1e-30, scalar2=None, op0=ALU.add),
                              reads=[BK(ob)], writes=[("den", g)])
                    P.add("dve", lambda e: e.reciprocal(out=rden[0:ntok, dsl], in_=den[0:ntok, dsl]),
                          reads=[("den", g)], writes=[("rden", g)])
                    for hh in range(4):
                        c = (4 * g + hh) * 64 if u["kind"] == "A" else 512 + (2 * hh + g) * 64
                        P.add("dve", lambda e, hh=hh, c=c: e.tensor_scalar(
                            out=Otok[0:ntok, c:c + 64], in0=bk(ob)[0:ntok, hh * 65:hh * 65 + 64],
                            scalar1=rden[0:ntok, 4 * g + hh:4 * g + hh + 1], scalar2=None, op0=ALU.mult),
                            reads=[BK(ob), ("rden", g)], writes=[("Otok", c // 128)])

            L = 3
            for i in range(len(units) + L):
                if i < len(units):
                    emit_qk(i, units[i])
                if i - L >= 0:
                    emit_pv(units[i - L])

        def p1_back(T, mid=None, early=None, late=None):
            t, kind, ntok, samp, slot, hi, sl = T["t"], T["kind"], T["ntok"], T["samp"], T["slot"], T["hi"], T["sl"]
            if early is not None:
                early()
            if kind != "full":
                if mid is not None:
                    mid()
                if late is not None:
                    late()
                return None
            attention(T)
            if mid is not None:
                mid()
            if late is not None:
                late()
            if samp:
                dbg_dump("QT", QT[:, :, 0:16], [128, 8, 16], BF16, [("QT", c) for c in range(8)])
                dbg_dump("hT", hT[hi][:, :, 0:16], [128, 8, 16], BF16, [("hT", hi)])
                dbg_dump("Otok", Otok[0:16, :], [16, 1024], BF16, [("Otok", c) for c in range(8)])
                dbg_dump("den", den[0:16, :], [16, 8], F32, [("den", 0), ("den", 1)])
                dbg_dump("KTb", KTb[:, :, 0:640], [128, 4, 640], BF16, [("KTb", s_, c) for s_ in range(5) for c in range(4)])
                dbg_dump("Vr", Vr[:, 0:5, :, :], [128, 5, 10, 65], BF16, [("Vb", s_) for s_ in range(5)] + [("Va", 3), ("Va", 4)] + [("V1", s_) for s_ in range(5)])
            b = 7
            for kc in range(8):
                P.add("pe", lambda e, kc=kc, b=b: e.transpose(out=bkb(b)[:, kc * 128:kc * 128 + ntok],
                                                         in_=Otok[0:ntok, kc * 128:(kc + 1) * 128],
                                                         identity=ident[0:ntok, 0:ntok]),
                      reads=[("Otok", kc), "ident"], writes=[BK(b)])
            eng = "dve"
            P.add(eng, copy_op(eng, OT[:, :, 0:ntok], bkb(b).rearrange("p (k t) -> p k t", k=8)[:, :, 0:ntok]),
                  reads=[BK(b)], writes=["OT"])
            for fc in range(8):
                bs = [0, 1, 2, 3] if fc % 2 == 0 else [4, 5, 6, 7]
                gt = gtmp[fc % 2]
                for (b, c0) in ((bs[0], C_GA + fc * 128), (bs[1], C_GB + fc * 128)):
                    for kc in range(8):
                        P.add("pe", lambda e, kc=kc, b=b, c0=c0: e.matmul(
                            bk(b)[:, 0:ntok], lhsT=Win[:, kc, c0:c0 + 128], rhs=hT[hi][:, kc, 0:ntok],
                            start=(kc == 0), stop=(kc == 7)),
                            reads=[("hT", hi)] + wkeys("Win", kc, c0, c0 + 128), writes=[BK(b)])
                for (b, W, wn, off) in ((bs[2], Woa, "Woa", 0), (bs[3], Wob, "Wob", 4)):
                    for kc in range(4):
                        P.add("pe", lambda e, kc=kc, b=b, W=W, off=off, fc=fc: e.matmul(
                            bk(b)[:, 0:ntok], lhsT=W[:, kc, fc * 128:(fc + 1) * 128], rhs=OT[:, off + kc, 0:ntok],
                            start=(kc == 0), stop=(kc == 3)),
                            reads=["OT", (wn, kc, 0)], writes=[BK(b)])
                for i in range(2):
                    P.add("act", lambda e, i=i, gt=gt, bs=bs: e.activation(out=gt[:, i * 128:i * 128 + ntok],
                                                             in_=bk(bs[i])[:, 0:ntok], func=AF.Tanh, scale=0.5),
                          reads=[BK(bs[i])], writes=[("gt", fc % 2, i)])
                for i in range(2):
                    P.add("dve", lambda e, i=i, gt=gt, bs=bs: e.scalar_tensor_tensor(
                        out=gt[:, i * 128:i * 128 + ntok], in0=gt[:, i * 128:i * 128 + ntok], scalar=1.0,
                        in1=bk(bs[2 + i])[:, 0:ntok], op0=ALU.add, op1=ALU.mult),
                        reads=[("gt", fc % 2, i), BK(bs[2 + i])], writes=[("gt", fc % 2, i)])
                P.add("pool", lambda e, fc=fc, gt=gt: e.tensor_tensor(out=mT[:, fc, 0:ntok], in0=gt[:, 0:ntok],
                                                        in1=gt[:, 128:128 + ntok], op=ALU.add),
                      reads=[("gt", fc % 2, 0), ("gt", fc % 2, 1)], writes=[("mT", fc)])
            for half in range(2):
                b = half
                for kc in range(8):
                    P.add("pe", lambda e, kc=kc, b=b, half=half: e.matmul(
                        bk(b)[0:ntok, :], lhsT=mT[:, kc, 0:ntok], rhs=Wout[:, kc, half * 512:(half + 1) * 512],
                        start=(kc == 0), stop=(kc == 7)),
                        reads=[("mT", kc), ("Wout", kc, 0)], writes=[BK(b)])
            if samp:
                dbg_dump("mT", mT[:, :, 0:16], [128, 8, 16], BF16, [("mT", c) for c in range(8)])
                dbg_dump("OT", OT[:, :, 0:16], [128, 8, 16], BF16, ["OT"])
            def fin():
                post_norm_residual((0, 1), sl, ntok, "gbc", xin[sl], ("xin", sl))
                row = T["x1row"]
                P.add("sp", lambda e: e.dma_start(out=x1s_d[row:row + ntok, :], in_=xin[sl][0:ntok, :]),
                      reads=[("xin", sl)], writes=[("x1s", row)], dsem=("xst", sl))
            return fin

        def load_caches():
            for kt in range(4):
                sl = cnt["xin"] % 4
                cnt["xin"] += 1
                P.add("sp", lambda e, kt=kt, sl=sl: e.dma_start(out=xin[sl][:, 0:512],
                                                                in_=cbk_d[kt * 128:(kt + 1) * 128, :]),
                      writes=[("xin", sl)], dsem=("xin", sl))
                P.add("sp", lambda e, kt=kt, sl=sl: e.dma_start(out=xin[sl][:, 512:1024],
                                                                in_=cbv_d[kt * 128:(kt + 1) * 128, :]),
                      writes=[("xin", sl)], dsem=("xin", sl))
                xb = xn[kt % 2]
                P.add("dve", copy_op("dve", xb[:, 0:512], xin[sl][:, 0:512]), reads=[("xin", sl)],
                      writes=[("xn", kt % 2)])
                P.add("act", copy_op("act", Vr[:, kt, 2:10, 0:64],
                                     xin[sl][:, 512:1024].rearrange("p (h d) -> p h d", h=8)),
                      reads=[("xin", sl)], writes=[("Vb", kt)])
                P.add("pool", lambda e, kt=kt: e.memset(Vr[:, kt, :, 64], 1.0), writes=[("V1", kt)])
                b = kt % 2
                for c in range(4):
                    P.add("pe", lambda e, c=c, b=b, xb=xb: e.transpose(out=bkb(b)[:, c * 128:(c + 1) * 128],
                                                                       in_=xb[:, c * 128:(c + 1) * 128],
                                                                       identity=ident[:, :]),
                          reads=[("xn", kt % 2), "ident"], writes=[BK(b)])
                P.add("dve", copy_op("dve", KTb[:, :, kt * 128:(kt + 1) * 128],
                                     bkb(b)[:, 0:512].rearrange("p (c k) -> p c k", c=4)),
                      reads=[BK(b)], writes=[("KTb", kt, c) for c in range(4)])
            sl = cnt["xin"] % 4
            cnt["xin"] += 1
            P.add("sp", lambda e: e.dma_start(out=xin[sl][:, 0:128], in_=cak_d), writes=[("xin", sl)],
                  dsem=("xin", sl))
            P.add("sp", lambda e: e.dma_start(out=xin[sl][:, 128:256], in_=cav_d), writes=[("xin", sl)],
                  dsem=("xin", sl))
            xb = xn[0]
            P.add("dve", copy_op("dve", xb[:, 0:128], xin[sl][:, 0:128]), reads=[("xin", sl)], writes=[("xn", 0)])
            P.add("act", copy_op("act", Vr[:, 3, 0:2, 0:64], xin[sl][:, 128:256].rearrange("p (h d) -> p h d", h=2)),
                  reads=[("xin", sl)], writes=[("Va", 3)])
            P.add("pe", lambda e: e.transpose(out=bkb(0)[:, 0:128], in_=xb[:, 0:128], identity=ident[:, :]),
                  reads=[("xn", 0), "ident"], writes=[BK(0)])
            P.add("dve", copy_op("dve", KTa[:, 3 * 128:4 * 128], bkb(0)[:, 0:128]), reads=[BK(0)],
                  writes=[("KTa", 3)])

        tiles = []
        samp = dict(t=-1, kind="full", ntok=16, samp=True, slot=SAMP_SLOT, kvout=True,
                    kvdst=[(sak_d, 0, 128), (sav_d, 128, 128), (sbk_d, 256, 512), (sbv_d, 768, 512)],
                    a_tiles=[(3, 128, 0), (SAMP_SLOT, 16, 1)],
                    b_tiles=[(0, 128, 0), (1, 128, 1), (2, 128, 2), (3, 128, 3), (SAMP_SLOT, 16, 4)],
                    TA=BAs, TB=BBs, TAk="BAs", TBk="BBs", x1row=NTILES * 128)
        tiles.append(samp)
        for t in range(NTILES):
            kind = "kv" if t < T_HALO_KV else "full"
            T = dict(t=t, kind=kind, ntok=128, samp=False, slot=t % RT, kvout=(t >= NTILES - 4), x1row=t * 128)
            if T["kvout"]:
                r = (t - (NTILES - 4)) * 128
                dst = [(obk_d[r:r + 128, :], 256, 512), (obv_d[r:r + 128, :], 768, 512)]
                if t == NTILES - 1:
                    dst += [(oak_d, 0, 128), (oav_d, 128, 128)]
                T["kvdst"] = dst
            if kind == "full":
                T["a_tiles"] = [((t - 1) % RT, 128, 0), (t % RT, 128, 1)]
                T["b_tiles"] = [((t - 4 + i) % RT, 128, i) for i in range(5)]
                T.update(TA=BA, TB=BB, TAk="BA", TBk="BB")
            tiles.append(T)
        for i, T in enumerate(tiles):
            T["hi"] = i % 2

        if ntiles_p1 is not None:
            tiles = tiles[:ntiles_p1]
        if stage >= 1:
            load_caches()
        if stage >= 2:
            def front_proj(Tn, between=None):
                p1_front_b(Tn)
                if between is not None:
                    between()
                p1_proj(Tn)

            p1_front(tiles[0])
            p1_front_a(tiles[0])
            front_proj(tiles[0])
            if len(tiles) > 1:
                p1_front(tiles[1])
                p1_front_a(tiles[1])
            pend1 = [None]
            for i, T in enumerate(tiles):
                def early(i=i):
                    if i + 2 < len(tiles):
                        p1_front(tiles[i + 2])

                def mid(i=i):
                    def flush():
                        if pend1[0] is not None:
                            pend1[0]()
                            pend1[0] = None
                    if i + 1 < len(tiles):
                        front_proj(tiles[i + 1], flush)
                    else:
                        flush()

                def late(i=i):
                    if i + 2 < len(tiles):
                        p1_front_a(tiles[i + 2])

                r = p1_back(T, mid, early, late)
                if r is not None:
                    pend1[0] = r
            if pend1[0] is not None:
                pend1[0]()
        if stage < 3:
            mo = int(os.environ.get("KMAXOPS", "0"))
            if mo:
                for i, op in enumerate(P.ops[:mo]):
                    print("OP", i, op.eng, op.reads, op.writes, op.dsem)
                P.ops = P.ops[:mo]
            P.finalize()
            P.emit(nc)
            return nc

        for op in reversed(P.ops):
            if op.eng == "pe":
                op.writes.append(("bar", "pe"))
                break
        for i, eng in enumerate(("act", "dve", "pool")):
            if eng == "act":
                P.add(eng, lambda e, i=i: e.activation(out=bars[:, i:i + 1], in_=epst[:, 0:1], func=AF.Copy),
                      writes=[("bar", eng)])
            else:
                P.add(eng, lambda e, i=i: e.memset(bars[:, i:i + 1], 0.0), writes=[("bar", eng)])
        BARS = [("bar", e) for e in ("pe", "act", "dve", "pool")]
        for i, eng in enumerate(("act", "dve", "pool")):
            if eng == "act":
                P.add(eng, lambda e, i=i: e.activation(out=bars[:, 4 + i:5 + i], in_=epst[:, 0:1], func=AF.Copy),
                      reads=BARS, writes=[("bar2", eng)])
            else:
                P.add(eng, lambda e, i=i: e.memset(bars[:, 4 + i:5 + i], 0.0), reads=BARS, writes=[("bar2", eng)])

        P.add("sp", lambda e: e.dma_start(out=gbc[:, :], in_=gqf_d.partition_broadcast(128)),
              reads=BARS, writes=["gbc"], dsem="gbc2")
        prep_weight("Wup", wup_d, 8, 2 * DFF, Wup, scale_col=8, extra_reads=BARS)
        prep_weight("Wdn", wdn_d, 24, D, Wdn, extra_reads=BARS)

        def p2_front(G):
            hi = G["hi"]
            for i, T in enumerate(G["tiles"]):
                ntok = T["ntok"]
                sl = cnt["xin"] % 4
                cnt["xin"] += 1
                T["sl"] = sl
                row = T["x1row"]
                P.add("sp", lambda e, sl=sl, row=row, ntok=ntok: e.dma_start(out=xin[sl][0:ntok, :],
                                                                            in_=x1s_d[row:row + ntok, :]),
                      reads=[("x1s", row)], writes=[("xin", sl)], dsem=("xin", sl))
                cnt["xn2"] = cnt.get("xn2", 0) + 1
                T["xi"] = cnt["xn2"] % 2

        def p2_front_a(G, step=None):
            tl_ = G["tiles"]
            ntok = tl_[0]["ntok"]
            nt_ = len(tl_)
            for st_ in ([0, 1, 2, 3, 4] if step is None else [step]):
                if st_ in (0, 1) and st_ < nt_:
                    T = tl_[st_]
                    xi, sl = T["xi"], T["sl"]
                    P.add("act", lambda e, xi=xi, sl=sl, c=st_: e.activation(
                        out=xn[xi][0:ntok, :], in_=xin[sl][0:ntok, :], func=AF.Square,
                        accum_out=ss[0:ntok, c:c + 1]),
                        reads=[("xin", sl)], writes=[("xn", xi), ("ss", st_)])
                elif st_ == 2:
                    P.add("act", lambda e: e.activation(out=sd[0:ntok, 0:nt_], in_=ss[0:ntok, 0:nt_], func=AF.Sqrt,
                                                        scale=1.0 / D, bias=epst[0:ntok, :]),
                          reads=[("ss", c) for c in range(nt_)] + ["epst"], writes=[("sd", c) for c in range(nt_)])
                    P.add("dve", lambda e: e.reciprocal(out=rstd[0:ntok, 0:nt_], in_=sd[0:ntok, 0:nt_]),
                          reads=[("sd", c) for c in range(nt_)], writes=[("rstd", c) for c in range(nt_)])
                elif st_ in (3, 4) and st_ - 3 < nt_:
                    T = tl_[st_ - 3]
                    xi, sl, c = T["xi"], T["sl"], st_ - 3
                    P.add("act", lambda e, xi=xi, sl=sl, c=c: e.activation(
                        out=xn[xi][0:ntok, :], in_=xin[sl][0:ntok, :], func=AF.Copy, scale=rstd[0:ntok, c:c + 1]),
                        reads=[("xin", sl), ("rstd", c)], writes=[("xn", xi)])

        def p2_front_b(G):
            hi = G["hi"]
            for i, T in enumerate(G["tiles"]):
                xi = T["xi"]
                norm_b(T["ntok"], h2T[hi][:, :, i * 128:(i + 1) * 128], ("h2T", hi, i), xn[xi], ("xn", xi),
                       b=(2 if i == 0 else 4), eng="dve")

        GR = int(os.environ.get("KGR", "2"))

        def p2_back(G, mid=None, pre=None, early=None):
            tl = G["tiles"]
            pre = pre if pre is not None else []
            hi = G["hi"]
            samp = tl[0]["samp"]
            N = sum(T["ntok"] for T in tl)
            halo = (not samp) and tl[0]["t"] == T_HALO
            H = hists if samp else hist
            hk = "hists" if samp else "hist"
            hkeys = [("h2T", hi, i) for i in range(len(tl))]

            def tail(j):
                cbf = cbuf[j % 2]
                bg = 4 + j % GR
                P.add("act", lambda e: e.activation(out=cbf[:, 0:N], in_=cbf[:, 0:N], func=AF.Gelu_apprx_tanh),
                      reads=[("cb", j % 2)], writes=[("cb", j % 2)])
                P.add("dve", lambda e: e.tensor_tensor(out=actT[:, j, 0:N], in0=cbf[:, 0:N],
                                                       in1=bk(bg)[:, 0:N], op=ALU.mult),
                      reads=[("cb", j % 2), BK(bg)], writes=[("actT", j)])

            for j in range(24):
                ub, cbf = ubuf[j % 2], cbuf[j % 2]
                bu, bg = 2 + j % 2, 4 + j % GR
                if pre and j in (0, 1):
                    pre.pop(0)()
                if early is not None and j == 3:
                    early[0]()
                if early is not None and j in (6, 8, 10, 12, 14):
                    early[1]((j - 6) // 2)
                for kc in range(8):
                    P.add("pe", lambda e, kc=kc, j=j, bu=bu: e.matmul(
                        bk(bu)[:, 0:N], lhsT=Wup[:, kc, j * 128:(j + 1) * 128], rhs=h2T[hi][:, kc, 0:N],
                        start=(kc == 0), stop=(kc == 7)),
                        reads=hkeys + wkeys("Wup", kc, j * 128, (j + 1) * 128), writes=[BK(bu)])
                if not halo:
                    for kc in range(8):
                        P.add("pe", lambda e, kc=kc, j=j, bg=bg: e.matmul(
                            bk(bg)[:, 0:N], lhsT=Wup[:, kc, DFF + j * 128:DFF + (j + 1) * 128],
                            rhs=h2T[hi][:, kc, 0:N], start=(kc == 0), stop=(kc == 7)),
                            reads=hkeys + wkeys("Wup", kc, DFF + j * 128, DFF + (j + 1) * 128),
                            writes=[BK(bg)])
                P.add("act", copy_op("act", ub[:, 2:2 + N], bk(bu)[:, 0:N]), reads=[BK(bu)],
                      writes=[("ub", j % 2)])
                P.add("pool", lambda e, j=j, ub=ub: e.tensor_copy(out=ub[:, 0:2], in_=H[:, 2 * j:2 * j + 2]),
                      reads=[(hk, j)], writes=[("ubh", j % 2)])
                if halo:
                    P.add("pool", lambda e, j=j, ub=ub: e.tensor_scalar(out=H[:, 2 * j:2 * j + 2],
                                                                        in0=ub[:, N:N + 2], scalar1=hv[:, 0:1],
                                                                        scalar2=None, op0=ALU.mult),
                          reads=[("ub", j % 2), ("ubh", j % 2), "hv"], writes=[(hk, j)])
                    continue
                P.add("pool", lambda e, j=j, ub=ub: e.tensor_copy(out=H[:, 2 * j:2 * j + 2], in_=ub[:, N:N + 2]),
                      reads=[("ub", j % 2), ("ubh", j % 2)], writes=[(hk, j)])
                if j >= 1:
                    tail(j - 1)
                P.add("act", lambda e, j=j, ub=ub, cbf=cbf: e.activation(
                    out=cbf[:, 0:N], in_=ub[:, 2:2 + N], func=AF.Identity,
                    scale=cw[:, 48 + j:49 + j], bias=cbias[:, j:j + 1]),
                    reads=[("ub", j % 2), "cw", "cbias"], writes=[("cb", j % 2)])
                for tap in (1, 0):
                    P.add("dve", lambda e, j=j, ub=ub, cbf=cbf, tap=tap: e.scalar_tensor_tensor(
                        out=cbf[:, 0:N], in0=ub[:, tap:tap + N], scalar=cw[:, tap * 24 + j:tap * 24 + j + 1],
                        in1=cbf[:, 0:N], op0=ALU.mult, op1=ALU.add),
                        reads=[("ub", j % 2), ("ubh", j % 2), ("cb", j % 2), "cw"], writes=[("cb", j % 2)])
            while pre:
                pre.pop(0)()
            if halo:
                if mid is not None:
                    mid()
                return []
            tail(23)
            if mid is not None:
                mid()
            deferred = []
            for i, T in enumerate(tl):
                ntok, sl, t = T["ntok"], T["sl"], T["t"]
                bp = (0, 1) if i == 0 else (6, 7)
                for half in range(2):
                    b = bp[half]
                    for j in range(24):
                        P.add("pe", lambda e, j=j, b=b, half=half, i=i, ntok=ntok: e.matmul(
                            bk(b)[0:ntok, :], lhsT=actT[:, j, i * 128:i * 128 + ntok],
                            rhs=Wdn[:, j, half * 512:(half + 1) * 512], start=(j == 0), stop=(j == 23)),
                            reads=[("actT", j), ("Wdn", j, 0)], writes=[BK(b)])
                def fin(bp=bp, sl=sl, ntok=ntok, i=i, t=t):
                    post_norm_residual(bp, sl, ntok, "gbc", xin[sl], ("xin", sl), cb=4 + 4 * i)
                    dst = ys_d if samp else yp_d[(t - T0) * 128:(t - T0 + 1) * 128, :]
                    P.add("sp", lambda e: e.dma_start(out=dst, in_=xin[sl][0:ntok, :]),
                          reads=[("xin", sl)], dsem=("xst", sl))
                deferred.append(fin)
            if samp or tl[-1]["t"] == NTILES - 1:
                od = scv_d if samp else ocv_d
                for c in range(24):
                    P.add("sp", lambda e, c=c: e.dma_start(out=od[:, c * 128:(c + 1) * 128].rearrange("t p -> p t"),
                                                           in_=H[:, 2 * c:2 * c + 2]),
                          reads=[(hk, c)], dsem="cvout")
            return deferred

        full = [T for T in tiles if T["kind"] == "full"]
        pt = [T for T in full if not T["samp"]]
        groups = [dict(tiles=[pt[0]])]
        for i in range(1, len(pt), 2):
            groups.append(dict(tiles=pt[i:i + 2]))
        groups.append(dict(tiles=[T for T in full if T["samp"]]))
        for i, G in enumerate(groups):
            G["hi"] = i % 2
        p2_front(groups[0])
        p2_front_a(groups[0])
        p2_front_b(groups[0])
        pend = []
        for i, G in enumerate(groups):
            nxt = (lambda Gn=groups[i + 1]: p2_front_b(Gn)) if i + 1 < len(groups) else None
            early = ((lambda Gn=groups[i + 1]: p2_front(Gn)), (lambda st_, Gn=groups[i + 1]: p2_front_a(Gn, st_))) \
                if i + 1 < len(groups) else None
            pend = p2_back(G, nxt, pend, early)
        for f_ in pend:
            f_()

        P.finalize()
        P.emit(nc)
    return nc


def _t5_bucket_np(n):
    import jax
    import jax.numpy as jnp
    try:
        cpu = jax.devices("cpu")[0]
    except Exception:
        cpu = None
    ctx = jax.default_device(cpu) if cpu is not None else contextlib.nullcontext()
    with ctx:
        n = jnp.asarray(n, dtype=jnp.int32)
        half = 16
        max_exact = 8
        ret = jnp.where(n < 0, half, 0)
        a = jnp.abs(n)
        af = jnp.maximum(a, 1).astype(jnp.float32)
        large = max_exact + (jnp.log(af / max_exact) / math.log(128 / max_exact)
                             * (half - max_exact)).astype(jnp.int32)
        large = jnp.minimum(large, half - 1)
        out = ret + jnp.where(a < max_exact, a, large)
        return np.asarray(out)


def _bias_tables(t5_table, rel_b):
    q = np.arange(128)
    kk = np.arange(256)
    rel = (128 + q)[None, :] - kk[:, None]
    cq = 2 + q // 64
    ck = kk // 64
    vis = ((cq[None, :] - ck[:, None]) >= 0) & ((cq[None, :] - ck[:, None]) <= 2)
    idx = _t5_bucket_np(rel)
    A = t5_table[:, idx]
    A = np.where(vis[None], A, np.float32(NEG)).astype(np.float32)
    A = A.reshape(8, 2, 128, 128).transpose(2, 1, 0, 3)
    kk = np.arange(640)
    rel = (512 + q)[None, :] - kk[:, None]
    cq = 8 + q // 64
    ck = kk // 64
    vis = ((cq[None, :] - ck[:, None]) >= 0) & ((cq[None, :] - ck[:, None]) <= 8)
    idx = np.clip(rel, -128, 128) + 128
    B = rel_b[:, idx]
    B = np.where(vis[None], B, np.float32(NEG)).astype(np.float32)
    perm = [0, 2, 4, 6, 1, 3, 5, 7]
    B = B.reshape(8, 5, 128, 128)[perm].transpose(2, 1, 0, 3)
    qs = np.arange(16)
    kk = np.arange(256)
    rel = qs[None, :] - kk[:, None] + 128
    As = t5_table[:, _t5_bucket_np(rel)].astype(np.float32)
    As = As.reshape(8, 2, 128, 16).transpose(2, 1, 0, 3)
    kk = np.arange(640)
    rel = qs[None, :] - kk[:, None] + 512
    Bs = rel_b[:, np.clip(rel, -128, 128) + 128].astype(np.float32)
    Bs = Bs.reshape(8, 5, 128, 16)[perm].transpose(2, 1, 0, 3)
    c = np.ascontiguousarray
    return (c(A).reshape(128, -1), c(B).reshape(128, -1), c(As).reshape(128, -1), c(Bs).reshape(128, -1))


_NC_CACHE = {}


def kernel(x_prompt, x_sample, cache_a_k, cache_a_v, cache_b_k, cache_b_v, state_conv,
           w_in, w_oa, w_ob, w_out, sink_a, t5_table, rel_table_b,
           g_pre_mix, g_post_mix, g_pre_ffn, g_post_ffn, w_upg, conv_w, conv_b, w_down):
    f = lambda a: np.ascontiguousarray(np.asarray(a, dtype=np.float32))
    x_prompt, x_sample = f(x_prompt), f(x_sample)
    bA, bB, bAs, bBs = _bias_tables(f(t5_table), f(rel_table_b)[0])
    shared = {
        "w_in": f(w_in)[0], "w_oa": f(w_oa)[0], "w_ob": f(w_ob)[0], "w_out": f(w_out)[0],
        "sink": f(sink_a).reshape(1, 8), "g_pre_mix": f(g_pre_mix).reshape(1, D),
        "g_post_mix": f(g_post_mix).reshape(1, D), "g_pre_ffn": f(g_pre_ffn).reshape(1, D),
        "g_post_ffn": f(g_post_ffn).reshape(1, D), "w_upg": f(w_upg)[0], "conv_w": f(conv_w)[0],
        "conv_b": f(conv_b).reshape(1, DFF), "w_down": f(w_down)[0],
        "biasA": bA, "biasB": bB, "biasAs": bAs, "biasBs": bBs,
        "ident": np.eye(128, dtype=np.float32),
    }
    halo = T0 * 128
    in_maps = []
    for c in range(8):
        b, half = c // 2, c % 2
        start = half * HALF
        xc = np.zeros((NTILES * 128, D), np.float32)
        if half == 0:
            xc[halo:] = x_prompt[b, 0:HALF]
        else:
            xc[:] = x_prompt[b, start - halo:start + HALF]
        m = dict(shared)
        m.update({
            "xc": xc, "xs": x_sample[c],
            "cak": f(cache_a_k)[0, c].reshape(128, 128), "cav": f(cache_a_v)[0, c].reshape(128, 128),
            "cbk": f(cache_b_k)[0, c].reshape(512, 512), "cbv": f(cache_b_v)[0, c].reshape(512, 512),
            "sconv": f(state_conv)[0, c],
            "hv": np.full((128, 1), float(half), np.float32),
        })
        in_maps.append(m)
    if "nc" not in _NC_CACHE:
        _NC_CACHE["nc"] = build_nc()
    nc = _NC_CACHE["nc"]
    res = run_bass_kernel_spmd(nc, in_maps, core_ids=list(range(8)))
    R = res.results
    y_prompt = np.stack([np.concatenate([R[2 * b]["yp"], R[2 * b + 1]["yp"]], axis=0) for b in range(4)], 0)
    y_sample = np.stack([R[c]["ys"] for c in range(8)], 0)
    odd = [R[2 * b + 1] for b in range(4)]
    nak = np.stack([r["oak"].reshape(128, 2, 64) for r in odd], 0)[None]
    nav = np.stack([r["oav"].reshape(128, 2, 64) for r in odd], 0)[None]
    nbk = np.stack([r["obk"].reshape(512, 8, 64) for r in odd], 0)[None]
    nbv = np.stack([r["obv"].reshape(512, 8, 64) for r in odd], 0)[None]
    ncv = np.stack([r["ocv"] for r in odd], 0)[None]
    sak = np.stack([R[c]["sak"].reshape(16, 2, 64) for c in range(8)], 0)[None]
    sav = np.stack([R[c]["sav"].reshape(16, 2, 64) for c in range(8)], 0)[None]
    sbk = np.stack([R[c]["sbk"].reshape(16, 8, 64) for c in range(8)], 0)[None]
    sbv = np.stack([R[c]["sbv"].reshape(16, 8, 64) for c in range(8)], 0)[None]
    scv = np.stack([R[c]["scv"] for c in range(8)], 0)[None]
    out = (y_prompt, y_sample, nak, nav, nbk, nbv, ncv, sak, sav, sbk, sbv, scv)
    return tuple(np.ascontiguousarray(o.astype(np.float32)) for o in out)
```
